# Optimizing a Trainium2 kernel written in Bass

```python
import jax
import jax.numpy as jnp
from jax import lax
import numpy as np

D_MODEL = 2048
BATCH = 8
SEQ = 2048
DEPTH = 2

GRID_W = 64
CTX_LEN = 256
HEAD_DIM = 128
ROPE_THETA = 10000.0
EPS = 1e-6
Q_BLOCK = 128
N_MOD = 9
D_FF = 5632

CONV_CHANNELS = D_MODEL // 2
GQA_HEADS = (D_MODEL // 2) // HEAD_DIM
GQA_KV_HEADS = 2
GQA_GROUP = GQA_HEADS // GQA_KV_HEADS
HYB_IN = 3 * CONV_CHANNELS + (GQA_HEADS + 2 * GQA_KV_HEADS) * HEAD_DIM
HYB_OUT = CONV_CHANNELS + GQA_HEADS * HEAD_DIM

MLA_HEADS = D_MODEL // HEAD_DIM
MLA_Q_RANK = 768
MLA_KV_RANK = 512
MLA_NOPE = 128
MLA_ROPE = 64
MLA_V = 128
MLA_DOWN = MLA_Q_RANK + MLA_KV_RANK + MLA_ROPE
MLA_SCALE = (MLA_NOPE + MLA_ROPE) ** -0.5

N_EVEN = (DEPTH + 1) // 2
N_ODD = DEPTH // 2

kernel_name = 'hybrid_conv_gqa_mla_macaron_prefix_dit'


def rms_norm(x, g):
    xf = x.astype(jnp.float32)
    y = xf * lax.rsqrt(jnp.mean(xf * xf, axis=-1, keepdims=True) + EPS)
    return (y * g.astype(jnp.float32)).astype(x.dtype)


def modulated_norm(h, g, shift, scale):
    return rms_norm(h, g) * (1 + scale) + shift


def swiglu(h, w_gate, w_up, w_down):
    return (jax.nn.silu(h @ w_gate) * (h @ w_up)) @ w_down


def axial_rope_tables(n_tok, dim, dtype):
    n_rows = n_tok // GRID_W
    row = jnp.repeat(jnp.arange(n_rows), GRID_W).astype(jnp.float32)
    col = jnp.tile(jnp.arange(GRID_W), n_rows).astype(jnp.float32)
    half = dim // 2
    inv = 1.0 / (ROPE_THETA ** (jnp.arange(0, half, 2, dtype=jnp.float32) / half))
    ang = jnp.concatenate([row[:, None] * inv, col[:, None] * inv], axis=-1)
    return jnp.cos(ang).astype(dtype), jnp.sin(ang).astype(dtype)


def apply_rope(x, cos, sin):
    xp = x.reshape(*x.shape[:-1], x.shape[-1] // 2, 2)
    x0, x1 = xp[..., 0], xp[..., 1]
    return jnp.stack([x0 * cos - x1 * sin, x0 * sin + x1 * cos], axis=-1).reshape(x.shape)


def short_conv3(u, w):
    up = jnp.pad(u, ((0, 0), (1, 1), (0, 0)))
    return up[:, :-2] * w[0] + up[:, 1:-1] * w[1] + up[:, 2:] * w[2]


def sweep_query_blocks(fn, *qs):
    b, t = qs[0].shape[:2]
    nb = t // Q_BLOCK
    blocks = tuple(jnp.moveaxis(q.reshape(b, nb, Q_BLOCK, *q.shape[2:]), 1, 0) for q in qs)
    out = lax.map(lambda qb: fn(*qb), blocks)
    out = jnp.moveaxis(out, 0, 1)
    return out.reshape(b, t, *out.shape[3:])


def gqa_attend(q, k, v):
    s = jnp.einsum('bqgrd,bkgd->bgrqk', q, k).astype(jnp.float32) * (HEAD_DIM ** -0.5)
    p = jax.nn.softmax(s, axis=-1).astype(v.dtype)
    return jnp.einsum('bgrqk,bkgd->bqgrd', p, v)


def mla_attend(qn, qr, kn, kr, v):
    s = (jnp.einsum('bqhd,bkhd->bhqk', qn, kn)
         + jnp.einsum('bqhr,bkr->bhqk', qr, kr)).astype(jnp.float32) * MLA_SCALE
    p = jax.nn.softmax(s, axis=-1).astype(v.dtype)
    return jnp.einsum('bhqk,bkhd->bqhd', p, v)


def conv_attn_mixer(hc, hl, w_in, conv_w, q_norm, k_norm, w_out, cos, sin, need_ctx):
    splits = [CONV_CHANNELS, 2 * CONV_CHANNELS, 3 * CONV_CHANNELS,
              3 * CONV_CHANNELS + GQA_HEADS * HEAD_DIM,
              3 * CONV_CHANNELS + (GQA_HEADS + GQA_KV_HEADS) * HEAD_DIM]

    def project(h):
        b, t = h.shape[:2]
        gate_b, gate_c, u, q, k, v = jnp.split(h @ w_in, splits, axis=-1)
        conv_out = gate_b * short_conv3(gate_c * u, conv_w)
        q = rms_norm(q.reshape(b, t, GQA_KV_HEADS, GQA_GROUP, HEAD_DIM), q_norm)
        k = rms_norm(k.reshape(b, t, GQA_KV_HEADS, HEAD_DIM), k_norm)
        v = v.reshape(b, t, GQA_KV_HEADS, HEAD_DIM)
        return conv_out, q, k, v

    b, t = hl.shape[:2]
    conv_l, ql, kl, vl = project(hl)
    ql = apply_rope(ql, cos[:, None, None], sin[:, None, None])
    kl = apply_rope(kl, cos[:, None], sin[:, None])
    conv_c, qc, kc, vc = project(hc)
    k_all = jnp.concatenate([kc, kl], axis=1)
    v_all = jnp.concatenate([vc, vl], axis=1)
    att_l = sweep_query_blocks(lambda qb: gqa_attend(qb, k_all, v_all), ql)
    out_l = jnp.concatenate([conv_l, att_l.reshape(b, t, -1)], axis=-1) @ w_out
    out_c = None
    if need_ctx:
        att_c = gqa_attend(qc, kc, vc)
        out_c = jnp.concatenate([conv_c, att_c.reshape(hc.shape[0], hc.shape[1], -1)], axis=-1) @ w_out
    return out_c, out_l


def mla_mixer(hc, hl, w_down, q_norm, kv_norm, w_uq, w_ukv, w_o, cos, sin, need_ctx):
    def project(h):
        b, t = h.shape[:2]
        cq, ckv, kr = jnp.split(h @ w_down, [MLA_Q_RANK, MLA_Q_RANK + MLA_KV_RANK], axis=-1)
        q = (rms_norm(cq, q_norm) @ w_uq).reshape(b, t, MLA_HEADS, MLA_NOPE + MLA_ROPE)
        kv = (rms_norm(ckv, kv_norm) @ w_ukv).reshape(b, t, MLA_HEADS, MLA_NOPE + MLA_V)
        qn, qr = jnp.split(q, [MLA_NOPE], axis=-1)
        kn, v = jnp.split(kv, [MLA_NOPE], axis=-1)
        return qn, qr, kn, kr, v

    b, t = hl.shape[:2]
    qnl, qrl, knl, krl, vl = project(hl)
    qrl = apply_rope(qrl, cos[:, None], sin[:, None])
    krl = apply_rope(krl, cos, sin)
    qnc, qrc, knc, krc, vc = project(hc)
    kn_all = jnp.concatenate([knc, knl], axis=1)
    kr_all = jnp.concatenate([krc, krl], axis=1)
    v_all = jnp.concatenate([vc, vl], axis=1)
    att_l = sweep_query_blocks(lambda qn_b, qr_b: mla_attend(qn_b, qr_b, kn_all, kr_all, v_all), qnl, qrl)
    out_l = att_l.reshape(b, t, -1) @ w_o
    out_c = None
    if need_ctx:
        out_c = mla_attend(qnc, qrc, knc, krc, vc).reshape(hc.shape[0], hc.shape[1], -1) @ w_o
    return out_c, out_l


def setup_inputs(seed: int = 0) -> dict:
    key = jax.random.key(seed)
    ks = iter(jax.random.split(key, 32))
    f32 = jnp.float32

    def nrm(shape, fan_in, g=1.0):
        return g * fan_in ** -0.5 * jax.random.normal(next(ks), shape, f32)

    def gain(shape):
        return 1.0 + 0.05 * jax.random.normal(next(ks), shape, f32)

    return {
        'x': jax.random.normal(next(ks), (BATCH, SEQ, D_MODEL), f32),
        'c': jax.random.normal(next(ks), (BATCH, D_MODEL), f32),
        'ctx': jax.random.normal(next(ks), (BATCH, CTX_LEN, D_MODEL), f32),
        'c_ctx': jax.random.normal(next(ks), (D_MODEL,), f32),
        'mod_w': nrm((DEPTH, D_MODEL, N_MOD * D_MODEL), D_MODEL, 0.5),
        'mod_b': 0.02 * jax.random.normal(next(ks), (DEPTH, N_MOD * D_MODEL), f32),
        'norm_ffn1': gain((DEPTH, D_MODEL)),
        'norm_mix': gain((DEPTH, D_MODEL)),
        'norm_ffn2': gain((DEPTH, D_MODEL)),
        'ffn1_w_gate': nrm((DEPTH, D_MODEL, D_FF), D_MODEL),
        'ffn1_w_up': nrm((DEPTH, D_MODEL, D_FF), D_MODEL),
        'ffn1_w_down': nrm((DEPTH, D_FF, D_MODEL), D_FF),
        'ffn2_w_gate': nrm((DEPTH, D_MODEL, D_FF), D_MODEL),
        'ffn2_w_up': nrm((DEPTH, D_MODEL, D_FF), D_MODEL),
        'ffn2_w_down': nrm((DEPTH, D_FF, D_MODEL), D_FF),
        'hyb_w_in': nrm((N_EVEN, D_MODEL, HYB_IN), D_MODEL),
        'hyb_conv_w': nrm((N_EVEN, 3, CONV_CHANNELS), 3),
        'hyb_q_norm': gain((N_EVEN, HEAD_DIM)),
        'hyb_k_norm': gain((N_EVEN, HEAD_DIM)),
        'hyb_w_out': nrm((N_EVEN, HYB_OUT, D_MODEL), HYB_OUT),
        'mla_w_down': nrm((N_ODD, D_MODEL, MLA_DOWN), D_MODEL),
        'mla_q_norm': gain((N_ODD, MLA_Q_RANK)),
        'mla_kv_norm': gain((N_ODD, MLA_KV_RANK)),
        'mla_w_uq': nrm((N_ODD, MLA_Q_RANK, MLA_HEADS * (MLA_NOPE + MLA_ROPE)), MLA_Q_RANK),
        'mla_w_ukv': nrm((N_ODD, MLA_KV_RANK, MLA_HEADS * (MLA_NOPE + MLA_V)), MLA_KV_RANK),
        'mla_w_o': nrm((N_ODD, MLA_HEADS * MLA_V, D_MODEL), MLA_HEADS * MLA_V),
        'final_norm': gain((D_MODEL,)),
    }


def reference(x, c, ctx, c_ctx, mod_w, mod_b, norm_ffn1, norm_mix, norm_ffn2,
              ffn1_w_gate, ffn1_w_up, ffn1_w_down, ffn2_w_gate, ffn2_w_up, ffn2_w_down,
              hyb_w_in, hyb_conv_w, hyb_q_norm, hyb_k_norm, hyb_w_out,
              mla_w_down, mla_q_norm, mla_kv_norm, mla_w_uq, mla_w_ukv, mla_w_o, final_norm):
    n_lat = x.shape[1]
    cos_a, sin_a = axial_rope_tables(n_lat, HEAD_DIM, x.dtype)
    cos_m, sin_m = axial_rope_tables(n_lat, MLA_ROPE, x.dtype)
    silu_c = jax.nn.silu(c)
    silu_cc = jax.nn.silu(c_ctx)
    xl, xc = x, ctx
    for layer in range(DEPTH):
        need_ctx = layer < DEPTH - 1
        m_l = jnp.split((silu_c @ mod_w[layer] + mod_b[layer])[:, None, :], N_MOD, axis=-1)
        m_c = jnp.split(silu_cc @ mod_w[layer] + mod_b[layer], N_MOD, axis=-1)
        f1 = (ffn1_w_gate[layer], ffn1_w_up[layer], ffn1_w_down[layer])
        f2 = (ffn2_w_gate[layer], ffn2_w_up[layer], ffn2_w_down[layer])

        xl = xl + 0.5 * m_l[2] * swiglu(modulated_norm(xl, norm_ffn1[layer], m_l[0], m_l[1]), *f1)
        xc = xc + 0.5 * m_c[2] * swiglu(modulated_norm(xc, norm_ffn1[layer], m_c[0], m_c[1]), *f1)

        hl = modulated_norm(xl, norm_mix[layer], m_l[3], m_l[4])
        hc = modulated_norm(xc, norm_mix[layer], m_c[3], m_c[4])
        i = layer // 2
        if layer % 2 == 0:
            mc, ml = conv_attn_mixer(hc, hl, hyb_w_in[i], hyb_conv_w[i], hyb_q_norm[i], hyb_k_norm[i],
                                     hyb_w_out[i], cos_a, sin_a, need_ctx)
        else:
            mc, ml = mla_mixer(hc, hl, mla_w_down[i], mla_q_norm[i], mla_kv_norm[i], mla_w_uq[i],
                               mla_w_ukv[i], mla_w_o[i], cos_m, sin_m, need_ctx)
        xl = xl + m_l[5] * ml

        xl = xl + 0.5 * m_l[8] * swiglu(modulated_norm(xl, norm_ffn2[layer], m_l[6], m_l[7]), *f2)
        if need_ctx:
            xc = xc + m_c[5] * mc
            xc = xc + 0.5 * m_c[8] * swiglu(modulated_norm(xc, norm_ffn2[layer], m_c[6], m_c[7]), *f2)
    return rms_norm(xl, final_norm)
```

```python
import contextlib
import numpy as np
import concourse.bass as bass
import concourse.mybir as mybir
from concourse.bass_utils import run_bass_kernel_spmd

F32 = mybir.dt.float32
BF16 = mybir.dt.bfloat16
ALU = mybir.AluOpType
AF = mybir.ActivationFunctionType

D = 2048
KC = 16
DFF = 5632
NFG = 11
NCTX = 256
NLAT = 2048
NTOK = 2304
TP = 768
EPS = 1e-6
NSLOT = 4
SLOT = 8192

PASS_CGS = [
    [("c", 0, 256), ("l", 0, 512)],
    [("l", 512, 384), ("l", 896, 384)],
    [("l", 1280, 384), ("l", 1664, 384)],
]

ENGS = ("pe", "act", "dve", "pool", "sp")


class Buf:
    __slots__ = ("name", "lw", "rd", "pend", "excl")

    def __init__(self, name, excl=False):
        self.name = name
        self.lw = None
        self.rd = {}
        self.pend = None
        self.excl = excl


class Sched:
    def __init__(self):
        self.streams = {e: [] for e in ENGS}
        self.cnt = {}
        self.waited = {e: {} for e in ENGS}
        self.pending = {e: [] for e in ENGS}
        self.ninstr = 0

    def _deps(self, eng, reads, writes):
        deps = {}

        def add(k, v):
            if deps.get(k, 0) < v:
                deps[k] = v
        for b in reads:
            if b.pend is not None and b.pend != eng:
                raise RuntimeError(f"buffer {b.name} pending on {b.pend}, used by {eng}")
            if b.lw is not None:
                add(*b.lw)
        for b in writes:
            if b.pend is not None and b.pend != eng:
                raise RuntimeError(f"buffer {b.name} pending on {b.pend}, used by {eng}")
            if b.lw is not None:
                add(*b.lw)
            for k, v in b.rd.items():
                add(k, v)
        waits = []
        w = self.waited[eng]
        for k, v in deps.items():
            if k == "pe" and eng == "pe":
                continue
            if w.get(k, 0) >= v:
                continue
            w[k] = v
            waits.append((k, v))
        return waits

    def op(self, eng, fn, reads=(), writes=(), inc=True, semkey=None, amount=1):
        if eng != "pe" and any(b.excl for b in reads):
            writes = list(writes) + [b for b in reads if b.excl]
            reads = [b for b in reads if not b.excl]
        waits = self._deps(eng, reads, writes)
        self.ninstr += 1
        if not inc:
            self.streams[eng].append((waits, fn, None))
            for b in reads:
                self.pending[eng].append((b, "r")); b.pend = eng
            for b in writes:
                self.pending[eng].append((b, "w")); b.pend = eng
            return None
        k = semkey if semkey is not None else eng
        v = self.cnt.get(k, 0) + amount
        self.cnt[k] = v
        self.streams[eng].append((waits, fn, (k, amount)))
        acc = self.pending[eng] + [(b, "r") for b in reads] + [(b, "w") for b in writes]
        self.pending[eng] = []
        for b, m in acc:
            b.pend = None
        for b, m in acc:
            if m == "r":
                if b.rd.get(k, 0) < v:
                    b.rd[k] = v
        for b, m in acc:
            if m == "w":
                b.lw = (k, v)
                b.rd = {}
        return (k, v)

    def dma(self, eng, out, in_, semkey, reads=(), writes=()):
        return self.op(eng, lambda e: e.dma_start(out=out, in_=in_), reads=reads, writes=writes,
                       semkey=semkey, amount=16)

    def barrier(self, engs=("pe", "act", "dve", "sp")):
        for e in ("pe", "act", "dve"):
            assert not self.pending[e], f"pending on {e} at barrier"
        snap = {k: v for k, v in self.cnt.items() if not (k.startswith("w") or k == "pool")}
        for e in engs:
            waits = []
            for k, v in snap.items():
                if self.waited[e].get(k, 0) >= v:
                    continue
                self.waited[e][k] = v
                waits.append((k, v))
            self.streams[e].append((waits, None, None))

    def final_wait(self, eng, bufs):
        waits = self._deps(eng, [], bufs)
        self.streams[eng].append((waits, None, None))


def emit_program(nc, sched):
    keys = list(sched.cnt.keys())
    with contextlib.ExitStack() as st:
        sems = {k: st.enter_context(nc.semaphore(f"s_{k}")) for k in keys}
        block = st.enter_context(nc.Block())

        def run(engname):
            def body(e):
                for waits, fn, inc in sched.streams[engname]:
                    for (k, v) in waits:
                        e.wait_ge(sems[k], v)
                    if fn is None:
                        continue
                    ins = fn(e)
                    if inc is not None:
                        ins.then_inc(sems[inc[0]], inc[1])
            return body

        block.tensor(run("pe"))
        block.scalar(run("act"))
        block.vector(run("dve"))
        block.gpsimd(run("pool"))
        block.sync(run("sp"))


O_CVEC, O_MODB, O_NORM, O_CONVW, O_QKN, NSMALL = 0, 32, 320, 432, 456, 468


def build_program(stop_after=None, dbg=False, skip=()):
    nc = bass.Bass("TRN2", target_bir_lowering=False)
    S = Sched()
    dram_in = lambda name, shape: nc.dram_tensor(name, shape, F32, kind="ExternalInput").ap()
    xin = dram_in("xin", [NTOK, D])
    small_d = dram_in("small", [128, NSMALL])
    ident_d = dram_in("ident", [128, 128])
    pt_d = dram_in("pt", [128, 256])
    ropeA_d = dram_in("ropeA", [128, 2, NLAT])
    ropeM_d = dram_in("ropeM", [64, 2, NLAT])
    mod_w = dram_in("mod_w", [2, D, 9 * D])
    Wf = {n: dram_in(n, [2, D, DFF]) for n in ("ffn1_w_gate", "ffn1_w_up", "ffn2_w_gate", "ffn2_w_up")}
    Wf.update({n: dram_in(n, [2, DFF, D]) for n in ("ffn1_w_down", "ffn2_w_down")})
    hyb_w_in = dram_in("hyb_w_in", [1, D, 4608])
    hyb_w_out = dram_in("hyb_w_out", [1, D, D])
    mla_w_down = dram_in("mla_w_down", [1, D, 1344])
    mla_w_uq = dram_in("mla_w_uq", [1, 768, 3072])
    mla_w_ukv = dram_in("mla_w_ukv", [1, 512, 4096])
    mla_w_o = dram_in("mla_w_o", [1, D, D])
    out_d = nc.dram_tensor("out", [NLAT, D], F32, kind="ExternalOutput").ap()

    scr = lambda name, shape, dt: nc.dram_tensor(name, shape, dt, kind="Internal").ap()
    xT_scr = (nc.dram_tensor("xT_scr", [128, KC, NTOK], F32, kind="ExternalOutput").ap() if dbg
              else scr("xT_scr", [128, KC, NTOK], F32))
    q0_scr = scr("q0_scr", [128, 8, NTOK], BF16)
    k0_scr = scr("k0_scr", [128, 2, NTOK], BF16)
    v0_scr = scr("v0_scr", [128, 18, 256], BF16)
    gb_scr = scr("gb_scr", [128, 8, NTOK], F32)
    gu_scr = scr("gu_scr", [128, 8, NTOK], F32)
    qn_scr = scr("qn_scr", [128, 16, NTOK], BF16)
    qr_scr = scr("qr_scr", [64, 16, NTOK], BF16)
    kn_scr = scr("kn_scr", [128, 16, NTOK], BF16)
    kr_scr = scr("kr_scr", [64, NTOK], BF16)
    v1_scr = scr("v1_scr", [128, 18, 2048], BF16)

    with contextlib.ExitStack() as st:
        def sb(name, shape, dt):
            return st.enter_context(nc.sbuf_tensor(name, shape, dt))

        hT = sb("hT", [128, KC, TP], BF16)
        wslot = [sb(f"wslot{i}", [128, SLOT], BF16) for i in range(NSLOT)]
        region = sb("region", [128, 19456], F32)
        small = sb("small_sb", [128, NSMALL], F32)
        ident = sb("ident_sb", [128, 128], F32)
        ptb = sb("ptb", [128, 256], BF16)
        ones = sb("ones", [128, 128], BF16)
        sc32 = sb("sc32", [128, 32], F32)
        scb = sb("scb", [128, KC, 2], BF16)
        modsT = sb("modsT", [128, 2 * 288], F32)
        coef = sb("coef", [128, 2 * 2 * 3 * 3 * 16], F32)
        rstd = sb("rstd", [128, TP], F32)
        sqb = [sb(f"sqb{i}", [128, TP], BF16) for i in range(2)]
        tmpb = [sb(f"tmpb{i}", [128, TP], F32) for i in range(2)]
        sil = [sb(f"sil{i}", [128, 512], F32) for i in range(2)]
        stg = [sb(f"stg{i}", [128, 1024], F32) for i in range(2)]
        banks = [st.enter_context(nc.psum_tensor(f"bank{i}", [128, 512], F32)) for i in range(8)]

        xT = region[:, 0:KC * TP].rearrange("p (k t) -> p k t", k=KC)
        A_region = region[:, KC * TP:KC * TP + 3072].bitcast(BF16)
        A_t = [A_region[:, i * 3072:(i + 1) * 3072].rearrange("p (f t) -> p f t", f=4) for i in range(2)]

        class WS:
            def __init__(self):
                self.off = 0

            def f32(self, n):
                a = region[:, self.off:self.off + n]
                self.off += n
                assert self.off <= 19456
                return a

            def bf(self, n):
                m = (n + 1) // 2
                a = region[:, self.off:self.off + m].bitcast(BF16)
                self.off += m
                assert self.off <= 19456
                return a[:, 0:n]

        bankB = [Buf(f"bank{i}", excl=True) for i in range(8)]
        xB = [[Buf(f"x{d}_{c}") for c in range(2)] for d in range(KC)]
        hB = [[Buf(f"h{d}_{c}") for c in range(2)] for d in range(KC)]
        AB = [[[Buf(f"A{i}_{f}_{c}") for c in range(2)] for f in range(4)] for i in range(2)]
        wB = [Buf(f"wslot{i}") for i in range(NSLOT)]
        smallB, identB, ptbB, onesB, scB, scbB = (Buf(n) for n in ("small", "ident", "ptb", "ones", "sc32", "scb"))
        modsB = [Buf("mods0"), Buf("mods1")]
        coefB = [[Buf(f"coef{l}_{s_}") for s_ in range(3)] for l in range(2)]
        rstdB = [Buf("rstd0"), Buf("rstd1")]
        sqB = [Buf("sq0"), Buf("sq1")]
        tmpB = [Buf("tmp0"), Buf("tmp1")]
        silB = [Buf("sil0"), Buf("sil1")]
        stgB = [Buf("stg0"), Buf("stg1")]
        xscrB = [[Buf(f"xscr{p}_{d}") for d in range(KC)] for p in range(3)]
        outB = Buf("out")
        scrB = {n: [Buf(f"{n}{p}") for p in range(3)] for n in
                ("q0", "k0", "v0", "gb", "gu", "qn", "qr", "kn", "kr", "v1")}

        rr = {"b": 0, "sil": 0, "sq": 0, "tmp": 0}

        def nb():
            b = rr["b"]
            rr["b"] = (b + 1) % 6
            return b

        def MM(out, lhsT, rhs, start, stop, rd, wr, inc=None):
            S.op("pe", lambda e: e.matmul(out, lhsT, rhs, start=start, stop=stop), reads=rd, writes=wr,
                 inc=(stop if inc is None else inc))

        def ACT(out, in_, func, rd, wr, bias=None, scale=None):
            kw = {}
            if bias is not None:
                kw["bias"] = bias
            if scale is not None:
                kw["scale"] = scale
            S.op("act", lambda e: e.activation(out, in_, func, **kw), reads=rd, writes=wr)

        def TT(out, a, b, op, rd, wr, eng="dve"):
            S.op(eng, lambda e: e.tensor_tensor(out, a, b, op), reads=rd, writes=wr)

        def STT(out, in0, scalar, in1, op0, op1, rd, wr, eng="dve"):
            S.op(eng, lambda e: e.scalar_tensor_tensor(out, in0, scalar, in1, op0, op1), reads=rd, writes=wr)

        def TS(out, in0, s1, op0, rd, wr, s2=None, op1=None, eng="dve"):
            if op1 is None:
                S.op(eng, lambda e: e.tensor_scalar(out, in0, s1, None, op0), reads=rd, writes=wr)
            else:
                S.op(eng, lambda e: e.tensor_scalar(out, in0, s1, s2, op0, op1), reads=rd, writes=wr)

        def RSQRT(dst, src, rd, wr, scale):
            ACT(dst, src, AF.Ln, rd, wr, bias=EPS, scale=scale)
            ACT(dst, dst, AF.Exp, wr, wr, scale=-0.5)

        def RECIP(out, in_, rd, wr):
            S.op("dve", lambda e: e.reciprocal(out, in_), reads=rd, writes=wr)

        def COPY(out, in_, rd, wr, eng):
            if eng == "act":
                S.op("act", lambda e: e.activation(out, in_, AF.Copy), reads=rd, writes=wr)
            else:
                S.op(eng, lambda e: e.tensor_copy(out, in_), reads=rd, writes=wr)

        def MEMSET(ap, val, wr, eng="dve"):
            S.op(eng, lambda e: e.memset(ap, val), writes=wr)

        def V3(ap2d, a, b):
            return ap2d[:, 0:a * b].rearrange("p (a b) -> p a b", a=a)

        def wsrc(w2d, c0, n):
            return w2d.rearrange("(k p) f -> p k f", p=128)[:, :, c0:c0 + n]

        stages = []

        def stage(specs, fn):
            stages.append((specs, fn))

        def cginfo(p):
            res = []
            off = 0
            for ci, (kind, s0, n) in enumerate(PASS_CGS[p]):
                res.append(dict(ci=ci, kind=kind, s0=s0, n=n, off=off, g0=p * TP + off, v=(1 if kind == "c" else 0)))
                off += n
            return res

        def cg_of_tile(p, tt):
            for cg in cginfo(p):
                if cg["off"] <= tt * 128 < cg["off"] + cg["n"]:
                    return cg["ci"]
            raise AssertionError

        def coef_col(l, v, s, which, d):
            i = ((((l * 2 + v) * 3 + s) * 3 + which) * 16) + d
            return coef[:, i:i + 1]

        def init_stage(_):
            S.dma("sp", small[:], small_d, "ld_small", writes=[smallB])
            S.dma("sp", ident[:], ident_d, "ld_ident", writes=[identB])
            S.dma("pool", ptb[:], pt_d, "w_pt", writes=[ptbB])
            MEMSET(ones[:], 1.0, [onesB])
            ACT(sc32[:], small[:, O_CVEC:O_CVEC + 32], AF.Silu, [smallB], [scB])
            for v in range(2):
                COPY(scb[:, :, v], sc32[:, v * 16:(v + 1) * 16], [scB], [scbB], "dve")
        stage([], init_stage)

        modq = []

        def mod_stage(l, ct):
            spec = [lambda wt: [(V3(wt, 16, 512), wsrc(mod_w[l], ct * 512, 512))]]

            def fn(slots):
                (wt, wb), = slots
                w3 = V3(wt, 16, 512)
                for cc in range(4):
                    for k in range(KC):
                        MM(banks[6][:, cc * 2:(cc + 1) * 2], w3[:, k, cc * 128:(cc + 1) * 128], scb[:, k, :],
                           k == 0, k == KC - 1, [wb, scbB], [bankB[6]])
                pv = banks[6][:, 0:8].rearrange("p (m v) -> p m v", v=2)
                mv = modsT[:, l * 288:(l + 1) * 288].rearrange("p (m v) -> p m v", v=2)
                for v in range(2):
                    TT(mv[:, ct * 4:(ct + 1) * 4, v], pv[:, :, v],
                       small[:, O_MODB + l * 144 + ct * 4:O_MODB + l * 144 + (ct + 1) * 4], ALU.add,
                       [bankB[6], smallB], [modsB[l]])
                if ct % 12 in (7, 11):
                    s = ct // 12
                    for v in range(2):
                        def mcol(m):
                            return mv[:, m * 16:(m + 1) * 16, v]
                        base = (((l * 2 + v) * 3 + s) * 3) * 16
                        g = small[:, O_NORM + (l * 3 + s) * 16:O_NORM + (l * 3 + s + 1) * 16]
                        if ct % 12 == 7:
                            STT(coef[:, base:base + 16], mcol(3 * s + 1), 1.0, g, ALU.add, ALU.mult,
                                [modsB[l], smallB], [coefB[l][s]])
                            COPY(coef[:, base + 16:base + 32], mcol(3 * s), [modsB[l]], [coefB[l][s]], "dve")
                        else:
                            TS(coef[:, base + 32:base + 48], mcol(3 * s + 2), (1.0 if s == 1 else 0.5), ALU.mult,
                               [modsB[l]], [coefB[l][s]])
            stage(spec, fn)

        def drain_mods(n=1, urgent_only=False):
            while n > 0 and modq:
                if urgent_only and not modq[0][0]:
                    return
                _, l, ct = modq.pop(0)
                mod_stage(l, ct)
                n -= 1

        if "mods" not in skip:
            load_x_from_input_early = True
            modq.extend((True, 0, ct) for ct in range(0, 24))
            modq.extend((False, 0, ct) for ct in range(24, 36))
            modq.extend((False, 1, ct) for ct in range(36))

        def load_x_from_input(p):
            def fn(_):
                for tt in range(6):
                    gt = p * 6 + tt
                    ci = cg_of_tile(p, tt)
                    for half in range(2):
                        sg = half
                        S.dma("sp", stg[sg][:], xin[gt * 128:(gt + 1) * 128, half * 1024:(half + 1) * 1024],
                              f"ld_stg{sg}", writes=[stgB[sg]])
                        for q in range(2):
                            b = nb()
                            for j in range(4):
                                dl = q * 4 + j
                                S.op("pe", lambda e, b=b, j=j, dl=dl, sg=sg: e.transpose(
                                    banks[b][:, j * 128:(j + 1) * 128], stg[sg][:, dl * 128:(dl + 1) * 128], ident[:]),
                                    reads=[stgB[sg], identB], writes=[bankB[b]], inc=(j == 3))
                            d0 = half * 8 + q * 4
                            COPY(xT[:, d0:d0 + 4, tt * 128:(tt + 1) * 128],
                                 banks[b][:, 0:512].rearrange("p (a t) -> p a t", a=4),
                                 [bankB[b]], [xB[d][ci] for d in range(d0, d0 + 4)], "act" if q == 0 else "dve")
            stage([], fn)

        def load_x_from_scr(p, cgs):
            def fn(_):
                c0 = cgs[0]["off"]; c1 = cgs[-1]["off"] + cgs[-1]["n"]
                for d in range(KC):
                    S.dma("sp", xT[:, d, c0:c1], xT_scr[:, d, p * TP + c0:p * TP + c1], f"ld_x{d}",
                          reads=[xscrB[p][d]], writes=[xB[d][cg["ci"]] for cg in cgs])
            stage([], fn)

        def store_x_chunk(p, cgs, d):
            c0 = cgs[0]["off"]; c1 = cgs[-1]["off"] + cgs[-1]["n"]
            S.dma("sp", xT_scr[:, d, p * TP + c0:p * TP + c1], xT[:, d, c0:c1], f"st_x{d}",
                  reads=[xB[d][cg["ci"]] for cg in cgs], writes=[xscrB[p][d]])

        def store_x_to_scr(p, cgs):
            def fn(_):
                for d in range(KC):
                    store_x_chunk(p, cgs, d)
            stage([], fn)

        def norm_stats(cg, nchunks, src_fn, src_bufs_fn, inv_n):
            n, off, ci = cg["n"], cg["off"], cg["ci"]
            for d in range(nchunks):
                i = rr["sq"]; rr["sq"] ^= 1
                ACT(sqb[i][:, 0:n], src_fn(d), AF.Square, src_bufs_fn(d), [sqB[i]])
                MM(banks[7][:, 0:n], ones[:], sqb[i][:, 0:n], d == 0, d == nchunks - 1, [onesB, sqB[i]], [bankB[7]], inc=True)
            RSQRT(rstd[:, off:off + n], banks[7][:, 0:n], [bankB[7]], [rstdB[ci]], inv_n)

        def modnorm(l, s, cgs):
            def fn(_):
                merged = (len(cgs) == 2 and cgs[0]["v"] == cgs[1]["v"]
                          and cgs[0]["off"] + cgs[0]["n"] == cgs[1]["off"])
                groups = [list(cgs)] if merged else [[cg] for cg in cgs]
                for grp in groups:
                    off = grp[0]["off"]; n = sum(cg["n"] for cg in grp); v = grp[0]["v"]
                    cis = [cg["ci"] for cg in grp]
                    for d in range(KC):
                        i = rr["sq"]; rr["sq"] ^= 1
                        xin_ = xT[:, d, off:off + n]
                        if d % 2 == 0:
                            ACT(sqb[i][:, 0:n], xin_, AF.Square, [xB[d][ci] for ci in cis], [sqB[i]])
                        else:
                            TT(sqb[i][:, 0:n], xin_, xin_, ALU.mult, [xB[d][ci] for ci in cis], [sqB[i]])
                        for gi, cg in enumerate(grp):
                            o2 = cg["off"] - off
                            MM(banks[7 - gi][:, 0:cg["n"]], ones[:], sqb[i][:, o2:o2 + cg["n"]], d == 0, d == KC - 1,
                               [onesB, sqB[i]], [bankB[7 - gi]], inc=True)
                    for gi, cg in enumerate(grp):
                        RSQRT(rstd[:, cg["off"]:cg["off"] + cg["n"]], banks[7 - gi][:, 0:cg["n"]], [bankB[7 - gi]],
                              [rstdB[cg["ci"]]], 1.0 / D)
                    for d in range(KC):
                        i = rr["tmp"]; rr["tmp"] ^= 1
                        TT(tmpb[i][:, 0:n], xT[:, d, off:off + n], rstd[:, off:off + n], ALU.mult,
                           [xB[d][ci] for ci in cis] + [rstdB[ci] for ci in cis], [tmpB[i]])
                        ACT(hT[:, d, off:off + n], tmpb[i][:, 0:n], AF.Identity, [tmpB[i], coefB[l][s]],
                            [hB[d][ci] for ci in cis], bias=coef_col(l, v, s, 1, d), scale=coef_col(l, v, s, 0, d))
            stage([], fn)

        def ffn(l, s, cgs, Wg, Wu, Wd, store_p=None):
            def gu_stage(fg):
                specs = [lambda wt: [(V3(wt, 16, 512), wsrc(Wg[l], fg * 512, 512))],
                         lambda wt: [(V3(wt, 16, 512), wsrc(Wu[l], fg * 512, 512))]]

                def fn(slots):
                    (wg, wgb), (wu, wub) = slots
                    wg3, wu3 = V3(wg, 16, 512), V3(wu, 16, 512)
                    ab = fg % 2
                    if fg == 0:
                        order = [(fl, cg) for cg in cgs for fl in range(4)]
                    else:
                        order = [(fl, cg) for fl in range(4) for cg in cgs]
                    for fl, cg in order:
                        n, off, ci = cg["n"], cg["off"], cg["ci"]
                        gbk, ubk = nb(), nb()
                        for k in range(KC):
                            MM(banks[gbk][:, 0:n], wg3[:, k, fl * 128:(fl + 1) * 128], hT[:, k, off:off + n],
                               k == 0, k == KC - 1, [wgb, hB[k][ci]], [bankB[gbk]])
                        for k in range(KC):
                            MM(banks[ubk][:, 0:n], wu3[:, k, fl * 128:(fl + 1) * 128], hT[:, k, off:off + n],
                               k == 0, k == KC - 1, [wub, hB[k][ci]], [bankB[ubk]])
                        i = rr["sil"]; rr["sil"] ^= 1
                        ACT(sil[i][:, 0:n], banks[gbk][:, 0:n], AF.Silu, [bankB[gbk]], [silB[i]])
                        TT(A_t[ab][:, fl, off:off + n], sil[i][:, 0:n], banks[ubk][:, 0:n], ALU.mult,
                           [silB[i], bankB[ubk]], [AB[ab][fl][ci]])
                stage(specs, fn)
                drain_mods(4 if (fg == 0 and modq and modq[0][0]) else 1)

            def down_stage(fg):
                specs = [lambda wt: [(V3(wt, 4, 2048),
                                      Wd[l][fg * 512:(fg + 1) * 512, :].rearrange("(f p) d -> p f d", p=128))]]

                def fn(slots):
                    (wd, wdb), = slots
                    wd3 = V3(wd, 4, 2048)
                    ab = fg % 2
                    for d in range(KC):
                        for cg in cgs:
                            n, off, ci, v = cg["n"], cg["off"], cg["ci"], cg["v"]
                            yb = nb()
                            for fl in range(4):
                                MM(banks[yb][:, 0:n], wd3[:, fl, d * 128:(d + 1) * 128], A_t[ab][:, fl, off:off + n],
                                   fl == 0, fl == 3, [wdb, AB[ab][fl][ci]], [bankB[yb]])
                            STT(xT[:, d, off:off + n], banks[yb][:, 0:n], coef_col(l, v, s, 2, d), xT[:, d, off:off + n],
                                ALU.mult, ALU.add, [bankB[yb], xB[d][ci], coefB[l][s]], [xB[d][ci]])
                        if store_p is not None and fg == NFG - 1:
                            store_x_chunk(store_p, cgs, d)
                stage(specs, fn)
                if fg % 2 == 1:
                    drain_mods(1, urgent_only=True)

            if "ffn" in skip:
                if store_p is not None:
                    store_x_to_scr(store_p, cgs)
                return
            gu_stage(0)
            for fg in range(1, NFG):
                gu_stage(fg)
                down_stage(fg - 1)
            down_stage(NFG - 1)

        def barrier_stage(engs=("pe", "act", "dve", "sp")):
            stage([], lambda _: S.barrier(engs=engs))

        def proj_residual(l, cgs, W2d, src_fn, src_buf_fn, nk):
            def pstage(t4):
                specs = [lambda wt: [(V3(wt, nk, 512), wsrc(W2d, t4 * 512, 512))]]

                def fn(slots):
                    (wt, wb), = slots
                    w3 = V3(wt, nk, 512)
                    for dl in range(4):
                        d = t4 * 4 + dl
                        for cg in cgs:
                            n, off, ci, v = cg["n"], cg["off"], cg["ci"], cg["v"]
                            yb = nb()
                            for k in range(nk):
                                MM(banks[yb][:, 0:n], w3[:, k, dl * 128:(dl + 1) * 128], src_fn(k, off, n),
                                   k == 0, k == nk - 1, [wb, src_buf_fn(k, ci)], [bankB[yb]])
                            STT(xT[:, d, off:off + n], banks[yb][:, 0:n], coef_col(l, v, 1, 2, d), xT[:, d, off:off + n],
                                ALU.mult, ALU.add, [bankB[yb], xB[d][ci], coefB[l][1]], [xB[d][ci]])
                stage(specs, fn)
            for t4 in range(4):
                pstage(t4)

        pipeB, pipeC = [], []

        def pipe_step():
            nB = list(pipeB); pipeB.clear()
            nC = list(pipeC); pipeC.clear()
            for f in nB:
                f()
            for f in nC:
                f()

        def pipe_flush():
            while pipeB or pipeC:
                pipe_step()

        def rope_tail(P, n, src32, src32B, srcbf, srcbfB, ptv, cosv, sinv, ropeBuf, t1, t1B, t2, t2B, dst, dstB):
            b3 = nb()
            MM(banks[b3][0:P, 0:n], ptv, srcbf, True, True, [ptbB, srcbfB], [bankB[b3]])
            TT(t1, src32, cosv, ALU.mult, [src32B, ropeBuf], [t1B])
            TT(t2, banks[b3][0:P, 0:n], sinv, ALU.mult, [bankB[b3], ropeBuf], [t2B])
            TT(dst, t1, t2, ALU.add, [t1B, t2B], [dstB])

        def rope_apply(P, n, src32, src32B, srcbf, srcbfB, ptv, cosv, sinv, ropeBuf, t1, t1B, t2, t2B, dst, dstB):
            COPY(srcbf, src32, [src32B], [srcbfB], "act")
            b3 = nb()
            MM(banks[b3][0:P, 0:n], ptv, srcbf, True, True, [ptbB, srcbfB], [bankB[b3]])
            TT(t1, src32, cosv, ALU.mult, [src32B, ropeBuf], [t1B])
            TT(t2, banks[b3][0:P, 0:n], sinv, ALU.mult, [bankB[b3], ropeBuf], [t2B])
            TT(dst, t1, t2, ALU.add, [t1B, t2B], [dstB])

        def load_rope(p, cgs, tab_d, P, ropeT, ropeBuf, base=0):
            for cg in cgs:
                if cg["kind"] != "l":
                    continue
                n, off, s0 = cg["n"], cg["off"], cg["s0"]
                S.dma("sp", ropeT[base:base + P, :, off:off + n], tab_d[:, :, s0:s0 + n], "ld_rope", writes=[ropeBuf])

        def rope_hi_tail(n, src32, src32B, srcbf, srcbfB, cosv, sinv, ropeBuf, t1, t1B, t2, t2B, dst, dstB):
            b3 = nb()
            MM(banks[b3][:, 0:n], ptb[:, 128:256], srcbf[:, 0:n], True, True, [ptbB, srcbfB], [bankB[b3]])
            TT(t1[64:128, 0:n], src32[64:128, 0:n], cosv, ALU.mult, [src32B, ropeBuf], [t1B])
            TT(t2[64:128, 0:n], banks[b3][64:128, 0:n], sinv, ALU.mult, [bankB[b3], ropeBuf], [t2B])
            TT(dst, t1[64:128, 0:n], t2[64:128, 0:n], ALU.add, [t1B, t2B], [dstB])

        def hyb_inproj(p):
            cgs = cginfo(p)
            ws = WS()
            gcs = ws.f32(512); gust = [ws.f32(TP) for _ in range(2)]; gbst = [ws.f32(TP) for _ in range(2)]
            rsq_ = [ws.f32(512) for _ in range(2)]; qn32_ = [ws.f32(512) for _ in range(2)]
            t1_ = [ws.f32(512) for _ in range(2)]; t2_ = [ws.f32(512) for _ in range(2)]
            ropeT = ws.f32(2 * TP).rearrange("p (a t) -> p a t", a=2)
            sqq_ = [ws.bf(512) for _ in range(2)]; qnb_ = [ws.bf(512) for _ in range(2)]
            qkc = {"i": 0}
            qst = [ws.bf(TP) for _ in range(2)]
            vst = ws.bf(6 * 256).rearrange("p (a t) -> p a t", a=6)
            B_ = {n: Buf("hyb_" + n) for n in ("gcs", "gust0", "gust1", "gbst0", "gbst1", "rsq0", "qn320", "t10", "t20",
                                               "rsq1", "qn321", "t11", "t21", "sqq0", "qnb0", "sqq1", "qnb1",
                                               "rope", "qst0", "qst1", "vst")}
            W = hyb_w_in[0]

            def pre(_):
                load_rope(p, cgs, ropeA_d, 128, ropeT, B_["rope"])
            stage([], pre)

            def conv_stage(j):
                specs = [lambda wt: [(V3(wt, 16, 512)[:, :, i * 128:(i + 1) * 128], wsrc(W, i * 1024 + j * 128, 128))
                                     for i in range(3)]]

                def fn(slots):
                    (wt, wb), = slots
                    w3 = V3(wt, 16, 512)
                    i2 = j % 2
                    for cg in cgs:
                        n, off, ci = cg["n"], cg["off"], cg["ci"]
                        b0, b1, b2 = nb(), nb(), nb()
                        for bi, bk in enumerate((b0, b1, b2)):
                            for k in range(KC):
                                MM(banks[bk][:, 0:n], w3[:, k, bi * 128:(bi + 1) * 128], hT[:, k, off:off + n],
                                   k == 0, k == KC - 1, [wb, hB[k][ci]], [bankB[bk]])
                        COPY(gbst[i2][:, off:off + n], banks[b0][:, 0:n], [bankB[b0]], [B_[f"gbst{i2}"]], "act")
                        COPY(gcs[:, 0:n], banks[b1][:, 0:n], [bankB[b1]], [B_["gcs"]], "act")
                        TT(gust[i2][:, off:off + n], gcs[:, 0:n], banks[b2][:, 0:n], ALU.mult,
                           [B_["gcs"], bankB[b2]], [B_[f"gust{i2}"]])
                    S.dma("sp", gb_scr[:, j, p * TP:(p + 1) * TP], gbst[i2], f"st_gb{i2}",
                          reads=[B_[f"gbst{i2}"]], writes=[scrB["gb"][p]])
                    S.dma("sp", gu_scr[:, j, p * TP:(p + 1) * TP], gust[i2], f"st_gu{i2}",
                          reads=[B_[f"gust{i2}"]], writes=[scrB["gu"][p]])
                stage(specs, fn)

            def qk_chunk(w3, wb, col0, gaincol, cg, dst, dstB, after=None):
                n, off, ci = cg["n"], cg["off"], cg["ci"]
                z = qkc["i"]; qkc["i"] ^= 1
                rsq, qn32, t1, t2, sqq, qnb = rsq_[z], qn32_[z], t1_[z], t2_[z], sqq_[z], qnb_[z]
                Bz = {k: B_[f"{k}{z}"] for k in ("rsq", "qn32", "t1", "t2", "sqq", "qnb")}
                b = nb()
                for k in range(KC):
                    MM(banks[b][:, 0:n], w3[:, k, col0:col0 + 128], hT[:, k, off:off + n], k == 0, k == KC - 1,
                       [wb, hB[k][ci]], [bankB[b]])
                ACT(sqq[:, 0:n], banks[b][:, 0:n], AF.Square, [bankB[b]], [Bz["sqq"]])
                pipe_step()

                def Bf():
                    b2 = nb()
                    MM(banks[b2][:, 0:n], ones[:], sqq[:, 0:n], True, True, [onesB, Bz["sqq"]], [bankB[b2]])
                    RSQRT(rsq[:, 0:n], banks[b2][:, 0:n], [bankB[b2]], [Bz["rsq"]], 1.0 / 128)
                    if cg["kind"] == "c":
                        STT(dst[:, off:off + n], banks[b][:, 0:n], gaincol, rsq[:, 0:n], ALU.mult, ALU.mult,
                            [bankB[b], Bz["rsq"], smallB], [dstB])
                        if after is not None:
                            pipeC.append(after)
                    else:
                        STT(qn32[:, 0:n], banks[b][:, 0:n], gaincol, rsq[:, 0:n], ALU.mult, ALU.mult,
                            [bankB[b], Bz["rsq"], smallB], [Bz["qn32"]])
                        COPY(qnb[:, 0:n], qn32[:, 0:n], [Bz["qn32"]], [Bz["qnb"]], "act")

                        def Cf():
                            rope_tail(128, n, qn32[:, 0:n], Bz["qn32"], qnb[:, 0:n], Bz["qnb"], ptb[:, 0:128],
                                      ropeT[:, 0, off:off + n], ropeT[:, 1, off:off + n], B_["rope"],
                                      t1[:, 0:n], Bz["t1"], t2[:, 0:n], Bz["t2"], dst[:, off:off + n], dstB)
                            if after is not None:
                                after()
                        pipeC.append(Cf)
                pipeB.append(Bf)

            def q_stage(qt):
                specs = [lambda wt: [(V3(wt, 16, 512), wsrc(W, 3072 + qt * 512, 512))]]

                def fn(slots):
                    (wt, wb), = slots
                    w3 = V3(wt, 16, 512)
                    for hh in range(4):
                        h = qt * 4 + hh
                        i2 = h % 2

                        def store(h=h, i2=i2):
                            S.dma("sp", q0_scr[:, h, p * TP:(p + 1) * TP], qst[i2], f"st_q{i2}",
                                  reads=[B_[f"qst{i2}"]], writes=[scrB["q0"][p]])
                        for cg in cgs:
                            qk_chunk(w3, wb, hh * 128, small[:, O_QKN:O_QKN + 1], cg, qst[i2], B_[f"qst{i2}"],
                                     after=(store if cg is cgs[-1] else None))
                    pipe_flush()
                stage(specs, fn)

            def kv_stage():
                specs = [lambda wt: [(V3(wt, 16, 512), wsrc(W, 4096, 512))]]

                def fn(slots):
                    (wt, wb), = slots
                    w3 = V3(wt, 16, 512)
                    for g in range(2):
                        i2 = g % 2

                        def store(g=g, i2=i2):
                            S.dma("sp", k0_scr[:, g, p * TP:(p + 1) * TP], qst[i2], f"st_q{i2}",
                                  reads=[B_[f"qst{i2}"]], writes=[scrB["k0"][p]])
                        for cg in cgs:
                            qk_chunk(w3, wb, g * 128, small[:, O_QKN + 1:O_QKN + 2], cg, qst[i2], B_[f"qst{i2}"],
                                     after=(store if cg is cgs[-1] else None))
                    for tt in range(6):
                        ci = cg_of_tile(p, tt)
                        vb = nb()
                        for k in range(KC):
                            MM(banks[vb][:, 0:256], hT[:, k, tt * 128:(tt + 1) * 128], w3[:, k, 256:512],
                               k == 0, k == KC - 1, [wb, hB[k][ci]], [bankB[vb]])
                        COPY(vst[:, tt, :], banks[vb][:, 0:256], [bankB[vb]], [B_["vst"]], "act" if tt % 2 else "dve")
                        if tt < 4:
                            pipe_step()
                    pipe_flush()
                    S.dma("sp", v0_scr[:, p * 6:(p + 1) * 6, :], vst, "st_v", reads=[B_["vst"]], writes=[scrB["v0"][p]])
                stage(specs, fn)

            for j in range(8):
                conv_stage(j)
            for qt in range(2):
                q_stage(qt)
            kv_stage()

        def attention(p, cgs, layer):
            ws = WS()
            kT = [ws.bf(NTOK) for _ in range(2)]
            Vt = [ws.bf(18 * 128).rearrange("p (a t) -> p a t", a=18) for _ in range(2)]
            krT = ws.bf(NTOK)
            qh = [ws.bf(TP) for _ in range(2)]
            qrh = [ws.bf(TP) for _ in range(2)]
            pT = [ws.bf(512) for _ in range(4)]
            rec = ws.f32(512)
            gux = [ws.f32(514) for _ in range(2)]
            gbx = [ws.f32(512) for _ in range(2)]
            acc = [ws.f32(512) for _ in range(2)]
            B_ = {n: Buf(f"att_{n}") for n in ("kT0", "kT1", "V0", "V1", "krT", "qh0", "qh1", "qrh0", "qrh1",
                                               "pT0", "pT1", "pT2", "pT3", "rec", "gux0", "gux1", "gbx0", "gbx1", "acc0", "acc1")}
            sbanks, obanks, dbanks = [0, 1, 2, 3], [4, 5], [6, 7]
            st_ = {"o": 0, "q": 0, "kv": 0}
            allp = lambda name: scrB[name]
            PD = 2

            def mk_unit(q_ap, qB, qr_ap, qrB, kt_ap, ktB, kr_ap, krB, v_ap, vB, cg, scale, chunk):
                oi = st_["o"]; st_["o"] ^= 1
                return dict(q=q_ap, qB=qB, qr=qr_ap, qrB=qrB, kt=kt_ap, ktB=ktB, kr=kr_ap, krB=krB, v=v_ap, vB=vB,
                            cg=cg, scale=scale, chunk=chunk, nkt=(2 if cg["kind"] == "c" else 18),
                            ob=obanks[oi], db=dbanks[oi])

            def run_units(gen, side=()):
                side = list(side)
                flat = []
                it = iter(gen)

                def ensure(idx):
                    while len(flat) <= idx:
                        try:
                            u = next(it)
                        except StopIteration:
                            return False
                        for kt in range(u["nkt"]):
                            flat.append((u, kt))
                    return True

                def s_mm(idx):
                    u, kt = flat[idx]
                    n, off = u["cg"]["n"], u["cg"]["off"]
                    sbk = sbanks[idx % 4]
                    MM(banks[sbk][:, 0:n], u["kt"][:, kt * 128:(kt + 1) * 128], u["q"][:, off:off + n], True, u["qr"] is None,
                       [u["ktB"], u["qB"]], [bankB[sbk]])
                    if u["qr"] is not None:
                        MM(banks[sbk][:, 0:n], u["kr"][:, kt * 128:(kt + 1) * 128], u["qr"][:, off:off + n], False, True,
                           [u["krB"], u["qrB"]], [bankB[sbk]])

                t = 0
                issued = 0
                while ensure(t):
                    while issued <= t + PD and ensure(issued):
                        s_mm(issued)
                        issued += 1
                    u, kt = flat[t]
                    n, off, ci = u["cg"]["n"], u["cg"]["off"], u["cg"]["ci"]
                    sbk = sbanks[t % 4]
                    pt = pT[t % 4]; ptB_ = B_[f"pT{t % 4}"]
                    ACT(pt[:, 0:n], banks[sbk][:, 0:n], AF.Exp, [bankB[sbk]], [ptB_], scale=u["scale"])
                    last = kt == u["nkt"] - 1
                    ob, db = u["ob"], u["db"]
                    MM(banks[ob][:, 0:n], u["v"][:, kt, :], pt[:, 0:n], kt == 0, last, [u["vB"], ptB_], [bankB[ob]])
                    MM(banks[db][:, 0:n], ones[:], pt[:, 0:n], kt == 0, last, [onesB, ptB_], [bankB[db]])
                    if last:
                        RECIP(rec[:, 0:n], banks[db][:, 0:n], [bankB[db]], [B_["rec"]])
                        TT(hT[:, u["chunk"], off:off + n], banks[ob][:, 0:n], rec[:, 0:n], ALU.mult,
                           [bankB[ob], B_["rec"]], [hB[u["chunk"]][ci]])
                        if side:
                            side.pop(0)()
                    t += 1
                while side:
                    side.pop(0)()

            c0 = cgs[0]["off"]; c1 = cgs[-1]["off"] + cgs[-1]["n"]

            def gqa_units():
                for g in range(2):
                    i = st_["kv"]; st_["kv"] ^= 1
                    S.dma("sp", kT[i], k0_scr[:, g, :], f"ld_kT{i}", reads=allp("k0"), writes=[B_[f"kT{i}"]])
                    S.dma("sp", Vt[i], v0_scr[:, :, g * 128:(g + 1) * 128], f"ld_V{i}", reads=allp("v0"),
                          writes=[B_[f"V{i}"]])
                    for r in range(4):
                        h = g * 4 + r
                        qi = st_["q"]; st_["q"] ^= 1
                        S.dma("sp", qh[qi][:, c0:c1], q0_scr[:, h, p * TP + c0:p * TP + c1], f"ld_q{qi}",
                              reads=[scrB["q0"][p]], writes=[B_[f"qh{qi}"]])
                        for cg in cgs:
                            yield mk_unit(qh[qi], B_[f"qh{qi}"], None, None, kT[i], B_[f"kT{i}"], None, None,
                                          Vt[i], B_[f"V{i}"], cg, 128 ** -0.5, 8 + h)

            def mla_units():
                MEMSET(krT[64:128, :], 0.0, [B_["krT"]])
                for i_ in range(2):
                    MEMSET(qrh[i_][64:128, :], 0.0, [B_[f"qrh{i_}"]])
                S.dma("sp", krT[0:64, :], kr_scr, "ld_kr", reads=allp("kr"), writes=[B_["krT"]])
                for h in range(16):
                    i = st_["kv"]; st_["kv"] ^= 1
                    S.dma("sp", kT[i], kn_scr[:, h, :], f"ld_kT{i}", reads=allp("kn"), writes=[B_[f"kT{i}"]])
                    S.dma("sp", Vt[i], v1_scr[:, :, h * 128:(h + 1) * 128], f"ld_V{i}", reads=allp("v1"),
                          writes=[B_[f"V{i}"]])
                    qi = st_["q"]; st_["q"] ^= 1
                    S.dma("sp", qh[qi][:, c0:c1], qn_scr[:, h, p * TP + c0:p * TP + c1], f"ld_q{qi}",
                          reads=[scrB["qn"][p]], writes=[B_[f"qh{qi}"]])
                    S.dma("sp", qrh[qi][0:64, c0:c1], qr_scr[:, h, p * TP + c0:p * TP + c1], f"ld_qr{qi}",
                          reads=[scrB["qr"][p]], writes=[B_[f"qrh{qi}"]])
                    for cg in cgs:
                        yield mk_unit(qh[qi], B_[f"qh{qi}"], qrh[qi], B_[f"qrh{qi}"], kT[i], B_[f"kT{i}"], krT, B_["krT"],
                                      Vt[i], B_[f"V{i}"], cg, 192 ** -0.5, h)

            def gqa_fn(_):
                run_units(gqa_units(), side=conv_jobs())

            def mla_fn(_):
                run_units(mla_units())

            def conv_jobs():
                jobs = []
                cnt = {"u": 0}
                for j in range(8):
                    for cg in cgs:
                        def job(j=j, cg=cg):
                            n, off, ci, a = cg["n"], cg["off"], cg["ci"], cg["g0"]
                            b = a + n
                            s_lo, s_hi = (0, NCTX) if cg["kind"] == "c" else (NCTX, NTOK)
                            i = cnt["u"] % 2; cnt["u"] += 1
                            lo = max(a - 1, s_lo); hi = min(b + 1, s_hi)
                            if a - 1 < s_lo:
                                MEMSET(gux[i][:, 0:1], 0.0, [B_[f"gux{i}"]])
                            if b + 1 > s_hi:
                                MEMSET(gux[i][:, n + 1:n + 2], 0.0, [B_[f"gux{i}"]])
                            S.dma("sp", gux[i][:, lo - (a - 1):hi - (a - 1)], gu_scr[:, j, lo:hi], f"ld_gux{i}",
                                  reads=allp("gu"), writes=[B_[f"gux{i}"]])
                            S.dma("sp", gbx[i][:, 0:n], gb_scr[:, j, a:b], f"ld_gbx{i}", reads=allp("gb"),
                                  writes=[B_[f"gbx{i}"]])
                            cw = lambda tap: small[:, O_CONVW + tap * 8 + j:O_CONVW + tap * 8 + j + 1]
                            TS(acc[i][:, 0:n], gux[i][:, 1:n + 1], cw(1), ALU.mult, [B_[f"gux{i}"], smallB], [B_[f"acc{i}"]])
                            STT(acc[i][:, 0:n], gux[i][:, 0:n], cw(0), acc[i][:, 0:n], ALU.mult, ALU.add,
                                [B_[f"gux{i}"], smallB, B_[f"acc{i}"]], [B_[f"acc{i}"]])
                            STT(acc[i][:, 0:n], gux[i][:, 2:n + 2], cw(2), acc[i][:, 0:n], ALU.mult, ALU.add,
                                [B_[f"gux{i}"], smallB, B_[f"acc{i}"]], [B_[f"acc{i}"]])
                            TT(hT[:, j, off:off + n], gbx[i][:, 0:n], acc[i][:, 0:n], ALU.mult,
                               [B_[f"gbx{i}"], B_[f"acc{i}"]], [hB[j][ci]])
                        jobs.append(job)
                return jobs

            if layer == 0:
                stage([], gqa_fn)
            else:
                stage([], mla_fn)

        def mla_inproj(p):
            cgs = cginfo(p)
            lat = [cg for cg in cgs if cg["kind"] == "l"]
            ws = WS()
            ckv32 = ws.f32(4 * TP).rearrange("p (a t) -> p a t", a=4)
            kr32 = ws.f32(TP)
            rq = ws.f32(TP); rkv = ws.f32(TP)
            ropeT = ws.f32(2 * TP).rearrange("p (a t) -> p a t", a=2)
            qr32_ = [ws.f32(512) for _ in range(2)]; t1_ = [ws.f32(512) for _ in range(2)]
            t2_ = [ws.f32(512) for _ in range(2)]
            t1, t2 = t1_[0], t2_[0]
            cqg = ws.bf(6 * TP).rearrange("p (a t) -> p a t", a=6)
            ckvn = ws.bf(4 * TP).rearrange("p (a t) -> p a t", a=4)
            sq1_ = [ws.bf(512) for _ in range(2)]; qrb_ = [ws.bf(512) for _ in range(2)]; krb = ws.bf(TP)
            zz = {"sq": 0, "q": 0}
            qnst = [ws.bf(TP) for _ in range(2)]
            qrst = [ws.bf(TP) for _ in range(2)]
            knst = [ws.bf(TP) for _ in range(2)]
            krst = ws.bf(TP)
            vst = [ws.bf(512) for _ in range(2)]
            names = ["kr32", "rq0", "rq1", "rkv0", "rkv1", "rope", "qr320", "qr321", "t1", "t2", "t11", "t21", "sq10", "sq11",
                     "qrb0", "qrb1", "krb", "qnst0", "qnst1",
                     "qrst0", "qrst1", "knst0", "knst1", "krst", "vst0", "vst1"]
            B_ = {n: Buf("mla_" + n) for n in names}
            cqB = [[Buf(f"cqg{c}_{i}") for i in range(2)] for c in range(6)]
            ckv32B = [[Buf(f"ckv32{c}_{i}") for i in range(2)] for c in range(4)]
            ckvnB = [[Buf(f"ckvn{c}_{i}") for i in range(2)] for c in range(4)]
            Wd_, Wuq, Wukv = mla_w_down[0], mla_w_uq[0], mla_w_ukv[0]
            gq = lambda c: small[:, O_QKN + 2 + c:O_QKN + 3 + c]
            gkv = lambda c: small[:, O_QKN + 8 + c:O_QKN + 9 + c]

            def pre(_):
                if "mrope" not in skip:
                    load_rope(p, cgs, ropeM_d, 64, ropeT, B_["rope"], base=64)
            stage([], pre)

            def down_chunk(w3, wb, col0, M, cg):
                n, off, ci = cg["n"], cg["off"], cg["ci"]
                b = nb()
                for k in range(KC):
                    MM(banks[b][0:M, 0:n], w3[:, k, col0:col0 + M], hT[:, k, off:off + n], k == 0, k == KC - 1,
                       [wb, hB[k][ci]], [bankB[b]])
                return b

            def stats_acc(b, cg, first, last_, sbank, fin=None):
                n = cg["n"]
                if "mstats" in skip:
                    return
                z = zz["sq"]; zz["sq"] ^= 1
                sq1 = sq1_[z]; sqB_ = B_[f"sq1{z}"]
                ACT(sq1[:, 0:n], banks[b][:, 0:n], AF.Square, [bankB[b]], [sqB_])
                pipe_step()

                def Bf():
                    MM(banks[sbank][:, 0:n], ones[:], sq1[:, 0:n], first, last_, [onesB, sqB_], [bankB[sbank]], inc=True)
                    if fin is not None:
                        fin()
                pipeB.append(Bf)

            def fin_stats(cg, sbank, dst, dstB, inv_n):
                n, off = cg["n"], cg["off"]
                RSQRT(dst[:, off:off + n], banks[sbank][:, 0:n], [bankB[sbank]], [dstB], inv_n)

            def down_stage(t):
                ncol = 512 if t < 2 else 320
                specs = [lambda wt: [(V3(wt, 16, 512)[:, :, 0:ncol], wsrc(Wd_, t * 512, ncol))]]

                def fn(slots):
                    (wt, wb), = slots
                    w3 = V3(wt, 16, 512)
                    for cg in cgs:
                        n, off, ci = cg["n"], cg["off"], cg["ci"]
                        for cl in range(4 if t < 2 else 3):
                            gc = t * 4 + cl
                            if gc < 6:
                                b = down_chunk(w3, wb, cl * 128, 128, cg)
                                TS(cqg[:, gc, off:off + n], banks[b][:, 0:n], gq(gc), ALU.mult, [bankB[b], smallB],
                                   [cqB[gc][ci]])
                                stats_acc(b, cg, gc == 0, gc == 5, 7 - ci,
                                          fin=((lambda cg=cg, ci=ci: fin_stats(cg, 7 - ci, rq, B_[f"rq{ci}"], 1.0 / 768))
                                               if gc == 5 else None))
                            elif gc < 10:
                                c = gc - 6
                                b = down_chunk(w3, wb, cl * 128, 128, cg)
                                TS(ckv32[:, c, off:off + n], banks[b][:, 0:n], gkv(c), ALU.mult, [bankB[b], smallB],
                                   [ckv32B[c][ci]])
                                def fin_kv(cg=cg, ci=ci, n=n, off=off):
                                    fin_stats(cg, 7 - ci, rkv, B_[f"rkv{ci}"], 1.0 / 512)
                                    for c2 in range(4):
                                        TT(ckvn[:, c2, off:off + n], ckv32[:, c2, off:off + n], rkv[:, off:off + n],
                                           ALU.mult, [ckv32B[c2][ci], B_[f"rkv{ci}"]], [ckvnB[c2][ci]])
                                stats_acc(b, cg, c == 0, c == 3, 7 - ci, fin=(fin_kv if c == 3 else None))
                            else:
                                b = down_chunk(w3, wb, cl * 128 - 64, 128, cg)
                                if cg["kind"] == "c":
                                    COPY(krst[64:128, off:off + n], banks[b][64:128, 0:n], [bankB[b]], [B_["krst"]], "act")
                                else:
                                    COPY(kr32[:, off:off + n], banks[b][:, 0:n], [bankB[b]], [B_["kr32"]], "dve")
                                    COPY(krb[:, off:off + n], kr32[:, off:off + n], [B_["kr32"]], [B_["krb"]], "act")
                                    rope_hi_tail(n, kr32[:, off:off + n], B_["kr32"], krb[:, off:off + n], B_["krb"],
                                                 ropeT[64:128, 0, off:off + n], ropeT[64:128, 1, off:off + n], B_["rope"],
                                                 t1, B_["t1"], t2, B_["t2"], krst[64:128, off:off + n], B_["krst"])
                    pipe_flush()
                    if t == 2:
                        S.dma("sp", kr_scr[:, p * TP:(p + 1) * TP], krst[64:128, :], "st_kr", reads=[B_["krst"]],
                              writes=[scrB["kr"][p]])
                stage(specs, fn)

            def uq_stage(t):
                specs = [lambda wt: [(V3(wt, 6, 384), wsrc(Wuq, t * 384, 384))]]

                def fn(slots):
                    (wt, wb), = slots
                    w3 = V3(wt, 6, 384)
                    for hh in range(2):
                        h = t * 2 + hh
                        i2 = h % 2
                        if not lat:
                            continue
                        for cg in lat:
                            n, off, ci = cg["n"], cg["off"], cg["ci"]
                            b = nb()
                            for k in range(6):
                                MM(banks[b][:, 0:n], w3[:, k, hh * 192:hh * 192 + 128], cqg[:, k, off:off + n],
                                   k == 0, k == 5, [wb, cqB[k][ci]], [bankB[b]])
                            TT(qnst[i2][:, off:off + n], banks[b][:, 0:n], rq[:, off:off + n], ALU.mult,
                               [bankB[b], B_[f"rq{ci}"]], [B_[f"qnst{i2}"]])
                            b = nb()
                            for k in range(6):
                                MM(banks[b][:, 0:n], w3[:, k, hh * 192 + 64:hh * 192 + 192], cqg[:, k, off:off + n],
                                   k == 0, k == 5, [wb, cqB[k][ci]], [bankB[b]])
                            z = zz["q"]; zz["q"] ^= 1
                            qr32, qrb, t1z, t2z = qr32_[z], qrb_[z], t1_[z], t2_[z]
                            qB_, bB_, t1B_, t2B_ = B_[f"qr32{z}"], B_[f"qrb{z}"], B_["t1" if z == 0 else "t11"], B_["t2" if z == 0 else "t21"]
                            TT(qr32[:, 0:n], banks[b][:, 0:n], rq[:, off:off + n], ALU.mult,
                               [bankB[b], B_[f"rq{ci}"]], [qB_])
                            COPY(qrb[:, 0:n], qr32[:, 0:n], [qB_], [bB_], "act")
                            pipe_step()
                            is_last = cg is lat[-1]

                            def Cf(n=n, off=off, qr32=qr32, qrb=qrb, t1z=t1z, t2z=t2z, qB_=qB_, bB_=bB_, t1B_=t1B_, t2B_=t2B_,
                                   i2=i2, h=h, is_last=is_last):
                                rope_hi_tail(n, qr32, qB_, qrb, bB_,
                                             ropeT[64:128, 0, off:off + n], ropeT[64:128, 1, off:off + n], B_["rope"],
                                             t1z, t1B_, t2z, t2B_, qrst[i2][64:128, off:off + n], B_[f"qrst{i2}"])
                                if is_last:
                                    c0 = lat[0]["off"]; c1 = lat[-1]["off"] + lat[-1]["n"]
                                    S.dma("sp", qn_scr[:, h, p * TP + c0:p * TP + c1], qnst[i2][:, c0:c1], f"st_qn{i2}",
                                          reads=[B_[f"qnst{i2}"]], writes=[scrB["qn"][p]])
                                    S.dma("sp", qr_scr[:, h, p * TP + c0:p * TP + c1], qrst[i2][64:128, c0:c1], f"st_qr{i2}",
                                          reads=[B_[f"qrst{i2}"]], writes=[scrB["qr"][p]])
                            pipeB.append(Cf)
                    pipe_flush()
                stage(specs, fn)

            def ukv_stage(t):
                specs = [lambda wt: [(V3(wt, 4, 2048), wsrc(Wukv, t * 2048, 2048))]]

                def fn(slots):
                    (wt, wb), = slots
                    w3 = V3(wt, 4, 2048)
                    w4 = wt[:, 0:8192].rearrange("p (k h c) -> p k h c", k=4, h=8)
                    for hh in range(8):
                        h = t * 8 + hh
                        i2 = h % 2
                        for cg in cgs:
                            n, off, ci = cg["n"], cg["off"], cg["ci"]
                            b = nb()
                            for k in range(4):
                                MM(banks[b][:, 0:n], w3[:, k, hh * 256:hh * 256 + 128], ckvn[:, k, off:off + n],
                                   k == 0, k == 3, [wb, ckvnB[k][ci]], [bankB[b]])
                            COPY(knst[i2][:, off:off + n], banks[b][:, 0:n], [bankB[b]], [B_[f"knst{i2}"]],
                                 "act" if ci == 0 else "dve")
                        S.dma("sp", kn_scr[:, h, p * TP:(p + 1) * TP], knst[i2], f"st_kn{i2}",
                              reads=[B_[f"knst{i2}"]], writes=[scrB["kn"][p]])
                    u = 0
                    for tt in range(6):
                        ci = cg_of_tile(p, tt)
                        for hg in range(2):
                            b = nb()
                            for k in range(4):
                                MM(banks[b][:, 0:512], ckvn[:, k, tt * 128:(tt + 1) * 128], w4[:, k, hg * 4:(hg + 1) * 4, 128:256],
                                   k == 0, k == 3, [wb, ckvnB[k][ci]], [bankB[b]])
                            i2 = u % 2; u += 1
                            COPY(vst[i2][:, 0:512], banks[b][:, 0:512], [bankB[b]], [B_[f"vst{i2}"]], "act" if i2 else "dve")
                            h0 = t * 8 + hg * 4
                            S.dma("sp", v1_scr[:, p * 6 + tt, h0 * 128:(h0 + 4) * 128], vst[i2][:, 0:512], f"st_v1{i2}",
                                  reads=[B_[f"vst{i2}"]], writes=[scrB["v1"][p]])
                stage(specs, fn)

            for t in range(3):
                if "mdown" not in skip:
                    down_stage(t)
                if p == 0:
                    mark(f"A1d{t}")
            for t in range(8):
                uq_stage(t)
            if p == 0:
                mark("A1q")
            for t in range(2):
                ukv_stage(t)
            if p == 0:
                mark("A1k")

        def final_out(p, cgs):
            def fn(_):
                for cg in cgs:
                    n, off, ci = cg["n"], cg["off"], cg["ci"]
                    norm_stats(cg, KC, lambda d: xT[:, d, off:off + n], lambda d: [xB[d][ci]], 1.0 / D)
                    for d in range(KC):
                        STT(xT[:, d, off:off + n], xT[:, d, off:off + n], small[:, O_NORM + 96 + d:O_NORM + 97 + d],
                            rstd[:, off:off + n], ALU.mult, ALU.mult, [xB[d][ci], smallB, rstdB[ci]], [xB[d][ci]])
                    for tl in range(n // 128):
                        c0 = off + tl * 128
                        tok0 = cg["s0"] + tl * 128
                        for half in range(2):
                            sg = half
                            for q in range(2):
                                b = nb()
                                for j in range(4):
                                    d = half * 8 + q * 4 + j
                                    S.op("pe", lambda e, b=b, j=j, d=d, c0=c0: e.transpose(
                                        banks[b][:, j * 128:(j + 1) * 128], xT[:, d, c0:c0 + 128], ident[:]),
                                        reads=[xB[d][ci], identB], writes=[bankB[b]], inc=(j == 3))
                                COPY(stg[sg][:, q * 512:(q + 1) * 512], banks[b][:, 0:512], [bankB[b]], [stgB[sg]],
                                     "act" if q == 0 else "dve")
                            S.dma("sp", out_d[tok0:tok0 + 128, half * 1024:(half + 1) * 1024], stg[sg][:], f"st_stg{sg}",
                                  reads=[stgB[sg]], writes=[outB])
            stage([], fn)

        def mark(name):
            stage([], ("mark", name))

        for p in range(3):
            cgs = cginfo(p)
            load_x_from_input(p)
            if p == 0:
                drain_mods(8, urgent_only=True)
            modnorm(0, 0, cgs)
            ffn(0, 0, cgs, Wf["ffn1_w_gate"], Wf["ffn1_w_up"], Wf["ffn1_w_down"], store_p=p)
            if p == 0:
                drain_mods(100, urgent_only=True)
            modnorm(0, 1, cgs)
            barrier_stage()
            hyb_inproj(p)
            barrier_stage()
        mark("A0")
        for p in range(3):
            cgs = cginfo(p)
            attention(p, cgs, 0)
            barrier_stage(("sp",))
            mark(f"B0p{p}a")
            load_x_from_scr(p, cgs)
            proj_residual(0, cgs, hyb_w_out[0], lambda k, off, n: hT[:, k, off:off + n], lambda k, ci: hB[k][ci], 16)
            mark(f"B0p{p}b")
            modnorm(0, 2, cgs)
            ffn(0, 2, cgs, Wf["ffn2_w_gate"], Wf["ffn2_w_up"], Wf["ffn2_w_down"], store_p=p)
            barrier_stage(("act", "dve", "sp"))
            mark(f"B0p{p}c")
        mark("B0")
        drain_mods(100)
        for p in range(3):
            cgs = cginfo(p)
            load_x_from_scr(p, cgs)
            modnorm(1, 0, cgs)
            ffn(1, 0, cgs, Wf["ffn1_w_gate"], Wf["ffn1_w_up"], Wf["ffn1_w_down"], store_p=p)
            modnorm(1, 1, cgs)
            barrier_stage()
            mla_inproj(p)
            barrier_stage()
        mark("A1")
        for p in range(3):
            cgs = [cg for cg in cginfo(p) if cg["kind"] == "l"]
            attention(p, cgs, 1)
            barrier_stage(("sp",))
            load_x_from_scr(p, cgs)
            proj_residual(1, cgs, mla_w_o[0], lambda k, off, n: hT[:, k, off:off + n], lambda k, ci: hB[k][ci], 16)
            modnorm(1, 2, cgs)
            ffn(1, 2, cgs, Wf["ffn2_w_gate"], Wf["ffn2_w_up"], Wf["ffn2_w_down"], store_p=(p if dbg else None))
            final_out(p, cgs)
            barrier_stage(("act", "dve", "sp"))
        mark("B1")

        if stop_after is not None:
            cut = None
            for i, (specs, fn) in enumerate(stages):
                if isinstance(fn, tuple) and fn[1] == stop_after:
                    cut = i
            stages[:] = stages[:cut]
        stages[:] = [s for s in stages if not isinstance(s[1], tuple)]

        flat, first = [], []
        for specs, fn in stages:
            first.append(len(flat))
            flat.extend(specs)
        ptr, released = 0, 0
        for si, (specs, fn) in enumerate(stages):
            end = first[si] + len(specs)
            while ptr < len(flat) and ptr < released + NSLOT:
                sl = ptr % NSLOT
                for (o, i_) in flat[ptr](wslot[sl]):
                    S.dma("pool", o, i_, f"w{sl}", writes=[wB[sl]])
                ptr += 1
            assert ptr >= end
            fn([(wslot[n % NSLOT], wB[n % NSLOT]) for n in range(first[si], end)])
            released = end

        S.barrier(engs=("sp",))
        S.final_wait("sp", [outB] + [b for row in xscrB for b in row])
        emit_program(nc, S)
    return nc, S


def _rope_tables(n_tok, dim):
    grid_w = 64
    n_rows = n_tok // grid_w
    row = np.repeat(np.arange(n_rows), grid_w).astype(np.float32)
    col = np.tile(np.arange(grid_w), n_rows).astype(np.float32)
    half = dim // 2
    inv = (1.0 / (np.float32(10000.0) ** (np.arange(0, half, 2, dtype=np.float32) / np.float32(half)))).astype(np.float32)
    ang = np.concatenate([row[:, None] * inv, col[:, None] * inv], axis=-1).astype(np.float32)
    cos = np.cos(ang).astype(np.float32); sin = np.sin(ang).astype(np.float32)
    tab = np.stack([np.repeat(cos.T, 2, axis=0), np.repeat(sin.T, 2, axis=0)], axis=1)
    return np.ascontiguousarray(tab.astype(np.float32))


def _rot_lhsT(n):
    m = np.zeros((n, n), np.float32)
    for i in range(n // 2):
        m[2 * i + 1, 2 * i] = -1.0
        m[2 * i, 2 * i + 1] = 1.0
    return m


def _cols(v, nch):
    return np.ascontiguousarray(np.asarray(v, np.float32).reshape(nch, 128).T)


_CACHE = {}


def kernel(**inputs):
    inp = {k: np.asarray(v) for k, v in inputs.items()}
    if "prog" not in _CACHE:
        _CACHE["prog"] = build_program()[0]
    nc = _CACHE["prog"]
    B = inp["x"].shape[0]
    pt = np.zeros((128, 256), np.float32)
    pt[:, 0:128] = _rot_lhsT(128)
    pt[64:128, 192:256] = _rot_lhsT(64)
    shared = {
        "ident": np.eye(128, dtype=np.float32), "pt": pt,
        "ropeA": _rope_tables(NLAT, 128), "ropeM": _rope_tables(NLAT, 64),
        "mod_w": inp["mod_w"],
        "hyb_w_in": inp["hyb_w_in"], "hyb_w_out": inp["hyb_w_out"], "mla_w_down": inp["mla_w_down"],
        "mla_w_uq": inp["mla_w_uq"], "mla_w_ukv": inp["mla_w_ukv"], "mla_w_o": inp["mla_w_o"],
    }
    for n in ("ffn1_w_gate", "ffn1_w_up", "ffn1_w_down", "ffn2_w_gate", "ffn2_w_up", "ffn2_w_down"):
        shared[n] = inp[n]
    in_maps = []
    for b in range(B):
        small = np.zeros((128, NSMALL), np.float32)
        small[:, O_CVEC:O_CVEC + 16] = _cols(inp["c"][b], 16)
        small[:, O_CVEC + 16:O_CVEC + 32] = _cols(inp["c_ctx"], 16)
        for l in range(2):
            small[:, O_MODB + l * 144:O_MODB + (l + 1) * 144] = _cols(inp["mod_b"][l], 144)
            for s, nm in enumerate(("norm_ffn1", "norm_mix", "norm_ffn2")):
                small[:, O_NORM + (l * 3 + s) * 16:O_NORM + (l * 3 + s + 1) * 16] = _cols(inp[nm][l], 16)
        small[:, O_NORM + 96:O_NORM + 112] = _cols(inp["final_norm"], 16)
        for tap in range(3):
            small[:, O_CONVW + tap * 8:O_CONVW + (tap + 1) * 8] = _cols(inp["hyb_conv_w"][0, tap], 8)
        small[:, O_QKN] = inp["hyb_q_norm"][0]
        small[:, O_QKN + 1] = inp["hyb_k_norm"][0]
        small[:, O_QKN + 2:O_QKN + 8] = _cols(inp["mla_q_norm"][0], 6)
        small[:, O_QKN + 8:O_QKN + 12] = _cols(inp["mla_kv_norm"][0], 4)
        m = dict(shared)
        m["xin"] = np.ascontiguousarray(np.concatenate([inp["ctx"][b], inp["x"][b]], axis=0).astype(np.float32))
        m["small"] = small
        in_maps.append(m)
    res = run_bass_kernel_spmd(nc, in_maps, core_ids=list(range(B)))
    return np.stack([np.asarray(r["out"], dtype=np.float32) for r in res.results], axis=0)
```

```python
import contextlib
import numpy as np
import concourse.bass as bass
import concourse.mybir as mybir
from concourse.bass_utils import run_bass_kernel_spmd

F32 = mybir.dt.float32
BF16 = mybir.dt.bfloat16
ALU = mybir.AluOpType
AF = mybir.ActivationFunctionType

D = 2048
KC = 16
DFF = 5632
NFG = 11
NCTX = 256
NLAT = 2048
NTOK = 2304
TP = 768
EPS = 1e-6
NSLOT = 4
SLOT = 8192

PASS_CGS = [
    [("c", 0, 256), ("l", 0, 512)],
    [("l", 512, 384), ("l", 896, 384)],
    [("l", 1280, 384), ("l", 1664, 384)],
]

ENGS = ("pe", "act", "dve", "pool", "sp")


class Buf:
    __slots__ = ("name", "lw", "rd", "pend", "excl")

    def __init__(self, name, excl=False):
        self.name = name
        self.lw = None
        self.rd = {}
        self.pend = None
        self.excl = excl


class Sched:
    def __init__(self):
        self.streams = {e: [] for e in ENGS}
        self.cnt = {}
        self.waited = {e: {} for e in ENGS}
        self.pending = {e: [] for e in ENGS}
        self.ninstr = 0

    def _deps(self, eng, reads, writes):
        deps = {}

        def add(k, v):
            if deps.get(k, 0) < v:
                deps[k] = v
        for b in reads:
            if b.pend is not None and b.pend != eng:
                raise RuntimeError(f"buffer {b.name} pending on {b.pend}, used by {eng}")
            if b.lw is not None:
                add(*b.lw)
        for b in writes:
            if b.pend is not None and b.pend != eng:
                raise RuntimeError(f"buffer {b.name} pending on {b.pend}, used by {eng}")
            if b.lw is not None:
                add(*b.lw)
            for k, v in b.rd.items():
                add(k, v)
        waits = []
        w = self.waited[eng]
        for k, v in deps.items():
            if k == "pe" and eng == "pe":
                continue
            if w.get(k, 0) >= v:
                continue
            w[k] = v
            waits.append((k, v))
        return waits

    def op(self, eng, fn, reads=(), writes=(), inc=True, semkey=None, amount=1):
        if eng != "pe" and any(b.excl for b in reads):
            writes = list(writes) + [b for b in reads if b.excl]
            reads = [b for b in reads if not b.excl]
        waits = self._deps(eng, reads, writes)
        self.ninstr += 1
        if not inc:
            self.streams[eng].append((waits, fn, None))
            for b in reads:
                self.pending[eng].append((b, "r")); b.pend = eng
            for b in writes:
                self.pending[eng].append((b, "w")); b.pend = eng
            return None
        k = semkey if semkey is not None else eng
        v = self.cnt.get(k, 0) + amount
        self.cnt[k] = v
        self.streams[eng].append((waits, fn, (k, amount)))
        acc = self.pending[eng] + [(b, "r") for b in reads] + [(b, "w") for b in writes]
        self.pending[eng] = []
        for b, m in acc:
            b.pend = None
        for b, m in acc:
            if m == "r":
                if b.rd.get(k, 0) < v:
                    b.rd[k] = v
        for b, m in acc:
            if m == "w":
                b.lw = (k, v)
                b.rd = {}
        return (k, v)

    def dma(self, eng, out, in_, semkey, reads=(), writes=()):
        return self.op(eng, lambda e: e.dma_start(out=out, in_=in_), reads=reads, writes=writes,
                       semkey=semkey, amount=16)

    def barrier(self, engs=("pe", "act", "dve", "sp")):
        for e in ("pe", "act", "dve"):
            assert not self.pending[e], f"pending on {e} at barrier"
        snap = {k: v for k, v in self.cnt.items() if not (k.startswith("w") or k == "pool")}
        for e in engs:
            waits = []
            for k, v in snap.items():
                if self.waited[e].get(k, 0) >= v:
                    continue
                self.waited[e][k] = v
                waits.append((k, v))
            self.streams[e].append((waits, None, None))

    def final_wait(self, eng, bufs):
        waits = self._deps(eng, [], bufs)
        self.streams[eng].append((waits, None, None))


def emit_program(nc, sched):
    keys = list(sched.cnt.keys())
    with contextlib.ExitStack() as st:
        sems = {k: st.enter_context(nc.semaphore(f"s_{k}")) for k in keys}
        block = st.enter_context(nc.Block())

        def run(engname):
            def body(e):
                for waits, fn, inc in sched.streams[engname]:
                    for (k, v) in waits:
                        e.wait_ge(sems[k], v)
                    if fn is None:
                        continue
                    ins = fn(e)
                    if inc is not None:
                        ins.then_inc(sems[inc[0]], inc[1])
            return body

        block.tensor(run("pe"))
        block.scalar(run("act"))
        block.vector(run("dve"))
        block.gpsimd(run("pool"))
        block.sync(run("sp"))


O_CVEC, O_MODB, O_NORM, O_CONVW, O_QKN, NSMALL = 0, 32, 320, 432, 456, 468


def build_program(stop_after=None, dbg=False, skip=()):
    nc = bass.Bass("TRN2", target_bir_lowering=False)
    S = Sched()
    dram_in = lambda name, shape: nc.dram_tensor(name, shape, F32, kind="ExternalInput").ap()
    xin = dram_in("xin", [NTOK, D])
    small_d = dram_in("small", [128, NSMALL])
    ident_d = dram_in("ident", [128, 128])
    pt_d = dram_in("pt", [128, 256])
    ropeA_d = dram_in("ropeA", [128, 2, NLAT])
    ropeM_d = dram_in("ropeM", [64, 2, NLAT])
    mod_w = dram_in("mod_w", [2, D, 9 * D])
    Wf = {n: dram_in(n, [2, D, DFF]) for n in ("ffn1_w_gate", "ffn1_w_up", "ffn2_w_gate", "ffn2_w_up")}
    Wf.update({n: dram_in(n, [2, DFF, D]) for n in ("ffn1_w_down", "ffn2_w_down")})
    hyb_w_in = dram_in("hyb_w_in", [1, D, 4608])
    hyb_w_out = dram_in("hyb_w_out", [1, D, D])
    mla_w_down = dram_in("mla_w_down", [1, D, 1344])
    mla_w_uq = dram_in("mla_w_uq", [1, 768, 3072])
    mla_w_ukv = dram_in("mla_w_ukv", [1, 512, 4096])
    mla_w_o = dram_in("mla_w_o", [1, D, D])
    out_d = nc.dram_tensor("out", [NLAT, D], F32, kind="ExternalOutput").ap()

    scr = lambda name, shape, dt: nc.dram_tensor(name, shape, dt, kind="Internal").ap()
    xT_scr = (nc.dram_tensor("xT_scr", [128, KC, NTOK], F32, kind="ExternalOutput").ap() if dbg
              else scr("xT_scr", [128, KC, NTOK], F32))
    q0_scr = scr("q0_scr", [128, 8, NTOK], BF16)
    k0_scr = scr("k0_scr", [128, 2, NTOK], BF16)
    v0_scr = scr("v0_scr", [128, 18, 256], BF16)
    gb_scr = scr("gb_scr", [128, 8, NTOK], F32)
    gu_scr = scr("gu_scr", [128, 8, NTOK], F32)
    qn_scr = scr("qn_scr", [128, 16, NTOK], BF16)
    qr_scr = scr("qr_scr", [64, 16, NTOK], BF16)
    kn_scr = scr("kn_scr", [128, 16, NTOK], BF16)
    kr_scr = scr("kr_scr", [64, NTOK], BF16)
    v1_scr = scr("v1_scr", [128, 18, 2048], BF16)

    with contextlib.ExitStack() as st:
        def sb(name, shape, dt):
            return st.enter_context(nc.sbuf_tensor(name, shape, dt))

        hT = sb("hT", [128, KC, TP], BF16)
        wslot = [sb(f"wslot{i}", [128, SLOT], BF16) for i in range(NSLOT)]
        region = sb("region", [128, 19456], F32)
        small = sb("small_sb", [128, NSMALL], F32)
        ident = sb("ident_sb", [128, 128], F32)
        ptb = sb("ptb", [128, 256], BF16)
        ones = sb("ones", [128, 128], BF16)
        sc32 = sb("sc32", [128, 32], F32)
        scb = sb("scb", [128, KC, 2], BF16)
        modsT = sb("modsT", [128, 2 * 288], F32)
        coef = sb("coef", [128, 2 * 2 * 3 * 3 * 16], F32)
        rstd = sb("rstd", [128, TP], F32)
        sqb = [sb(f"sqb{i}", [128, TP], BF16) for i in range(2)]
        tmpb = [sb(f"tmpb{i}", [128, TP], F32) for i in range(2)]
        sil = [sb(f"sil{i}", [128, 512], F32) for i in range(2)]
        stg = [sb(f"stg{i}", [128, 1024], F32) for i in range(2)]
        banks = [st.enter_context(nc.psum_tensor(f"bank{i}", [128, 512], F32)) for i in range(8)]

        xT = region[:, 0:KC * TP].rearrange("p (k t) -> p k t", k=KC)
        A_region = region[:, KC * TP:KC * TP + 3072].bitcast(BF16)
        A_t = [A_region[:, i * 3072:(i + 1) * 3072].rearrange("p (f t) -> p f t", f=4) for i in range(2)]

        class WS:
            def __init__(self):
                self.off = 0

            def f32(self, n):
                a = region[:, self.off:self.off + n]
                self.off += n
                assert self.off <= 19456
                return a

            def bf(self, n):
                m = (n + 1) // 2
                a = region[:, self.off:self.off + m].bitcast(BF16)
                self.off += m
                assert self.off <= 19456
                return a[:, 0:n]

        bankB = [Buf(f"bank{i}", excl=True) for i in range(8)]
        xB = [[Buf(f"x{d}_{c}") for c in range(2)] for d in range(KC)]
        hB = [[Buf(f"h{d}_{c}") for c in range(2)] for d in range(KC)]
        AB = [[[Buf(f"A{i}_{f}_{c}") for c in range(2)] for f in range(4)] for i in range(2)]
        wB = [Buf(f"wslot{i}") for i in range(NSLOT)]
        smallB, identB, ptbB, onesB, scB, scbB = (Buf(n) for n in ("small", "ident", "ptb", "ones", "sc32", "scb"))
        modsB = [Buf("mods0"), Buf("mods1")]
        coefB = [[Buf(f"coef{l}_{s_}") for s_ in range(3)] for l in range(2)]
        rstdB = [Buf("rstd0"), Buf("rstd1")]
        sqB = [Buf("sq0"), Buf("sq1")]
        tmpB = [Buf("tmp0"), Buf("tmp1")]
        silB = [Buf("sil0"), Buf("sil1")]
        stgB = [Buf("stg0"), Buf("stg1")]
        xscrB = [[Buf(f"xscr{p}_{d}") for d in range(KC)] for p in range(3)]
        outB = Buf("out")
        scrB = {n: [Buf(f"{n}{p}") for p in range(3)] for n in
                ("q0", "k0", "v0", "gb", "gu", "qn", "qr", "kn", "kr", "v1")}

        rr = {"b": 0, "sil": 0, "sq": 0, "tmp": 0}

        def nb():
            b = rr["b"]
            rr["b"] = (b + 1) % 6
            return b

        def MM(out, lhsT, rhs, start, stop, rd, wr, inc=None):
            S.op("pe", lambda e: e.matmul(out, lhsT, rhs, start=start, stop=stop), reads=rd, writes=wr,
                 inc=(stop if inc is None else inc))

        def ACT(out, in_, func, rd, wr, bias=None, scale=None):
            kw = {}
            if bias is not None:
                kw["bias"] = bias
            if scale is not None:
                kw["scale"] = scale
            S.op("act", lambda e: e.activation(out, in_, func, **kw), reads=rd, writes=wr)

        def TT(out, a, b, op, rd, wr, eng="dve"):
            S.op(eng, lambda e: e.tensor_tensor(out, a, b, op), reads=rd, writes=wr)

        def STT(out, in0, scalar, in1, op0, op1, rd, wr, eng="dve"):
            S.op(eng, lambda e: e.scalar_tensor_tensor(out, in0, scalar, in1, op0, op1), reads=rd, writes=wr)

        def TS(out, in0, s1, op0, rd, wr, s2=None, op1=None, eng="dve"):
            if op1 is None:
                S.op(eng, lambda e: e.tensor_scalar(out, in0, s1, None, op0), reads=rd, writes=wr)
            else:
                S.op(eng, lambda e: e.tensor_scalar(out, in0, s1, s2, op0, op1), reads=rd, writes=wr)

        def RSQRT(dst, src, rd, wr, scale):
            ACT(dst, src, AF.Ln, rd, wr, bias=EPS, scale=scale)
            ACT(dst, dst, AF.Exp, wr, wr, scale=-0.5)

        def RECIP(out, in_, rd, wr):
            S.op("dve", lambda e: e.reciprocal(out, in_), reads=rd, writes=wr)

        def COPY(out, in_, rd, wr, eng):
            if eng == "act":
                S.op("act", lambda e: e.activation(out, in_, AF.Copy), reads=rd, writes=wr)
            else:
                S.op(eng, lambda e: e.tensor_copy(out, in_), reads=rd, writes=wr)

        def MEMSET(ap, val, wr, eng="dve"):
            S.op(eng, lambda e: e.memset(ap, val), writes=wr)

        def V3(ap2d, a, b):
            return ap2d[:, 0:a * b].rearrange("p (a b) -> p a b", a=a)

        def wsrc(w2d, c0, n):
            return w2d.rearrange("(k p) f -> p k f", p=128)[:, :, c0:c0 + n]

        stages = []

        def stage(specs, fn):
            stages.append((specs, fn))

        def cginfo(p):
            res = []
            off = 0
            for ci, (kind, s0, n) in enumerate(PASS_CGS[p]):
                res.append(dict(ci=ci, kind=kind, s0=s0, n=n, off=off, g0=p * TP + off, v=(1 if kind == "c" else 0)))
                off += n
            return res

        def cg_of_tile(p, tt):
            for cg in cginfo(p):
                if cg["off"] <= tt * 128 < cg["off"] + cg["n"]:
                    return cg["ci"]
            raise AssertionError

        def coef_col(l, v, s, which, d):
            i = ((((l * 2 + v) * 3 + s) * 3 + which) * 16) + d
            return coef[:, i:i + 1]

        def init_stage(_):
            S.dma("sp", small[:], small_d, "ld_small", writes=[smallB])
            S.dma("sp", ident[:], ident_d, "ld_ident", writes=[identB])
            S.dma("pool", ptb[:], pt_d, "w_pt", writes=[ptbB])
            MEMSET(ones[:], 1.0, [onesB])
            ACT(sc32[:], small[:, O_CVEC:O_CVEC + 32], AF.Silu, [smallB], [scB])
            for v in range(2):
                COPY(scb[:, :, v], sc32[:, v * 16:(v + 1) * 16], [scB], [scbB], "dve")
        stage([], init_stage)

        modq = []

        def mod_stage(l, ct):
            spec = [lambda wt: [(V3(wt, 16, 512), wsrc(mod_w[l], ct * 512, 512))]]

            def fn(slots):
                (wt, wb), = slots
                w3 = V3(wt, 16, 512)
                for cc in range(4):
                    for k in range(KC):
                        MM(banks[6][:, cc * 2:(cc + 1) * 2], w3[:, k, cc * 128:(cc + 1) * 128], scb[:, k, :],
                           k == 0, k == KC - 1, [wb, scbB], [bankB[6]])
                pv = banks[6][:, 0:8].rearrange("p (m v) -> p m v", v=2)
                mv = modsT[:, l * 288:(l + 1) * 288].rearrange("p (m v) -> p m v", v=2)
                for v in range(2):
                    TT(mv[:, ct * 4:(ct + 1) * 4, v], pv[:, :, v],
                       small[:, O_MODB + l * 144 + ct * 4:O_MODB + l * 144 + (ct + 1) * 4], ALU.add,
                       [bankB[6], smallB], [modsB[l]])
                if ct % 12 in (7, 11):
                    s = ct // 12
                    for v in range(2):
                        def mcol(m):
                            return mv[:, m * 16:(m + 1) * 16, v]
                        base = (((l * 2 + v) * 3 + s) * 3) * 16
                        g = small[:, O_NORM + (l * 3 + s) * 16:O_NORM + (l * 3 + s + 1) * 16]
                        if ct % 12 == 7:
                            STT(coef[:, base:base + 16], mcol(3 * s + 1), 1.0, g, ALU.add, ALU.mult,
                                [modsB[l], smallB], [coefB[l][s]])
                            COPY(coef[:, base + 16:base + 32], mcol(3 * s), [modsB[l]], [coefB[l][s]], "dve")
                        else:
                            TS(coef[:, base + 32:base + 48], mcol(3 * s + 2), (1.0 if s == 1 else 0.5), ALU.mult,
                               [modsB[l]], [coefB[l][s]])
            stage(spec, fn)

        def drain_mods(n=1, urgent_only=False):
            while n > 0 and modq:
                if urgent_only and not modq[0][0]:
                    return
                _, l, ct = modq.pop(0)
                mod_stage(l, ct)
                n -= 1

        if "mods" not in skip:
            load_x_from_input_early = True
            modq.extend((True, 0, ct) for ct in range(0, 24))
            modq.extend((False, 0, ct) for ct in range(24, 36))
            modq.extend((False, 1, ct) for ct in range(36))

        def load_x_from_input(p):
            def fn(_):
                for tt in range(6):
                    gt = p * 6 + tt
                    ci = cg_of_tile(p, tt)
                    for half in range(2):
                        sg = half
                        S.dma("sp", stg[sg][:], xin[gt * 128:(gt + 1) * 128, half * 1024:(half + 1) * 1024],
                              f"ld_stg{sg}", writes=[stgB[sg]])
                        for q in range(2):
                            b = nb()
                            for j in range(4):
                                dl = q * 4 + j
                                S.op("pe", lambda e, b=b, j=j, dl=dl, sg=sg: e.transpose(
                                    banks[b][:, j * 128:(j + 1) * 128], stg[sg][:, dl * 128:(dl + 1) * 128], ident[:]),
                                    reads=[stgB[sg], identB], writes=[bankB[b]], inc=(j == 3))
                            d0 = half * 8 + q * 4
                            COPY(xT[:, d0:d0 + 4, tt * 128:(tt + 1) * 128],
                                 banks[b][:, 0:512].rearrange("p (a t) -> p a t", a=4),
                                 [bankB[b]], [xB[d][ci] for d in range(d0, d0 + 4)], "act" if q == 0 else "dve")
            stage([], fn)

        def load_x_from_scr(p, cgs):
            def fn(_):
                c0 = cgs[0]["off"]; c1 = cgs[-1]["off"] + cgs[-1]["n"]
                for d in range(KC):
                    S.dma("sp", xT[:, d, c0:c1], xT_scr[:, d, p * TP + c0:p * TP + c1], f"ld_x{d}",
                          reads=[xscrB[p][d]], writes=[xB[d][cg["ci"]] for cg in cgs])
            stage([], fn)

        def store_x_chunk(p, cgs, d):
            c0 = cgs[0]["off"]; c1 = cgs[-1]["off"] + cgs[-1]["n"]
            S.dma("sp", xT_scr[:, d, p * TP + c0:p * TP + c1], xT[:, d, c0:c1], f"st_x{d}",
                  reads=[xB[d][cg["ci"]] for cg in cgs], writes=[xscrB[p][d]])

        def store_x_to_scr(p, cgs):
            def fn(_):
                for d in range(KC):
                    store_x_chunk(p, cgs, d)
            stage([], fn)

        def norm_stats(cg, nchunks, src_fn, src_bufs_fn, inv_n):
            n, off, ci = cg["n"], cg["off"], cg["ci"]
            for d in range(nchunks):
                i = rr["sq"]; rr["sq"] ^= 1
                ACT(sqb[i][:, 0:n], src_fn(d), AF.Square, src_bufs_fn(d), [sqB[i]])
                MM(banks[7][:, 0:n], ones[:], sqb[i][:, 0:n], d == 0, d == nchunks - 1, [onesB, sqB[i]], [bankB[7]], inc=True)
            RSQRT(rstd[:, off:off + n], banks[7][:, 0:n], [bankB[7]], [rstdB[ci]], inv_n)

        def modnorm(l, s, cgs):
            def fn(_):
                merged = (len(cgs) == 2 and cgs[0]["v"] == cgs[1]["v"]
                          and cgs[0]["off"] + cgs[0]["n"] == cgs[1]["off"])
                groups = [list(cgs)] if merged else [[cg] for cg in cgs]
                for grp in groups:
                    off = grp[0]["off"]; n = sum(cg["n"] for cg in grp); v = grp[0]["v"]
                    cis = [cg["ci"] for cg in grp]
                    for d in range(KC):
                        i = rr["sq"]; rr["sq"] ^= 1
                        xin_ = xT[:, d, off:off + n]
                        if d % 2 == 0:
                            ACT(sqb[i][:, 0:n], xin_, AF.Square, [xB[d][ci] for ci in cis], [sqB[i]])
                        else:
                            TT(sqb[i][:, 0:n], xin_, xin_, ALU.mult, [xB[d][ci] for ci in cis], [sqB[i]])
                        for gi, cg in enumerate(grp):
                            o2 = cg["off"] - off
                            MM(banks[7 - gi][:, 0:cg["n"]], ones[:], sqb[i][:, o2:o2 + cg["n"]], d == 0, d == KC - 1,
                               [onesB, sqB[i]], [bankB[7 - gi]], inc=True)
                    for gi, cg in enumerate(grp):
                        RSQRT(rstd[:, cg["off"]:cg["off"] + cg["n"]], banks[7 - gi][:, 0:cg["n"]], [bankB[7 - gi]],
                              [rstdB[cg["ci"]]], 1.0 / D)
                    for d in range(KC):
                        i = rr["tmp"]; rr["tmp"] ^= 1
                        TT(tmpb[i][:, 0:n], xT[:, d, off:off + n], rstd[:, off:off + n], ALU.mult,
                           [xB[d][ci] for ci in cis] + [rstdB[ci] for ci in cis], [tmpB[i]])
                        ACT(hT[:, d, off:off + n], tmpb[i][:, 0:n], AF.Identity, [tmpB[i], coefB[l][s]],
                            [hB[d][ci] for ci in cis], bias=coef_col(l, v, s, 1, d), scale=coef_col(l, v, s, 0, d))
            stage([], fn)

        def ffn(l, s, cgs, Wg, Wu, Wd, store_p=None):
            def gu_stage(fg):
                specs = [lambda wt: [(V3(wt, 16, 512), wsrc(Wg[l], fg * 512, 512))],
                         lambda wt: [(V3(wt, 16, 512), wsrc(Wu[l], fg * 512, 512))]]

                def fn(slots):
                    (wg, wgb), (wu, wub) = slots
                    wg3, wu3 = V3(wg, 16, 512), V3(wu, 16, 512)
                    ab = fg % 2
                    if fg == 0:
                        order = [(fl, cg) for cg in cgs for fl in range(4)]
                    else:
                        order = [(fl, cg) for fl in range(4) for cg in cgs]
                    for fl, cg in order:
                        n, off, ci = cg["n"], cg["off"], cg["ci"]
                        gbk, ubk = nb(), nb()
                        for k in range(KC):
                            MM(banks[gbk][:, 0:n], wg3[:, k, fl * 128:(fl + 1) * 128], hT[:, k, off:off + n],
                               k == 0, k == KC - 1, [wgb, hB[k][ci]], [bankB[gbk]])
                        for k in range(KC):
                            MM(banks[ubk][:, 0:n], wu3[:, k, fl * 128:(fl + 1) * 128], hT[:, k, off:off + n],
                               k == 0, k == KC - 1, [wub, hB[k][ci]], [bankB[ubk]])
                        i = rr["sil"]; rr["sil"] ^= 1
                        ACT(sil[i][:, 0:n], banks[gbk][:, 0:n], AF.Silu, [bankB[gbk]], [silB[i]])
                        TT(A_t[ab][:, fl, off:off + n], sil[i][:, 0:n], banks[ubk][:, 0:n], ALU.mult,
                           [silB[i], bankB[ubk]], [AB[ab][fl][ci]])
                stage(specs, fn)
                drain_mods(4 if (fg == 0 and modq and modq[0][0]) else 1)

            def down_stage(fg):
                specs = [lambda wt: [(V3(wt, 4, 2048),
                                      Wd[l][fg * 512:(fg + 1) * 512, :].rearrange("(f p) d -> p f d", p=128))]]

                def fn(slots):
                    (wd, wdb), = slots
                    wd3 = V3(wd, 4, 2048)
                    ab = fg % 2
                    for d in range(KC):
                        for cg in cgs:
                            n, off, ci, v = cg["n"], cg["off"], cg["ci"], cg["v"]
                            yb = nb()
                            for fl in range(4):
                                MM(banks[yb][:, 0:n], wd3[:, fl, d * 128:(d + 1) * 128], A_t[ab][:, fl, off:off + n],
                                   fl == 0, fl == 3, [wdb, AB[ab][fl][ci]], [bankB[yb]])
                            STT(xT[:, d, off:off + n], banks[yb][:, 0:n], coef_col(l, v, s, 2, d), xT[:, d, off:off + n],
                                ALU.mult, ALU.add, [bankB[yb], xB[d][ci], coefB[l][s]], [xB[d][ci]])
                        if store_p is not None and fg == NFG - 1:
                            store_x_chunk(store_p, cgs, d)
                stage(specs, fn)
                drain_mods(1, urgent_only=True)

            if "ffn" in skip:
                if store_p is not None:
                    store_x_to_scr(store_p, cgs)
                return
            gu_stage(0)
            for fg in range(1, NFG):
                gu_stage(fg)
                down_stage(fg - 1)
            down_stage(NFG - 1)

        def barrier_stage(engs=("pe", "act", "dve", "sp")):
            stage([], lambda _: S.barrier(engs=engs))

        def proj_residual(l, cgs, W2d, src_fn, src_buf_fn, nk):
            def pstage(t4):
                specs = [lambda wt: [(V3(wt, nk, 512), wsrc(W2d, t4 * 512, 512))]]

                def fn(slots):
                    (wt, wb), = slots
                    w3 = V3(wt, nk, 512)
                    for dl in range(4):
                        d = t4 * 4 + dl
                        for cg in cgs:
                            n, off, ci, v = cg["n"], cg["off"], cg["ci"], cg["v"]
                            yb = nb()
                            for k in range(nk):
                                MM(banks[yb][:, 0:n], w3[:, k, dl * 128:(dl + 1) * 128], src_fn(k, off, n),
                                   k == 0, k == nk - 1, [wb, src_buf_fn(k, ci)], [bankB[yb]])
                            STT(xT[:, d, off:off + n], banks[yb][:, 0:n], coef_col(l, v, 1, 2, d), xT[:, d, off:off + n],
                                ALU.mult, ALU.add, [bankB[yb], xB[d][ci], coefB[l][1]], [xB[d][ci]])
                stage(specs, fn)
            for t4 in range(4):
                pstage(t4)

        pipeB, pipeC = [], []

        def pipe_step():
            nB = list(pipeB); pipeB.clear()
            nC = list(pipeC); pipeC.clear()
            for f in nB:
                f()
            for f in nC:
                f()

        def pipe_flush():
            while pipeB or pipeC:
                pipe_step()

        def rope_tail(P, n, src32, src32B, srcbf, srcbfB, ptv, cosv, sinv, ropeBuf, t1, t1B, t2, t2B, dst, dstB):
            b3 = nb()
            MM(banks[b3][0:P, 0:n], ptv, srcbf, True, True, [ptbB, srcbfB], [bankB[b3]])
            TT(t1, src32, cosv, ALU.mult, [src32B, ropeBuf], [t1B])
            TT(t2, banks[b3][0:P, 0:n], sinv, ALU.mult, [bankB[b3], ropeBuf], [t2B])
            TT(dst, t1, t2, ALU.add, [t1B, t2B], [dstB])

        def rope_apply(P, n, src32, src32B, srcbf, srcbfB, ptv, cosv, sinv, ropeBuf, t1, t1B, t2, t2B, dst, dstB):
            COPY(srcbf, src32, [src32B], [srcbfB], "act")
            b3 = nb()
            MM(banks[b3][0:P, 0:n], ptv, srcbf, True, True, [ptbB, srcbfB], [bankB[b3]])
            TT(t1, src32, cosv, ALU.mult, [src32B, ropeBuf], [t1B])
            TT(t2, banks[b3][0:P, 0:n], sinv, ALU.mult, [bankB[b3], ropeBuf], [t2B])
            TT(dst, t1, t2, ALU.add, [t1B, t2B], [dstB])

        def load_rope(p, cgs, tab_d, P, ropeT, ropeBuf, base=0):
            for cg in cgs:
                if cg["kind"] != "l":
                    continue
                n, off, s0 = cg["n"], cg["off"], cg["s0"]
                S.dma("sp", ropeT[base:base + P, :, off:off + n], tab_d[:, :, s0:s0 + n], "ld_rope", writes=[ropeBuf])

        def rope_hi_tail(n, src32, src32B, srcbf, srcbfB, cosv, sinv, ropeBuf, t1, t1B, t2, t2B, dst, dstB):
            b3 = nb()
            MM(banks[b3][:, 0:n], ptb[:, 128:256], srcbf[:, 0:n], True, True, [ptbB, srcbfB], [bankB[b3]])
            TT(t1[64:128, 0:n], src32[64:128, 0:n], cosv, ALU.mult, [src32B, ropeBuf], [t1B])
            TT(t2[64:128, 0:n], banks[b3][64:128, 0:n], sinv, ALU.mult, [bankB[b3], ropeBuf], [t2B])
            TT(dst, t1[64:128, 0:n], t2[64:128, 0:n], ALU.add, [t1B, t2B], [dstB])

        def hyb_inproj(p):
            cgs = cginfo(p)
            ws = WS()
            gcs = ws.f32(512); gust = [ws.f32(TP) for _ in range(2)]; gbst = [ws.f32(TP) for _ in range(2)]
            rsq_ = [ws.f32(512) for _ in range(2)]; qn32_ = [ws.f32(512) for _ in range(2)]
            t1_ = [ws.f32(512) for _ in range(2)]; t2_ = [ws.f32(512) for _ in range(2)]
            ropeT = ws.f32(2 * TP).rearrange("p (a t) -> p a t", a=2)
            sqq_ = [ws.bf(512) for _ in range(2)]; qnb_ = [ws.bf(512) for _ in range(2)]
            qkc = {"i": 0}
            qst = [ws.bf(TP) for _ in range(2)]
            vst = ws.bf(6 * 256).rearrange("p (a t) -> p a t", a=6)
            B_ = {n: Buf("hyb_" + n) for n in ("gcs", "gust0", "gust1", "gbst0", "gbst1", "rsq0", "qn320", "t10", "t20",
                                               "rsq1", "qn321", "t11", "t21", "sqq0", "qnb0", "sqq1", "qnb1",
                                               "rope", "qst0", "qst1", "vst")}
            W = hyb_w_in[0]

            def pre(_):
                load_rope(p, cgs, ropeA_d, 128, ropeT, B_["rope"])
            stage([], pre)

            def conv_stage(j):
                specs = [lambda wt: [(V3(wt, 16, 512)[:, :, i * 128:(i + 1) * 128], wsrc(W, i * 1024 + j * 128, 128))
                                     for i in range(3)]]

                def fn(slots):
                    (wt, wb), = slots
                    w3 = V3(wt, 16, 512)
                    i2 = j % 2
                    for cg in cgs:
                        n, off, ci = cg["n"], cg["off"], cg["ci"]
                        b0, b1, b2 = nb(), nb(), nb()
                        for bi, bk in enumerate((b0, b1, b2)):
                            for k in range(KC):
                                MM(banks[bk][:, 0:n], w3[:, k, bi * 128:(bi + 1) * 128], hT[:, k, off:off + n],
                                   k == 0, k == KC - 1, [wb, hB[k][ci]], [bankB[bk]])
                        COPY(gbst[i2][:, off:off + n], banks[b0][:, 0:n], [bankB[b0]], [B_[f"gbst{i2}"]], "act")
                        COPY(gcs[:, 0:n], banks[b1][:, 0:n], [bankB[b1]], [B_["gcs"]], "act")
                        TT(gust[i2][:, off:off + n], gcs[:, 0:n], banks[b2][:, 0:n], ALU.mult,
                           [B_["gcs"], bankB[b2]], [B_[f"gust{i2}"]])
                    S.dma("sp", gb_scr[:, j, p * TP:(p + 1) * TP], gbst[i2], f"st_gb{i2}",
                          reads=[B_[f"gbst{i2}"]], writes=[scrB["gb"][p]])
                    S.dma("sp", gu_scr[:, j, p * TP:(p + 1) * TP], gust[i2], f"st_gu{i2}",
                          reads=[B_[f"gust{i2}"]], writes=[scrB["gu"][p]])
                stage(specs, fn)

            def qk_chunk(w3, wb, col0, gaincol, cg, dst, dstB, after=None):
                n, off, ci = cg["n"], cg["off"], cg["ci"]
                z = qkc["i"]; qkc["i"] ^= 1
                rsq, qn32, t1, t2, sqq, qnb = rsq_[z], qn32_[z], t1_[z], t2_[z], sqq_[z], qnb_[z]
                Bz = {k: B_[f"{k}{z}"] for k in ("rsq", "qn32", "t1", "t2", "sqq", "qnb")}
                b = nb()
                for k in range(KC):
                    MM(banks[b][:, 0:n], w3[:, k, col0:col0 + 128], hT[:, k, off:off + n], k == 0, k == KC - 1,
                       [wb, hB[k][ci]], [bankB[b]])
                ACT(sqq[:, 0:n], banks[b][:, 0:n], AF.Square, [bankB[b]], [Bz["sqq"]])
                pipe_step()

                def Bf():
                    b2 = nb()
                    MM(banks[b2][:, 0:n], ones[:], sqq[:, 0:n], True, True, [onesB, Bz["sqq"]], [bankB[b2]])
                    RSQRT(rsq[:, 0:n], banks[b2][:, 0:n], [bankB[b2]], [Bz["rsq"]], 1.0 / 128)
                    if cg["kind"] == "c":
                        STT(dst[:, off:off + n], banks[b][:, 0:n], gaincol, rsq[:, 0:n], ALU.mult, ALU.mult,
                            [bankB[b], Bz["rsq"], smallB], [dstB])
                        if after is not None:
                            pipeC.append(after)
                    else:
                        STT(qn32[:, 0:n], banks[b][:, 0:n], gaincol, rsq[:, 0:n], ALU.mult, ALU.mult,
                            [bankB[b], Bz["rsq"], smallB], [Bz["qn32"]])
                        COPY(qnb[:, 0:n], qn32[:, 0:n], [Bz["qn32"]], [Bz["qnb"]], "act")

                        def Cf():
                            rope_tail(128, n, qn32[:, 0:n], Bz["qn32"], qnb[:, 0:n], Bz["qnb"], ptb[:, 0:128],
                                      ropeT[:, 0, off:off + n], ropeT[:, 1, off:off + n], B_["rope"],
                                      t1[:, 0:n], Bz["t1"], t2[:, 0:n], Bz["t2"], dst[:, off:off + n], dstB)
                            if after is not None:
                                after()
                        pipeC.append(Cf)
                pipeB.append(Bf)

            def q_stage(qt):
                specs = [lambda wt: [(V3(wt, 16, 512), wsrc(W, 3072 + qt * 512, 512))]]

                def fn(slots):
                    (wt, wb), = slots
                    w3 = V3(wt, 16, 512)
                    for hh in range(4):
                        h = qt * 4 + hh
                        i2 = h % 2

                        def store(h=h, i2=i2):
                            S.dma("sp", q0_scr[:, h, p * TP:(p + 1) * TP], qst[i2], f"st_q{i2}",
                                  reads=[B_[f"qst{i2}"]], writes=[scrB["q0"][p]])
                        for cg in cgs:
                            qk_chunk(w3, wb, hh * 128, small[:, O_QKN:O_QKN + 1], cg, qst[i2], B_[f"qst{i2}"],
                                     after=(store if cg is cgs[-1] else None))
                stage(specs, fn)

            def kv_stage():
                specs = [lambda wt: [(V3(wt, 16, 512), wsrc(W, 4096, 512))]]

                def fn(slots):
                    (wt, wb), = slots
                    w3 = V3(wt, 16, 512)
                    for g in range(2):
                        i2 = g % 2

                        def store(g=g, i2=i2):
                            S.dma("sp", k0_scr[:, g, p * TP:(p + 1) * TP], qst[i2], f"st_q{i2}",
                                  reads=[B_[f"qst{i2}"]], writes=[scrB["k0"][p]])
                        for cg in cgs:
                            qk_chunk(w3, wb, g * 128, small[:, O_QKN + 1:O_QKN + 2], cg, qst[i2], B_[f"qst{i2}"],
                                     after=(store if cg is cgs[-1] else None))
                    for tt in range(6):
                        ci = cg_of_tile(p, tt)
                        vb = nb()
                        for k in range(KC):
                            MM(banks[vb][:, 0:256], hT[:, k, tt * 128:(tt + 1) * 128], w3[:, k, 256:512],
                               k == 0, k == KC - 1, [wb, hB[k][ci]], [bankB[vb]])
                        COPY(vst[:, tt, :], banks[vb][:, 0:256], [bankB[vb]], [B_["vst"]], "act" if tt % 2 else "dve")
                        if tt < 4:
                            pipe_step()
                    pipe_flush()
                    S.dma("sp", v0_scr[:, p * 6:(p + 1) * 6, :], vst, "st_v", reads=[B_["vst"]], writes=[scrB["v0"][p]])
                stage(specs, fn)

            for j in range(8):
                conv_stage(j)
            for qt in range(2):
                q_stage(qt)
            kv_stage()

        def attention(p, cgs, layer):
            ws = WS()
            kT = [ws.bf(NTOK) for _ in range(2)]
            Vt = [ws.bf(18 * 128).rearrange("p (a t) -> p a t", a=18) for _ in range(2)]
            krT = ws.bf(NTOK)
            qh = [ws.bf(TP) for _ in range(2)]
            qrh = [ws.bf(TP) for _ in range(2)]
            pT = [ws.bf(512) for _ in range(4)]
            rec = ws.f32(512)
            gux = [ws.f32(514) for _ in range(2)]
            gbx = [ws.f32(512) for _ in range(2)]
            acc = [ws.f32(512) for _ in range(2)]
            B_ = {n: Buf(f"att_{n}") for n in ("kT0", "kT1", "V0", "V1", "krT", "qh0", "qh1", "qrh0", "qrh1",
                                               "pT0", "pT1", "pT2", "pT3", "rec", "gux0", "gux1", "gbx0", "gbx1", "acc0", "acc1")}
            sbanks, obanks, dbanks = [0, 1, 2, 3], [4, 5], [6, 7]
            st_ = {"o": 0, "q": 0, "kv": 0}
            allp = lambda name: scrB[name]
            PD = 2

            def mk_unit(q_ap, qB, qr_ap, qrB, kt_ap, ktB, kr_ap, krB, v_ap, vB, cg, scale, chunk):
                oi = st_["o"]; st_["o"] ^= 1
                return dict(q=q_ap, qB=qB, qr=qr_ap, qrB=qrB, kt=kt_ap, ktB=ktB, kr=kr_ap, krB=krB, v=v_ap, vB=vB,
                            cg=cg, scale=scale, chunk=chunk, nkt=(2 if cg["kind"] == "c" else 18),
                            ob=obanks[oi], db=dbanks[oi])

            def run_units(gen, side=()):
                side = list(side)
                flat = []
                it = iter(gen)

                def ensure(idx):
                    while len(flat) <= idx:
                        try:
                            u = next(it)
                        except StopIteration:
                            return False
                        for kt in range(u["nkt"]):
                            flat.append((u, kt))
                    return True

                def s_mm(idx):
                    u, kt = flat[idx]
                    n, off = u["cg"]["n"], u["cg"]["off"]
                    sbk = sbanks[idx % 4]
                    MM(banks[sbk][:, 0:n], u["kt"][:, kt * 128:(kt + 1) * 128], u["q"][:, off:off + n], True, u["qr"] is None,
                       [u["ktB"], u["qB"]], [bankB[sbk]])
                    if u["qr"] is not None:
                        MM(banks[sbk][:, 0:n], u["kr"][:, kt * 128:(kt + 1) * 128], u["qr"][:, off:off + n], False, True,
                           [u["krB"], u["qrB"]], [bankB[sbk]])

                t = 0
                issued = 0
                while ensure(t):
                    while issued <= t + PD and ensure(issued):
                        s_mm(issued)
                        issued += 1
                    u, kt = flat[t]
                    n, off, ci = u["cg"]["n"], u["cg"]["off"], u["cg"]["ci"]
                    sbk = sbanks[t % 4]
                    pt = pT[t % 4]; ptB_ = B_[f"pT{t % 4}"]
                    ACT(pt[:, 0:n], banks[sbk][:, 0:n], AF.Exp, [bankB[sbk]], [ptB_], scale=u["scale"])
                    last = kt == u["nkt"] - 1
                    ob, db = u["ob"], u["db"]
                    MM(banks[ob][:, 0:n], u["v"][:, kt, :], pt[:, 0:n], kt == 0, last, [u["vB"], ptB_], [bankB[ob]])
                    MM(banks[db][:, 0:n], ones[:], pt[:, 0:n], kt == 0, last, [onesB, ptB_], [bankB[db]])
                    if last:
                        RECIP(rec[:, 0:n], banks[db][:, 0:n], [bankB[db]], [B_["rec"]])
                        TT(hT[:, u["chunk"], off:off + n], banks[ob][:, 0:n], rec[:, 0:n], ALU.mult,
                           [bankB[ob], B_["rec"]], [hB[u["chunk"]][ci]])
                        if side:
                            side.pop(0)()
                    t += 1
                while side:
                    side.pop(0)()

            c0 = cgs[0]["off"]; c1 = cgs[-1]["off"] + cgs[-1]["n"]

            def gqa_units():
                for g in range(2):
                    i = st_["kv"]; st_["kv"] ^= 1
                    S.dma("sp", kT[i], k0_scr[:, g, :], f"ld_kT{i}", reads=allp("k0"), writes=[B_[f"kT{i}"]])
                    S.dma("sp", Vt[i], v0_scr[:, :, g * 128:(g + 1) * 128], f"ld_V{i}", reads=allp("v0"),
                          writes=[B_[f"V{i}"]])
                    for r in range(4):
                        h = g * 4 + r
                        qi = st_["q"]; st_["q"] ^= 1
                        S.dma("sp", qh[qi][:, c0:c1], q0_scr[:, h, p * TP + c0:p * TP + c1], f"ld_q{qi}",
                              reads=[scrB["q0"][p]], writes=[B_[f"qh{qi}"]])
                        for cg in cgs:
                            yield mk_unit(qh[qi], B_[f"qh{qi}"], None, None, kT[i], B_[f"kT{i}"], None, None,
                                          Vt[i], B_[f"V{i}"], cg, 128 ** -0.5, 8 + h)

            def mla_units():
                MEMSET(krT[64:128, :], 0.0, [B_["krT"]])
                for i_ in range(2):
                    MEMSET(qrh[i_][64:128, :], 0.0, [B_[f"qrh{i_}"]])
                S.dma("sp", krT[0:64, :], kr_scr, "ld_kr", reads=allp("kr"), writes=[B_["krT"]])
                for h in range(16):
                    i = st_["kv"]; st_["kv"] ^= 1
                    S.dma("sp", kT[i], kn_scr[:, h, :], f"ld_kT{i}", reads=allp("kn"), writes=[B_[f"kT{i}"]])
                    S.dma("sp", Vt[i], v1_scr[:, :, h * 128:(h + 1) * 128], f"ld_V{i}", reads=allp("v1"),
                          writes=[B_[f"V{i}"]])
                    qi = st_["q"]; st_["q"] ^= 1
                    S.dma("sp", qh[qi][:, c0:c1], qn_scr[:, h, p * TP + c0:p * TP + c1], f"ld_q{qi}",
                          reads=[scrB["qn"][p]], writes=[B_[f"qh{qi}"]])
                    S.dma("sp", qrh[qi][0:64, c0:c1], qr_scr[:, h, p * TP + c0:p * TP + c1], f"ld_qr{qi}",
                          reads=[scrB["qr"][p]], writes=[B_[f"qrh{qi}"]])
                    for cg in cgs:
                        yield mk_unit(qh[qi], B_[f"qh{qi}"], qrh[qi], B_[f"qrh{qi}"], kT[i], B_[f"kT{i}"], krT, B_["krT"],
                                      Vt[i], B_[f"V{i}"], cg, 192 ** -0.5, h)

            def gqa_fn(_):
                run_units(gqa_units(), side=conv_jobs())

            def mla_fn(_):
                run_units(mla_units())

            def conv_jobs():
                jobs = []
                cnt = {"u": 0}
                for j in range(8):
                    for cg in cgs:
                        def job(j=j, cg=cg):
                            n, off, ci, a = cg["n"], cg["off"], cg["ci"], cg["g0"]
                            b = a + n
                            s_lo, s_hi = (0, NCTX) if cg["kind"] == "c" else (NCTX, NTOK)
                            i = cnt["u"] % 2; cnt["u"] += 1
                            lo = max(a - 1, s_lo); hi = min(b + 1, s_hi)
                            if a - 1 < s_lo:
                                MEMSET(gux[i][:, 0:1], 0.0, [B_[f"gux{i}"]])
                            if b + 1 > s_hi:
                                MEMSET(gux[i][:, n + 1:n + 2], 0.0, [B_[f"gux{i}"]])
                            S.dma("sp", gux[i][:, lo - (a - 1):hi - (a - 1)], gu_scr[:, j, lo:hi], f"ld_gux{i}",
                                  reads=allp("gu"), writes=[B_[f"gux{i}"]])
                            S.dma("sp", gbx[i][:, 0:n], gb_scr[:, j, a:b], f"ld_gbx{i}", reads=allp("gb"),
                                  writes=[B_[f"gbx{i}"]])
                            cw = lambda tap: small[:, O_CONVW + tap * 8 + j:O_CONVW + tap * 8 + j + 1]
                            TS(acc[i][:, 0:n], gux[i][:, 1:n + 1], cw(1), ALU.mult, [B_[f"gux{i}"], smallB], [B_[f"acc{i}"]])
                            STT(acc[i][:, 0:n], gux[i][:, 0:n], cw(0), acc[i][:, 0:n], ALU.mult, ALU.add,
                                [B_[f"gux{i}"], smallB, B_[f"acc{i}"]], [B_[f"acc{i}"]])
                            STT(acc[i][:, 0:n], gux[i][:, 2:n + 2], cw(2), acc[i][:, 0:n], ALU.mult, ALU.add,
                                [B_[f"gux{i}"], smallB, B_[f"acc{i}"]], [B_[f"acc{i}"]])
                            TT(hT[:, j, off:off + n], gbx[i][:, 0:n], acc[i][:, 0:n], ALU.mult,
                               [B_[f"gbx{i}"], B_[f"acc{i}"]], [hB[j][ci]])
                        jobs.append(job)
                return jobs

            if layer == 0:
                stage([], gqa_fn)
            else:
                stage([], mla_fn)

        def mla_inproj(p):
            cgs = cginfo(p)
            lat = [cg for cg in cgs if cg["kind"] == "l"]
            ws = WS()
            ckv32 = ws.f32(4 * TP).rearrange("p (a t) -> p a t", a=4)
            kr32 = ws.f32(TP)
            rq = ws.f32(TP); rkv = ws.f32(TP)
            ropeT = ws.f32(2 * TP).rearrange("p (a t) -> p a t", a=2)
            qr32_ = [ws.f32(512) for _ in range(2)]; t1_ = [ws.f32(512) for _ in range(2)]
            t2_ = [ws.f32(512) for _ in range(2)]
            t1, t2 = t1_[0], t2_[0]
            cqg = ws.bf(6 * TP).rearrange("p (a t) -> p a t", a=6)
            ckvn = ws.bf(4 * TP).rearrange("p (a t) -> p a t", a=4)
            sq1_ = [ws.bf(512) for _ in range(2)]; qrb_ = [ws.bf(512) for _ in range(2)]; krb = ws.bf(TP)
            zz = {"sq": 0, "q": 0}
            qnst = [ws.bf(TP) for _ in range(2)]
            qrst = [ws.bf(TP) for _ in range(2)]
            knst = [ws.bf(TP) for _ in range(2)]
            krst = ws.bf(TP)
            vst = [ws.bf(512) for _ in range(2)]
            names = ["kr32", "rq0", "rq1", "rkv0", "rkv1", "rope", "qr320", "qr321", "t1", "t2", "t11", "t21", "sq10", "sq11",
                     "qrb0", "qrb1", "krb", "qnst0", "qnst1",
                     "qrst0", "qrst1", "knst0", "knst1", "krst", "vst0", "vst1"]
            B_ = {n: Buf("mla_" + n) for n in names}
            cqB = [[Buf(f"cqg{c}_{i}") for i in range(2)] for c in range(6)]
            ckv32B = [[Buf(f"ckv32{c}_{i}") for i in range(2)] for c in range(4)]
            ckvnB = [[Buf(f"ckvn{c}_{i}") for i in range(2)] for c in range(4)]
            Wd_, Wuq, Wukv = mla_w_down[0], mla_w_uq[0], mla_w_ukv[0]
            gq = lambda c: small[:, O_QKN + 2 + c:O_QKN + 3 + c]
            gkv = lambda c: small[:, O_QKN + 8 + c:O_QKN + 9 + c]

            def pre(_):
                if "mrope" not in skip:
                    load_rope(p, cgs, ropeM_d, 64, ropeT, B_["rope"], base=64)
            stage([], pre)

            def down_chunk(w3, wb, col0, M, cg):
                n, off, ci = cg["n"], cg["off"], cg["ci"]
                b = nb()
                for k in range(KC):
                    MM(banks[b][0:M, 0:n], w3[:, k, col0:col0 + M], hT[:, k, off:off + n], k == 0, k == KC - 1,
                       [wb, hB[k][ci]], [bankB[b]])
                return b

            def stats_acc(b, cg, first, last_, sbank, fin=None):
                n = cg["n"]
                if "mstats" in skip:
                    return
                z = zz["sq"]; zz["sq"] ^= 1
                sq1 = sq1_[z]; sqB_ = B_[f"sq1{z}"]
                ACT(sq1[:, 0:n], banks[b][:, 0:n], AF.Square, [bankB[b]], [sqB_])
                pipe_step()

                def Bf():
                    MM(banks[sbank][:, 0:n], ones[:], sq1[:, 0:n], first, last_, [onesB, sqB_], [bankB[sbank]], inc=True)
                    if fin is not None:
                        fin()
                pipeB.append(Bf)

            def fin_stats(cg, sbank, dst, dstB, inv_n):
                n, off = cg["n"], cg["off"]
                RSQRT(dst[:, off:off + n], banks[sbank][:, 0:n], [bankB[sbank]], [dstB], inv_n)

            def down_stage(t):
                ncol = 512 if t < 2 else 320
                specs = [lambda wt: [(V3(wt, 16, 512)[:, :, 0:ncol], wsrc(Wd_, t * 512, ncol))]]

                def fn(slots):
                    (wt, wb), = slots
                    w3 = V3(wt, 16, 512)
                    for cg in cgs:
                        n, off, ci = cg["n"], cg["off"], cg["ci"]
                        for cl in range(4 if t < 2 else 3):
                            gc = t * 4 + cl
                            if gc < 6:
                                b = down_chunk(w3, wb, cl * 128, 128, cg)
                                TS(cqg[:, gc, off:off + n], banks[b][:, 0:n], gq(gc), ALU.mult, [bankB[b], smallB],
                                   [cqB[gc][ci]])
                                stats_acc(b, cg, gc == 0, gc == 5, 7 - ci,
                                          fin=((lambda cg=cg, ci=ci: fin_stats(cg, 7 - ci, rq, B_[f"rq{ci}"], 1.0 / 768))
                                               if gc == 5 else None))
                            elif gc < 10:
                                c = gc - 6
                                b = down_chunk(w3, wb, cl * 128, 128, cg)
                                TS(ckv32[:, c, off:off + n], banks[b][:, 0:n], gkv(c), ALU.mult, [bankB[b], smallB],
                                   [ckv32B[c][ci]])
                                def fin_kv(cg=cg, ci=ci, n=n, off=off):
                                    fin_stats(cg, 7 - ci, rkv, B_[f"rkv{ci}"], 1.0 / 512)
                                    for c2 in range(4):
                                        TT(ckvn[:, c2, off:off + n], ckv32[:, c2, off:off + n], rkv[:, off:off + n],
                                           ALU.mult, [ckv32B[c2][ci], B_[f"rkv{ci}"]], [ckvnB[c2][ci]])
                                stats_acc(b, cg, c == 0, c == 3, 7 - ci, fin=(fin_kv if c == 3 else None))
                            else:
                                b = down_chunk(w3, wb, cl * 128 - 64, 128, cg)
                                if cg["kind"] == "c":
                                    COPY(krst[64:128, off:off + n], banks[b][64:128, 0:n], [bankB[b]], [B_["krst"]], "act")
                                else:
                                    COPY(kr32[:, off:off + n], banks[b][:, 0:n], [bankB[b]], [B_["kr32"]], "dve")
                                    COPY(krb[:, off:off + n], kr32[:, off:off + n], [B_["kr32"]], [B_["krb"]], "act")
                                    rope_hi_tail(n, kr32[:, off:off + n], B_["kr32"], krb[:, off:off + n], B_["krb"],
                                                 ropeT[64:128, 0, off:off + n], ropeT[64:128, 1, off:off + n], B_["rope"],
                                                 t1, B_["t1"], t2, B_["t2"], krst[64:128, off:off + n], B_["krst"])
                    if t == 2:
                        pipe_flush()
                        S.dma("sp", kr_scr[:, p * TP:(p + 1) * TP], krst[64:128, :], "st_kr", reads=[B_["krst"]],
                              writes=[scrB["kr"][p]])
                stage(specs, fn)

            def uq_stage(t):
                specs = [lambda wt: [(V3(wt, 6, 384), wsrc(Wuq, t * 384, 384))]]

                def fn(slots):
                    (wt, wb), = slots
                    w3 = V3(wt, 6, 384)
                    for hh in range(2):
                        h = t * 2 + hh
                        i2 = h % 2
                        if not lat:
                            continue
                        for cg in lat:
                            n, off, ci = cg["n"], cg["off"], cg["ci"]
                            b = nb()
                            for k in range(6):
                                MM(banks[b][:, 0:n], w3[:, k, hh * 192:hh * 192 + 128], cqg[:, k, off:off + n],
                                   k == 0, k == 5, [wb, cqB[k][ci]], [bankB[b]])
                            TT(qnst[i2][:, off:off + n], banks[b][:, 0:n], rq[:, off:off + n], ALU.mult,
                               [bankB[b], B_[f"rq{ci}"]], [B_[f"qnst{i2}"]])
                            b = nb()
                            for k in range(6):
                                MM(banks[b][:, 0:n], w3[:, k, hh * 192 + 64:hh * 192 + 192], cqg[:, k, off:off + n],
                                   k == 0, k == 5, [wb, cqB[k][ci]], [bankB[b]])
                            z = zz["q"]; zz["q"] ^= 1
                            qr32, qrb, t1z, t2z = qr32_[z], qrb_[z], t1_[z], t2_[z]
                            qB_, bB_, t1B_, t2B_ = B_[f"qr32{z}"], B_[f"qrb{z}"], B_["t1" if z == 0 else "t11"], B_["t2" if z == 0 else "t21"]
                            TT(qr32[:, 0:n], banks[b][:, 0:n], rq[:, off:off + n], ALU.mult,
                               [bankB[b], B_[f"rq{ci}"]], [qB_])
                            COPY(qrb[:, 0:n], qr32[:, 0:n], [qB_], [bB_], "act")
                            pipe_step()
                            is_last = cg is lat[-1]

                            def Cf(n=n, off=off, qr32=qr32, qrb=qrb, t1z=t1z, t2z=t2z, qB_=qB_, bB_=bB_, t1B_=t1B_, t2B_=t2B_,
                                   i2=i2, h=h, is_last=is_last):
                                rope_hi_tail(n, qr32, qB_, qrb, bB_,
                                             ropeT[64:128, 0, off:off + n], ropeT[64:128, 1, off:off + n], B_["rope"],
                                             t1z, t1B_, t2z, t2B_, qrst[i2][64:128, off:off + n], B_[f"qrst{i2}"])
                                if is_last:
                                    c0 = lat[0]["off"]; c1 = lat[-1]["off"] + lat[-1]["n"]
                                    S.dma("sp", qn_scr[:, h, p * TP + c0:p * TP + c1], qnst[i2][:, c0:c1], f"st_qn{i2}",
                                          reads=[B_[f"qnst{i2}"]], writes=[scrB["qn"][p]])
                                    S.dma("sp", qr_scr[:, h, p * TP + c0:p * TP + c1], qrst[i2][64:128, c0:c1], f"st_qr{i2}",
                                          reads=[B_[f"qrst{i2}"]], writes=[scrB["qr"][p]])
                            pipeB.append(Cf)
                    if t == 7:
                        pipe_flush()
                stage(specs, fn)

            def ukv_stage(t):
                specs = [lambda wt: [(V3(wt, 4, 2048), wsrc(Wukv, t * 2048, 2048))]]

                def fn(slots):
                    (wt, wb), = slots
                    w3 = V3(wt, 4, 2048)
                    w4 = wt[:, 0:8192].rearrange("p (k h c) -> p k h c", k=4, h=8)
                    for hh in range(8):
                        h = t * 8 + hh
                        i2 = h % 2
                        for cg in cgs:
                            n, off, ci = cg["n"], cg["off"], cg["ci"]
                            b = nb()
                            for k in range(4):
                                MM(banks[b][:, 0:n], w3[:, k, hh * 256:hh * 256 + 128], ckvn[:, k, off:off + n],
                                   k == 0, k == 3, [wb, ckvnB[k][ci]], [bankB[b]])
                            COPY(knst[i2][:, off:off + n], banks[b][:, 0:n], [bankB[b]], [B_[f"knst{i2}"]],
                                 "act" if ci == 0 else "dve")
                        S.dma("sp", kn_scr[:, h, p * TP:(p + 1) * TP], knst[i2], f"st_kn{i2}",
                              reads=[B_[f"knst{i2}"]], writes=[scrB["kn"][p]])
                    u = 0
                    for tt in range(6):
                        ci = cg_of_tile(p, tt)
                        for hg in range(2):
                            b = nb()
                            for k in range(4):
                                MM(banks[b][:, 0:512], ckvn[:, k, tt * 128:(tt + 1) * 128], w4[:, k, hg * 4:(hg + 1) * 4, 128:256],
                                   k == 0, k == 3, [wb, ckvnB[k][ci]], [bankB[b]])
                            i2 = u % 2; u += 1
                            COPY(vst[i2][:, 0:512], banks[b][:, 0:512], [bankB[b]], [B_[f"vst{i2}"]], "act" if i2 else "dve")
                            h0 = t * 8 + hg * 4
                            S.dma("sp", v1_scr[:, p * 6 + tt, h0 * 128:(h0 + 4) * 128], vst[i2][:, 0:512], f"st_v1{i2}",
                                  reads=[B_[f"vst{i2}"]], writes=[scrB["v1"][p]])
                stage(specs, fn)

            for t in range(3):
                if "mdown" not in skip:
                    down_stage(t)
                if p == 0:
                    mark(f"A1d{t}")
            for t in range(8):
                uq_stage(t)
            if p == 0:
                mark("A1q")
            for t in range(2):
                ukv_stage(t)
            if p == 0:
                mark("A1k")

        def final_out(p, cgs):
            def fn(_):
                for cg in cgs:
                    n, off, ci = cg["n"], cg["off"], cg["ci"]
                    norm_stats(cg, KC, lambda d: xT[:, d, off:off + n], lambda d: [xB[d][ci]], 1.0 / D)
                    for d in range(KC):
                        STT(xT[:, d, off:off + n], xT[:, d, off:off + n], small[:, O_NORM + 96 + d:O_NORM + 97 + d],
                            rstd[:, off:off + n], ALU.mult, ALU.mult, [xB[d][ci], smallB, rstdB[ci]], [xB[d][ci]])
                    for tl in range(n // 128):
                        c0 = off + tl * 128
                        tok0 = cg["s0"] + tl * 128
                        for half in range(2):
                            sg = half
                            for q in range(2):
                                b = nb()
                                for j in range(4):
                                    d = half * 8 + q * 4 + j
                                    S.op("pe", lambda e, b=b, j=j, d=d, c0=c0: e.transpose(
                                        banks[b][:, j * 128:(j + 1) * 128], xT[:, d, c0:c0 + 128], ident[:]),
                                        reads=[xB[d][ci], identB], writes=[bankB[b]], inc=(j == 3))
                                COPY(stg[sg][:, q * 512:(q + 1) * 512], banks[b][:, 0:512], [bankB[b]], [stgB[sg]],
                                     "act" if q == 0 else "dve")
                            S.dma("sp", out_d[tok0:tok0 + 128, half * 1024:(half + 1) * 1024], stg[sg][:], f"st_stg{sg}",
                                  reads=[stgB[sg]], writes=[outB])
            stage([], fn)

        def mark(name):
            stage([], ("mark", name))

        for p in range(3):
            cgs = cginfo(p)
            load_x_from_input(p)
            if p == 0:
                drain_mods(8, urgent_only=True)
            modnorm(0, 0, cgs)
            ffn(0, 0, cgs, Wf["ffn1_w_gate"], Wf["ffn1_w_up"], Wf["ffn1_w_down"], store_p=p)
            if p == 0:
                drain_mods(100, urgent_only=True)
            modnorm(0, 1, cgs)
            barrier_stage()
            hyb_inproj(p)
            barrier_stage()
        mark("A0")
        for p in range(3):
            cgs = cginfo(p)
            attention(p, cgs, 0)
            barrier_stage(("sp",))
            mark(f"B0p{p}a")
            load_x_from_scr(p, cgs)
            proj_residual(0, cgs, hyb_w_out[0], lambda k, off, n: hT[:, k, off:off + n], lambda k, ci: hB[k][ci], 16)
            mark(f"B0p{p}b")
            modnorm(0, 2, cgs)
            ffn(0, 2, cgs, Wf["ffn2_w_gate"], Wf["ffn2_w_up"], Wf["ffn2_w_down"], store_p=p)
            barrier_stage(("act", "dve", "sp"))
            mark(f"B0p{p}c")
        mark("B0")
        drain_mods(100)
        for p in range(3):
            cgs = cginfo(p)
            load_x_from_scr(p, cgs)
            modnorm(1, 0, cgs)
            ffn(1, 0, cgs, Wf["ffn1_w_gate"], Wf["ffn1_w_up"], Wf["ffn1_w_down"], store_p=p)
            modnorm(1, 1, cgs)
            barrier_stage()
            mla_inproj(p)
            barrier_stage()
        mark("A1")
        for p in range(3):
            cgs = [cg for cg in cginfo(p) if cg["kind"] == "l"]
            attention(p, cgs, 1)
            barrier_stage(("sp",))
            load_x_from_scr(p, cgs)
            proj_residual(1, cgs, mla_w_o[0], lambda k, off, n: hT[:, k, off:off + n], lambda k, ci: hB[k][ci], 16)
            modnorm(1, 2, cgs)
            ffn(1, 2, cgs, Wf["ffn2_w_gate"], Wf["ffn2_w_up"], Wf["ffn2_w_down"], store_p=(p if dbg else None))
            final_out(p, cgs)
            barrier_stage(("act", "dve", "sp"))
        mark("B1")

        if stop_after is not None:
            cut = None
            for i, (specs, fn) in enumerate(stages):
                if isinstance(fn, tuple) and fn[1] == stop_after:
                    cut = i
            stages[:] = stages[:cut]
        stages[:] = [s for s in stages if not isinstance(s[1], tuple)]

        flat, first = [], []
        for specs, fn in stages:
            first.append(len(flat))
            flat.extend(specs)
        ptr, released = 0, 0
        for si, (specs, fn) in enumerate(stages):
            end = first[si] + len(specs)
            while ptr < len(flat) and ptr < released + NSLOT:
                sl = ptr % NSLOT
                for (o, i_) in flat[ptr](wslot[sl]):
                    S.dma("pool", o, i_, f"w{sl}", writes=[wB[sl]])
                ptr += 1
            assert ptr >= end
            fn([(wslot[n % NSLOT], wB[n % NSLOT]) for n in range(first[si], end)])
            released = end

        S.barrier(engs=("sp",))
        S.final_wait("sp", [outB] + [b for row in xscrB for b in row])
        emit_program(nc, S)
    return nc, S


def _rope_tables(n_tok, dim):
    grid_w = 64
    n_rows = n_tok // grid_w
    row = np.repeat(np.arange(n_rows), grid_w).astype(np.float32)
    col = np.tile(np.arange(grid_w), n_rows).astype(np.float32)
    half = dim // 2
    inv = (1.0 / (np.float32(10000.0) ** (np.arange(0, half, 2, dtype=np.float32) / np.float32(half)))).astype(np.float32)
    ang = np.concatenate([row[:, None] * inv, col[:, None] * inv], axis=-1).astype(np.float32)
    cos = np.cos(ang).astype(np.float32); sin = np.sin(ang).astype(np.float32)
    tab = np.stack([np.repeat(cos.T, 2, axis=0), np.repeat(sin.T, 2, axis=0)], axis=1)
    return np.ascontiguousarray(tab.astype(np.float32))


def _rot_lhsT(n):
    m = np.zeros((n, n), np.float32)
    for i in range(n // 2):
        m[2 * i + 1, 2 * i] = -1.0
        m[2 * i, 2 * i + 1] = 1.0
    return m


def _cols(v, nch):
    return np.ascontiguousarray(np.asarray(v, np.float32).reshape(nch, 128).T)


_CACHE = {}


def kernel(**inputs):
    inp = {k: np.asarray(v) for k, v in inputs.items()}
    if "prog" not in _CACHE:
        _CACHE["prog"] = build_program()[0]
    nc = _CACHE["prog"]
    B = inp["x"].shape[0]
    pt = np.zeros((128, 256), np.float32)
    pt[:, 0:128] = _rot_lhsT(128)
    pt[64:128, 192:256] = _rot_lhsT(64)
    shared = {
        "ident": np.eye(128, dtype=np.float32), "pt": pt,
        "ropeA": _rope_tables(NLAT, 128), "ropeM": _rope_tables(NLAT, 64),
        "mod_w": inp["mod_w"],
        "hyb_w_in": inp["hyb_w_in"], "hyb_w_out": inp["hyb_w_out"], "mla_w_down": inp["mla_w_down"],
        "mla_w_uq": inp["mla_w_uq"], "mla_w_ukv": inp["mla_w_ukv"], "mla_w_o": inp["mla_w_o"],
    }
    for n in ("ffn1_w_gate", "ffn1_w_up", "ffn1_w_down", "ffn2_w_gate", "ffn2_w_up", "ffn2_w_down"):
        shared[n] = inp[n]
    in_maps = []
    for b in range(B):
        small = np.zeros((128, NSMALL), np.float32)
        small[:, O_CVEC:O_CVEC + 16] = _cols(inp["c"][b], 16)
        small[:, O_CVEC + 16:O_CVEC + 32] = _cols(inp["c_ctx"], 16)
        for l in range(2):
            small[:, O_MODB + l * 144:O_MODB + (l + 1) * 144] = _cols(inp["mod_b"][l], 144)
            for s, nm in enumerate(("norm_ffn1", "norm_mix", "norm_ffn2")):
                small[:, O_NORM + (l * 3 + s) * 16:O_NORM + (l * 3 + s + 1) * 16] = _cols(inp[nm][l], 16)
        small[:, O_NORM + 96:O_NORM + 112] = _cols(inp["final_norm"], 16)
        for tap in range(3):
            small[:, O_CONVW + tap * 8:O_CONVW + (tap + 1) * 8] = _cols(inp["hyb_conv_w"][0, tap], 8)
        small[:, O_QKN] = inp["hyb_q_norm"][0]
        small[:, O_QKN + 1] = inp["hyb_k_norm"][0]
        small[:, O_QKN + 2:O_QKN + 8] = _cols(inp["mla_q_norm"][0], 6)
        small[:, O_QKN + 8:O_QKN + 12] = _cols(inp["mla_kv_norm"][0], 4)
        m = dict(shared)
        m["xin"] = np.ascontiguousarray(np.concatenate([inp["ctx"][b], inp["x"][b]], axis=0).astype(np.float32))
        m["small"] = small
        in_maps.append(m)
    res = run_bass_kernel_spmd(nc, in_maps, core_ids=list(range(B)))
    return np.stack([np.asarray(r["out"], dtype=np.float32) for r in res.results], axis=0)
```

```python
import contextlib
import numpy as np
import concourse.bass as bass
import concourse.mybir as mybir
from concourse.bass_utils import run_bass_kernel_spmd

F32 = mybir.dt.float32
BF16 = mybir.dt.bfloat16
ALU = mybir.AluOpType
AF = mybir.ActivationFunctionType

D = 2048
KC = 16
DFF = 5632
NFG = 11
NCTX = 256
NLAT = 2048
NTOK = 2304
TP = 768
EPS = 1e-6
NSLOT = 4
SLOT = 8192

PASS_CGS = [
    [("c", 0, 256), ("l", 0, 512)],
    [("l", 512, 384), ("l", 896, 384)],
    [("l", 1280, 384), ("l", 1664, 384)],
]

ENGS = ("pe", "act", "dve", "pool", "sp")


class Buf:
    __slots__ = ("name", "lw", "rd", "pend", "excl")

    def __init__(self, name, excl=False):
        self.name = name
        self.lw = None
        self.rd = {}
        self.pend = None
        self.excl = excl


class Sched:
    def __init__(self):
        self.streams = {e: [] for e in ENGS}
        self.cnt = {}
        self.waited = {e: {} for e in ENGS}
        self.pending = {e: [] for e in ENGS}
        self.ninstr = 0

    def _deps(self, eng, reads, writes):
        deps = {}

        def add(k, v):
            if deps.get(k, 0) < v:
                deps[k] = v
        for b in reads:
            if b.pend is not None and b.pend != eng:
                raise RuntimeError(f"buffer {b.name} pending on {b.pend}, used by {eng}")
            if b.lw is not None:
                add(*b.lw)
        for b in writes:
            if b.pend is not None and b.pend != eng:
                raise RuntimeError(f"buffer {b.name} pending on {b.pend}, used by {eng}")
            if b.lw is not None:
                add(*b.lw)
            for k, v in b.rd.items():
                add(k, v)
        waits = []
        w = self.waited[eng]
        for k, v in deps.items():
            if k == "pe" and eng == "pe":
                continue
            if w.get(k, 0) >= v:
                continue
            w[k] = v
            waits.append((k, v))
        return waits

    def op(self, eng, fn, reads=(), writes=(), inc=True, semkey=None, amount=1):
        if eng != "pe" and any(b.excl for b in reads):
            writes = list(writes) + [b for b in reads if b.excl]
            reads = [b for b in reads if not b.excl]
        waits = self._deps(eng, reads, writes)
        self.ninstr += 1
        if not inc:
            self.streams[eng].append((waits, fn, None))
            for b in reads:
                self.pending[eng].append((b, "r")); b.pend = eng
            for b in writes:
                self.pending[eng].append((b, "w")); b.pend = eng
            return None
        k = semkey if semkey is not None else eng
        v = self.cnt.get(k, 0) + amount
        self.cnt[k] = v
        self.streams[eng].append((waits, fn, (k, amount)))
        acc = self.pending[eng] + [(b, "r") for b in reads] + [(b, "w") for b in writes]
        self.pending[eng] = []
        for b, m in acc:
            b.pend = None
        for b, m in acc:
            if m == "r":
                if b.rd.get(k, 0) < v:
                    b.rd[k] = v
        for b, m in acc:
            if m == "w":
                b.lw = (k, v)
                b.rd = {}
        return (k, v)

    def dma(self, eng, out, in_, semkey, reads=(), writes=()):
        return self.op(eng, lambda e: e.dma_start(out=out, in_=in_), reads=reads, writes=writes,
                       semkey=semkey, amount=16)

    def barrier(self, engs=("pe", "act", "dve", "sp")):
        for e in ("pe", "act", "dve"):
            assert not self.pending[e], f"pending on {e} at barrier"
        snap = {k: v for k, v in self.cnt.items() if not (k.startswith("w") or k == "pool")}
        for e in engs:
            waits = []
            for k, v in snap.items():
                if self.waited[e].get(k, 0) >= v:
                    continue
                self.waited[e][k] = v
                waits.append((k, v))
            self.streams[e].append((waits, None, None))

    def final_wait(self, eng, bufs):
        waits = self._deps(eng, [], bufs)
        self.streams[eng].append((waits, None, None))


def emit_program(nc, sched):
    keys = list(sched.cnt.keys())
    with contextlib.ExitStack() as st:
        sems = {k: st.enter_context(nc.semaphore(f"s_{k}")) for k in keys}
        block = st.enter_context(nc.Block())

        def run(engname):
            def body(e):
                for waits, fn, inc in sched.streams[engname]:
                    for (k, v) in waits:
                        e.wait_ge(sems[k], v)
                    if fn is None:
                        continue
                    ins = fn(e)
                    if inc is not None:
                        ins.then_inc(sems[inc[0]], inc[1])
            return body

        block.tensor(run("pe"))
        block.scalar(run("act"))
        block.vector(run("dve"))
        block.gpsimd(run("pool"))
        block.sync(run("sp"))


O_CVEC, O_MODB, O_NORM, O_CONVW, O_QKN, NSMALL = 0, 32, 320, 432, 456, 468


def build_program(stop_after=None, dbg=False, skip=()):
    nc = bass.Bass("TRN2", target_bir_lowering=False)
    S = Sched()
    dram_in = lambda name, shape: nc.dram_tensor(name, shape, F32, kind="ExternalInput").ap()
    xin = dram_in("xin", [NTOK, D])
    small_d = dram_in("small", [128, NSMALL])
    ident_d = dram_in("ident", [128, 128])
    pt_d = dram_in("pt", [128, 256])
    ropeA_d = dram_in("ropeA", [128, 2, NLAT])
    ropeM_d = dram_in("ropeM", [64, 2, NLAT])
    mod_w = dram_in("mod_w", [2, D, 9 * D])
    Wf = {n: dram_in(n, [2, D, DFF]) for n in ("ffn1_w_gate", "ffn1_w_up", "ffn2_w_gate", "ffn2_w_up")}
    Wf.update({n: dram_in(n, [2, DFF, D]) for n in ("ffn1_w_down", "ffn2_w_down")})
    hyb_w_in = dram_in("hyb_w_in", [1, D, 4608])
    hyb_w_out = dram_in("hyb_w_out", [1, D, D])
    mla_w_down = dram_in("mla_w_down", [1, D, 1344])
    mla_w_uq = dram_in("mla_w_uq", [1, 768, 3072])
    mla_w_ukv = dram_in("mla_w_ukv", [1, 512, 4096])
    mla_w_o = dram_in("mla_w_o", [1, D, D])
    out_d = nc.dram_tensor("out", [NLAT, D], F32, kind="ExternalOutput").ap()

    scr = lambda name, shape, dt: nc.dram_tensor(name, shape, dt, kind="Internal").ap()
    xT_scr = (nc.dram_tensor("xT_scr", [128, KC, NTOK], F32, kind="ExternalOutput").ap() if dbg
              else scr("xT_scr", [128, KC, NTOK], F32))
    q0_scr = scr("q0_scr", [128, 8, NTOK], BF16)
    k0_scr = scr("k0_scr", [128, 2, NTOK], BF16)
    v0_scr = scr("v0_scr", [128, 18, 256], BF16)
    gb_scr = scr("gb_scr", [128, 8, NTOK], F32)
    gu_scr = scr("gu_scr", [128, 8, NTOK], F32)
    qn_scr = scr("qn_scr", [128, 16, NTOK], BF16)
    qr_scr = scr("qr_scr", [64, 16, NTOK], BF16)
    kn_scr = scr("kn_scr", [128, 16, NTOK], BF16)
    kr_scr = scr("kr_scr", [64, NTOK], BF16)
    v1_scr = scr("v1_scr", [128, 18, 2048], BF16)

    with contextlib.ExitStack() as st:
        def sb(name, shape, dt):
            return st.enter_context(nc.sbuf_tensor(name, shape, dt))

        hT = sb("hT", [128, KC, TP], BF16)
        wslot = [sb(f"wslot{i}", [128, SLOT], BF16) for i in range(NSLOT)]
        region = sb("region", [128, 19456], F32)
        small = sb("small_sb", [128, NSMALL], F32)
        ident = sb("ident_sb", [128, 128], F32)
        ptb = sb("ptb", [128, 256], BF16)
        ones = sb("ones", [128, 128], BF16)
        sc32 = sb("sc32", [128, 32], F32)
        scb = sb("scb", [128, KC, 2], BF16)
        modsT = sb("modsT", [128, 2 * 288], F32)
        coef = sb("coef", [128, 2 * 2 * 3 * 3 * 16], F32)
        rstd = sb("rstd", [128, TP], F32)
        sqb = [sb(f"sqb{i}", [128, TP], BF16) for i in range(2)]
        tmpb = [sb(f"tmpb{i}", [128, TP], F32) for i in range(2)]
        sil = [sb(f"sil{i}", [128, 512], F32) for i in range(2)]
        stg = [sb(f"stg{i}", [128, 1024], F32) for i in range(2)]
        banks = [st.enter_context(nc.psum_tensor(f"bank{i}", [128, 512], F32)) for i in range(8)]

        xT = region[:, 0:KC * TP].rearrange("p (k t) -> p k t", k=KC)
        A_region = region[:, KC * TP:KC * TP + 3072].bitcast(BF16)
        A_t = [A_region[:, i * 3072:(i + 1) * 3072].rearrange("p (f t) -> p f t", f=4) for i in range(2)]

        class WS:
            def __init__(self):
                self.off = 0

            def f32(self, n):
                a = region[:, self.off:self.off + n]
                self.off += n
                assert self.off <= 19456
                return a

            def bf(self, n):
                m = (n + 1) // 2
                a = region[:, self.off:self.off + m].bitcast(BF16)
                self.off += m
                assert self.off <= 19456
                return a[:, 0:n]

        bankB = [Buf(f"bank{i}", excl=True) for i in range(8)]
        xB = [[Buf(f"x{d}_{c}") for c in range(2)] for d in range(KC)]
        hB = [[Buf(f"h{d}_{c}") for c in range(2)] for d in range(KC)]
        AB = [[[Buf(f"A{i}_{f}_{c}") for c in range(2)] for f in range(4)] for i in range(2)]
        wB = [Buf(f"wslot{i}") for i in range(NSLOT)]
        smallB, identB, ptbB, onesB, scB, scbB = (Buf(n) for n in ("small", "ident", "ptb", "ones", "sc32", "scb"))
        modsB = [Buf("mods0"), Buf("mods1")]
        coefB = [[Buf(f"coef{l}_{s_}") for s_ in range(3)] for l in range(2)]
        rstdB = [Buf("rstd0"), Buf("rstd1")]
        sqB = [Buf("sq0"), Buf("sq1")]
        tmpB = [Buf("tmp0"), Buf("tmp1")]
        silB = [Buf("sil0"), Buf("sil1")]
        stgB = [Buf("stg0"), Buf("stg1")]
        xscrB = [[Buf(f"xscr{p}_{d}") for d in range(KC)] for p in range(3)]
        outB = Buf("out")
        scrB = {n: [Buf(f"{n}{p}") for p in range(3)] for n in
                ("q0", "k0", "v0", "gb", "gu", "qn", "qr", "kn", "kr", "v1")}

        rr = {"b": 0, "sil": 0, "sq": 0, "tmp": 0}

        def nb():
            b = rr["b"]
            rr["b"] = (b + 1) % 6
            return b

        def MM(out, lhsT, rhs, start, stop, rd, wr, inc=None):
            S.op("pe", lambda e: e.matmul(out, lhsT, rhs, start=start, stop=stop), reads=rd, writes=wr,
                 inc=(stop if inc is None else inc))

        def ACT(out, in_, func, rd, wr, bias=None, scale=None):
            kw = {}
            if bias is not None:
                kw["bias"] = bias
            if scale is not None:
                kw["scale"] = scale
            S.op("act", lambda e: e.activation(out, in_, func, **kw), reads=rd, writes=wr)

        def TT(out, a, b, op, rd, wr, eng="dve"):
            S.op(eng, lambda e: e.tensor_tensor(out, a, b, op), reads=rd, writes=wr)

        def STT(out, in0, scalar, in1, op0, op1, rd, wr, eng="dve"):
            S.op(eng, lambda e: e.scalar_tensor_tensor(out, in0, scalar, in1, op0, op1), reads=rd, writes=wr)

        def TS(out, in0, s1, op0, rd, wr, s2=None, op1=None, eng="dve"):
            if op1 is None:
                S.op(eng, lambda e: e.tensor_scalar(out, in0, s1, None, op0), reads=rd, writes=wr)
            else:
                S.op(eng, lambda e: e.tensor_scalar(out, in0, s1, s2, op0, op1), reads=rd, writes=wr)

        def RSQRT(dst, src, rd, wr, scale):
            ACT(dst, src, AF.Ln, rd, wr, bias=EPS, scale=scale)
            ACT(dst, dst, AF.Exp, wr, wr, scale=-0.5)

        def RECIP(out, in_, rd, wr):
            S.op("dve", lambda e: e.reciprocal(out, in_), reads=rd, writes=wr)

        def COPY(out, in_, rd, wr, eng):
            if eng == "act":
                S.op("act", lambda e: e.activation(out, in_, AF.Copy), reads=rd, writes=wr)
            else:
                S.op(eng, lambda e: e.tensor_copy(out, in_), reads=rd, writes=wr)

        def MEMSET(ap, val, wr, eng="dve"):
            S.op(eng, lambda e: e.memset(ap, val), writes=wr)

        def V3(ap2d, a, b):
            return ap2d[:, 0:a * b].rearrange("p (a b) -> p a b", a=a)

        def wsrc(w2d, c0, n):
            return w2d.rearrange("(k p) f -> p k f", p=128)[:, :, c0:c0 + n]

        stages = []

        def stage(specs, fn):
            stages.append((specs, fn))

        def cginfo(p):
            res = []
            off = 0
            for ci, (kind, s0, n) in enumerate(PASS_CGS[p]):
                res.append(dict(ci=ci, kind=kind, s0=s0, n=n, off=off, g0=p * TP + off, v=(1 if kind == "c" else 0)))
                off += n
            return res

        def cg_of_tile(p, tt):
            for cg in cginfo(p):
                if cg["off"] <= tt * 128 < cg["off"] + cg["n"]:
                    return cg["ci"]
            raise AssertionError

        def coef_col(l, v, s, which, d):
            i = ((((l * 2 + v) * 3 + s) * 3 + which) * 16) + d
            return coef[:, i:i + 1]

        def init_stage(_):
            S.dma("sp", small[:], small_d, "ld_small", writes=[smallB])
            S.dma("sp", ident[:], ident_d, "ld_ident", writes=[identB])
            S.dma("pool", ptb[:], pt_d, "w_pt", writes=[ptbB])
            MEMSET(ones[:], 1.0, [onesB])
            ACT(sc32[:], small[:, O_CVEC:O_CVEC + 32], AF.Silu, [smallB], [scB])
            for v in range(2):
                COPY(scb[:, :, v], sc32[:, v * 16:(v + 1) * 16], [scB], [scbB], "dve")
        stage([], init_stage)

        modq = []

        def mod_stage(l, ct):
            spec = [lambda wt: [(V3(wt, 16, 512), wsrc(mod_w[l], ct * 512, 512))]]

            def fn(slots):
                (wt, wb), = slots
                w3 = V3(wt, 16, 512)
                for cc in range(4):
                    for k in range(KC):
                        MM(banks[6][:, cc * 2:(cc + 1) * 2], w3[:, k, cc * 128:(cc + 1) * 128], scb[:, k, :],
                           k == 0, k == KC - 1, [wb, scbB], [bankB[6]])
                pv = banks[6][:, 0:8].rearrange("p (m v) -> p m v", v=2)
                mv = modsT[:, l * 288:(l + 1) * 288].rearrange("p (m v) -> p m v", v=2)
                for v in range(2):
                    TT(mv[:, ct * 4:(ct + 1) * 4, v], pv[:, :, v],
                       small[:, O_MODB + l * 144 + ct * 4:O_MODB + l * 144 + (ct + 1) * 4], ALU.add,
                       [bankB[6], smallB], [modsB[l]])
                if ct % 12 in (7, 11):
                    s = ct // 12
                    for v in range(2):
                        def mcol(m):
                            return mv[:, m * 16:(m + 1) * 16, v]
                        base = (((l * 2 + v) * 3 + s) * 3) * 16
                        g = small[:, O_NORM + (l * 3 + s) * 16:O_NORM + (l * 3 + s + 1) * 16]
                        if ct % 12 == 7:
                            STT(coef[:, base:base + 16], mcol(3 * s + 1), 1.0, g, ALU.add, ALU.mult,
                                [modsB[l], smallB], [coefB[l][s]])
                            COPY(coef[:, base + 16:base + 32], mcol(3 * s), [modsB[l]], [coefB[l][s]], "dve")
                        else:
                            TS(coef[:, base + 32:base + 48], mcol(3 * s + 2), (1.0 if s == 1 else 0.5), ALU.mult,
                               [modsB[l]], [coefB[l][s]])
            stage(spec, fn)

        def drain_mods(n=1, urgent_only=False):
            while n > 0 and modq:
                if urgent_only and not modq[0][0]:
                    return
                _, l, ct = modq.pop(0)
                mod_stage(l, ct)
                n -= 1

        if "mods" not in skip:
            load_x_from_input_early = True
            modq.extend((True, 0, ct) for ct in range(0, 24))
            modq.extend((False, 0, ct) for ct in range(24, 36))
            modq.extend((False, 1, ct) for ct in range(36))

        def load_x_from_input(p):
            def fn(_):
                for tt in range(6):
                    gt = p * 6 + tt
                    ci = cg_of_tile(p, tt)
                    for half in range(2):
                        sg = half
                        S.dma("sp", stg[sg][:], xin[gt * 128:(gt + 1) * 128, half * 1024:(half + 1) * 1024],
                              f"ld_stg{sg}", writes=[stgB[sg]])
                        for q in range(2):
                            b = nb()
                            for j in range(4):
                                dl = q * 4 + j
                                S.op("pe", lambda e, b=b, j=j, dl=dl, sg=sg: e.transpose(
                                    banks[b][:, j * 128:(j + 1) * 128], stg[sg][:, dl * 128:(dl + 1) * 128], ident[:]),
                                    reads=[stgB[sg], identB], writes=[bankB[b]], inc=(j == 3))
                            d0 = half * 8 + q * 4
                            COPY(xT[:, d0:d0 + 4, tt * 128:(tt + 1) * 128],
                                 banks[b][:, 0:512].rearrange("p (a t) -> p a t", a=4),
                                 [bankB[b]], [xB[d][ci] for d in range(d0, d0 + 4)], "act" if q == 0 else "dve")
            stage([], fn)

        def load_x_from_scr(p, cgs):
            def fn(_):
                c0 = cgs[0]["off"]; c1 = cgs[-1]["off"] + cgs[-1]["n"]
                for d in range(KC):
                    S.dma("sp", xT[:, d, c0:c1], xT_scr[:, d, p * TP + c0:p * TP + c1], f"ld_x{d}",
                          reads=[xscrB[p][d]], writes=[xB[d][cg["ci"]] for cg in cgs])
            stage([], fn)

        def store_x_chunk(p, cgs, d):
            c0 = cgs[0]["off"]; c1 = cgs[-1]["off"] + cgs[-1]["n"]
            S.dma("sp", xT_scr[:, d, p * TP + c0:p * TP + c1], xT[:, d, c0:c1], f"st_x{d}",
                  reads=[xB[d][cg["ci"]] for cg in cgs], writes=[xscrB[p][d]])

        def store_x_to_scr(p, cgs):
            def fn(_):
                for d in range(KC):
                    store_x_chunk(p, cgs, d)
            stage([], fn)

        def norm_stats(cg, nchunks, src_fn, src_bufs_fn, inv_n):
            n, off, ci = cg["n"], cg["off"], cg["ci"]
            for d in range(nchunks):
                i = rr["sq"]; rr["sq"] ^= 1
                ACT(sqb[i][:, 0:n], src_fn(d), AF.Square, src_bufs_fn(d), [sqB[i]])
                MM(banks[7][:, 0:n], ones[:], sqb[i][:, 0:n], d == 0, d == nchunks - 1, [onesB, sqB[i]], [bankB[7]], inc=True)
            RSQRT(rstd[:, off:off + n], banks[7][:, 0:n], [bankB[7]], [rstdB[ci]], inv_n)

        def modnorm(l, s, cgs):
            def fn(_):
                merged = (len(cgs) == 2 and cgs[0]["v"] == cgs[1]["v"]
                          and cgs[0]["off"] + cgs[0]["n"] == cgs[1]["off"])
                groups = [list(cgs)] if merged else [[cg] for cg in cgs]
                for grp in groups:
                    off = grp[0]["off"]; n = sum(cg["n"] for cg in grp); v = grp[0]["v"]
                    cis = [cg["ci"] for cg in grp]
                    for d in range(KC):
                        i = rr["sq"]; rr["sq"] ^= 1
                        xin_ = xT[:, d, off:off + n]
                        if d % 2 == 0:
                            ACT(sqb[i][:, 0:n], xin_, AF.Square, [xB[d][ci] for ci in cis], [sqB[i]])
                        else:
                            TT(sqb[i][:, 0:n], xin_, xin_, ALU.mult, [xB[d][ci] for ci in cis], [sqB[i]])
                        for gi, cg in enumerate(grp):
                            o2 = cg["off"] - off
                            MM(banks[7 - gi][:, 0:cg["n"]], ones[:], sqb[i][:, o2:o2 + cg["n"]], d == 0, d == KC - 1,
                               [onesB, sqB[i]], [bankB[7 - gi]], inc=True)
                    for gi, cg in enumerate(grp):
                        RSQRT(rstd[:, cg["off"]:cg["off"] + cg["n"]], banks[7 - gi][:, 0:cg["n"]], [bankB[7 - gi]],
                              [rstdB[cg["ci"]]], 1.0 / D)
                    for d in range(KC):
                        i = rr["tmp"]; rr["tmp"] ^= 1
                        TT(tmpb[i][:, 0:n], xT[:, d, off:off + n], rstd[:, off:off + n], ALU.mult,
                           [xB[d][ci] for ci in cis] + [rstdB[ci] for ci in cis], [tmpB[i]])
                        ACT(hT[:, d, off:off + n], tmpb[i][:, 0:n], AF.Identity, [tmpB[i], coefB[l][s]],
                            [hB[d][ci] for ci in cis], bias=coef_col(l, v, s, 1, d), scale=coef_col(l, v, s, 0, d))
            stage([], fn)

        def ffn(l, s, cgs, Wg, Wu, Wd, store_p=None):
            def gu_stage(fg):
                specs = [lambda wt: [(V3(wt, 16, 512), wsrc(Wg[l], fg * 512, 512))],
                         lambda wt: [(V3(wt, 16, 512), wsrc(Wu[l], fg * 512, 512))]]

                def fn(slots):
                    (wg, wgb), (wu, wub) = slots
                    wg3, wu3 = V3(wg, 16, 512), V3(wu, 16, 512)
                    ab = fg % 2
                    if fg == 0:
                        order = [(fl, cg) for cg in cgs for fl in range(4)]
                    else:
                        order = [(fl, cg) for fl in range(4) for cg in cgs]
                    for fl, cg in order:
                        n, off, ci = cg["n"], cg["off"], cg["ci"]
                        gbk, ubk = nb(), nb()
                        for k in range(KC):
                            MM(banks[gbk][:, 0:n], wg3[:, k, fl * 128:(fl + 1) * 128], hT[:, k, off:off + n],
                               k == 0, k == KC - 1, [wgb, hB[k][ci]], [bankB[gbk]])
                        for k in range(KC):
                            MM(banks[ubk][:, 0:n], wu3[:, k, fl * 128:(fl + 1) * 128], hT[:, k, off:off + n],
                               k == 0, k == KC - 1, [wub, hB[k][ci]], [bankB[ubk]])
                        i = rr["sil"]; rr["sil"] ^= 1
                        ACT(sil[i][:, 0:n], banks[gbk][:, 0:n], AF.Silu, [bankB[gbk]], [silB[i]])
                        TT(A_t[ab][:, fl, off:off + n], sil[i][:, 0:n], banks[ubk][:, 0:n], ALU.mult,
                           [silB[i], bankB[ubk]], [AB[ab][fl][ci]])
                stage(specs, fn)
                drain_mods(4 if (fg == 0 and modq and modq[0][0]) else 1)

            def down_stage(fg):
                specs = [lambda wt: [(V3(wt, 4, 2048),
                                      Wd[l][fg * 512:(fg + 1) * 512, :].rearrange("(f p) d -> p f d", p=128))]]

                def fn(slots):
                    (wd, wdb), = slots
                    wd3 = V3(wd, 4, 2048)
                    ab = fg % 2
                    for d in range(KC):
                        for cg in cgs:
                            n, off, ci, v = cg["n"], cg["off"], cg["ci"], cg["v"]
                            yb = nb()
                            for fl in range(4):
                                MM(banks[yb][:, 0:n], wd3[:, fl, d * 128:(d + 1) * 128], A_t[ab][:, fl, off:off + n],
                                   fl == 0, fl == 3, [wdb, AB[ab][fl][ci]], [bankB[yb]])
                            STT(xT[:, d, off:off + n], banks[yb][:, 0:n], coef_col(l, v, s, 2, d), xT[:, d, off:off + n],
                                ALU.mult, ALU.add, [bankB[yb], xB[d][ci], coefB[l][s]], [xB[d][ci]])
                        if store_p is not None and fg == NFG - 1:
                            store_x_chunk(store_p, cgs, d)
                stage(specs, fn)
                if fg % 2 == 1:
                    drain_mods(1, urgent_only=True)

            if "ffn" in skip:
                if store_p is not None:
                    store_x_to_scr(store_p, cgs)
                return
            gu_stage(0)
            for fg in range(1, NFG):
                gu_stage(fg)
                down_stage(fg - 1)
            down_stage(NFG - 1)

        def barrier_stage(engs=("pe", "act", "dve", "sp")):
            stage([], lambda _: S.barrier(engs=engs))

        def proj_residual(l, cgs, W2d, src_fn, src_buf_fn, nk):
            def pstage(t4):
                specs = [lambda wt: [(V3(wt, nk, 512), wsrc(W2d, t4 * 512, 512))]]

                def fn(slots):
                    (wt, wb), = slots
                    w3 = V3(wt, nk, 512)
                    for dl in range(4):
                        d = t4 * 4 + dl
                        for cg in cgs:
                            n, off, ci, v = cg["n"], cg["off"], cg["ci"], cg["v"]
                            yb = nb()
                            for k in range(nk):
                                MM(banks[yb][:, 0:n], w3[:, k, dl * 128:(dl + 1) * 128], src_fn(k, off, n),
                                   k == 0, k == nk - 1, [wb, src_buf_fn(k, ci)], [bankB[yb]])
                            STT(xT[:, d, off:off + n], banks[yb][:, 0:n], coef_col(l, v, 1, 2, d), xT[:, d, off:off + n],
                                ALU.mult, ALU.add, [bankB[yb], xB[d][ci], coefB[l][1]], [xB[d][ci]])
                stage(specs, fn)
            for t4 in range(4):
                pstage(t4)

        pipeB, pipeC = [], []

        def pipe_step():
            nB = list(pipeB); pipeB.clear()
            nC = list(pipeC); pipeC.clear()
            for f in nB:
                f()
            for f in nC:
                f()

        def pipe_flush():
            while pipeB or pipeC:
                pipe_step()

        def rope_tail(P, n, src32, src32B, srcbf, srcbfB, ptv, cosv, sinv, ropeBuf, t1, t1B, t2, t2B, dst, dstB):
            b3 = nb()
            MM(banks[b3][0:P, 0:n], ptv, srcbf, True, True, [ptbB, srcbfB], [bankB[b3]])
            TT(t1, src32, cosv, ALU.mult, [src32B, ropeBuf], [t1B])
            TT(t2, banks[b3][0:P, 0:n], sinv, ALU.mult, [bankB[b3], ropeBuf], [t2B])
            TT(dst, t1, t2, ALU.add, [t1B, t2B], [dstB])

        def rope_apply(P, n, src32, src32B, srcbf, srcbfB, ptv, cosv, sinv, ropeBuf, t1, t1B, t2, t2B, dst, dstB):
            COPY(srcbf, src32, [src32B], [srcbfB], "act")
            b3 = nb()
            MM(banks[b3][0:P, 0:n], ptv, srcbf, True, True, [ptbB, srcbfB], [bankB[b3]])
            TT(t1, src32, cosv, ALU.mult, [src32B, ropeBuf], [t1B])
            TT(t2, banks[b3][0:P, 0:n], sinv, ALU.mult, [bankB[b3], ropeBuf], [t2B])
            TT(dst, t1, t2, ALU.add, [t1B, t2B], [dstB])

        def load_rope(p, cgs, tab_d, P, ropeT, ropeBuf, base=0):
            for cg in cgs:
                if cg["kind"] != "l":
                    continue
                n, off, s0 = cg["n"], cg["off"], cg["s0"]
                S.dma("sp", ropeT[base:base + P, :, off:off + n], tab_d[:, :, s0:s0 + n], "ld_rope", writes=[ropeBuf])

        def rope_hi_tail(n, src32, src32B, srcbf, srcbfB, cosv, sinv, ropeBuf, t1, t1B, t2, t2B, dst, dstB):
            b3 = nb()
            MM(banks[b3][:, 0:n], ptb[:, 128:256], srcbf[:, 0:n], True, True, [ptbB, srcbfB], [bankB[b3]])
            TT(t1[64:128, 0:n], src32[64:128, 0:n], cosv, ALU.mult, [src32B, ropeBuf], [t1B])
            TT(t2[64:128, 0:n], banks[b3][64:128, 0:n], sinv, ALU.mult, [bankB[b3], ropeBuf], [t2B])
            TT(dst, t1[64:128, 0:n], t2[64:128, 0:n], ALU.add, [t1B, t2B], [dstB])

        def hyb_inproj(p):
            cgs = cginfo(p)
            ws = WS()
            gcs = ws.f32(512); gust = [ws.f32(TP) for _ in range(2)]; gbst = [ws.f32(TP) for _ in range(2)]
            rsq_ = [ws.f32(512) for _ in range(2)]; qn32_ = [ws.f32(512) for _ in range(2)]
            t1_ = [ws.f32(512) for _ in range(2)]; t2_ = [ws.f32(512) for _ in range(2)]
            ropeT = ws.f32(2 * TP).rearrange("p (a t) -> p a t", a=2)
            sqq_ = [ws.bf(512) for _ in range(2)]; qnb_ = [ws.bf(512) for _ in range(2)]
            qkc = {"i": 0}
            qst = [ws.bf(TP) for _ in range(2)]
            vst = ws.bf(6 * 256).rearrange("p (a t) -> p a t", a=6)
            B_ = {n: Buf("hyb_" + n) for n in ("gcs", "gust0", "gust1", "gbst0", "gbst1", "rsq0", "qn320", "t10", "t20",
                                               "rsq1", "qn321", "t11", "t21", "sqq0", "qnb0", "sqq1", "qnb1",
                                               "rope", "qst0", "qst1", "vst")}
            W = hyb_w_in[0]

            def pre(_):
                load_rope(p, cgs, ropeA_d, 128, ropeT, B_["rope"])
            stage([], pre)

            def conv_stage(j):
                specs = [lambda wt: [(V3(wt, 16, 512)[:, :, i * 128:(i + 1) * 128], wsrc(W, i * 1024 + j * 128, 128))
                                     for i in range(3)]]

                def fn(slots):
                    (wt, wb), = slots
                    w3 = V3(wt, 16, 512)
                    i2 = j % 2
                    for cg in cgs:
                        n, off, ci = cg["n"], cg["off"], cg["ci"]
                        b0, b1, b2 = nb(), nb(), nb()
                        for bi, bk in enumerate((b0, b1, b2)):
                            for k in range(KC):
                                MM(banks[bk][:, 0:n], w3[:, k, bi * 128:(bi + 1) * 128], hT[:, k, off:off + n],
                                   k == 0, k == KC - 1, [wb, hB[k][ci]], [bankB[bk]])
                        COPY(gbst[i2][:, off:off + n], banks[b0][:, 0:n], [bankB[b0]], [B_[f"gbst{i2}"]], "act")
                        COPY(gcs[:, 0:n], banks[b1][:, 0:n], [bankB[b1]], [B_["gcs"]], "act")
                        TT(gust[i2][:, off:off + n], gcs[:, 0:n], banks[b2][:, 0:n], ALU.mult,
                           [B_["gcs"], bankB[b2]], [B_[f"gust{i2}"]])
                    S.dma("sp", gb_scr[:, j, p * TP:(p + 1) * TP], gbst[i2], f"st_gb{i2}",
                          reads=[B_[f"gbst{i2}"]], writes=[scrB["gb"][p]])
                    S.dma("sp", gu_scr[:, j, p * TP:(p + 1) * TP], gust[i2], f"st_gu{i2}",
                          reads=[B_[f"gust{i2}"]], writes=[scrB["gu"][p]])
                stage(specs, fn)

            def qk_chunk(w3, wb, col0, gaincol, cg, dst, dstB, after=None):
                n, off, ci = cg["n"], cg["off"], cg["ci"]
                z = qkc["i"]; qkc["i"] ^= 1
                rsq, qn32, t1, t2, sqq, qnb = rsq_[z], qn32_[z], t1_[z], t2_[z], sqq_[z], qnb_[z]
                Bz = {k: B_[f"{k}{z}"] for k in ("rsq", "qn32", "t1", "t2", "sqq", "qnb")}
                b = nb()
                for k in range(KC):
                    MM(banks[b][:, 0:n], w3[:, k, col0:col0 + 128], hT[:, k, off:off + n], k == 0, k == KC - 1,
                       [wb, hB[k][ci]], [bankB[b]])
                ACT(sqq[:, 0:n], banks[b][:, 0:n], AF.Square, [bankB[b]], [Bz["sqq"]])
                pipe_step()

                def Bf():
                    b2 = nb()
                    MM(banks[b2][:, 0:n], ones[:], sqq[:, 0:n], True, True, [onesB, Bz["sqq"]], [bankB[b2]])
                    RSQRT(rsq[:, 0:n], banks[b2][:, 0:n], [bankB[b2]], [Bz["rsq"]], 1.0 / 128)
                    if cg["kind"] == "c":
                        STT(dst[:, off:off + n], banks[b][:, 0:n], gaincol, rsq[:, 0:n], ALU.mult, ALU.mult,
                            [bankB[b], Bz["rsq"], smallB], [dstB])
                        if after is not None:
                            pipeC.append(after)
                    else:
                        STT(qn32[:, 0:n], banks[b][:, 0:n], gaincol, rsq[:, 0:n], ALU.mult, ALU.mult,
                            [bankB[b], Bz["rsq"], smallB], [Bz["qn32"]])
                        COPY(qnb[:, 0:n], qn32[:, 0:n], [Bz["qn32"]], [Bz["qnb"]], "act")

                        def Cf():
                            rope_tail(128, n, qn32[:, 0:n], Bz["qn32"], qnb[:, 0:n], Bz["qnb"], ptb[:, 0:128],
                                      ropeT[:, 0, off:off + n], ropeT[:, 1, off:off + n], B_["rope"],
                                      t1[:, 0:n], Bz["t1"], t2[:, 0:n], Bz["t2"], dst[:, off:off + n], dstB)
                            if after is not None:
                                after()
                        pipeC.append(Cf)
                pipeB.append(Bf)

            def q_stage(qt):
                specs = [lambda wt: [(V3(wt, 16, 512), wsrc(W, 3072 + qt * 512, 512))]]

                def fn(slots):
                    (wt, wb), = slots
                    w3 = V3(wt, 16, 512)
                    for hh in range(4):
                        h = qt * 4 + hh
                        i2 = h % 2

                        def store(h=h, i2=i2):
                            S.dma("sp", q0_scr[:, h, p * TP:(p + 1) * TP], qst[i2], f"st_q{i2}",
                                  reads=[B_[f"qst{i2}"]], writes=[scrB["q0"][p]])
                        for cg in cgs:
                            qk_chunk(w3, wb, hh * 128, small[:, O_QKN:O_QKN + 1], cg, qst[i2], B_[f"qst{i2}"],
                                     after=(store if cg is cgs[-1] else None))
                stage(specs, fn)

            def kv_stage():
                specs = [lambda wt: [(V3(wt, 16, 512), wsrc(W, 4096, 512))]]

                def fn(slots):
                    (wt, wb), = slots
                    w3 = V3(wt, 16, 512)
                    for g in range(2):
                        i2 = g % 2

                        def store(g=g, i2=i2):
                            S.dma("sp", k0_scr[:, g, p * TP:(p + 1) * TP], qst[i2], f"st_q{i2}",
                                  reads=[B_[f"qst{i2}"]], writes=[scrB["k0"][p]])
                        for cg in cgs:
                            qk_chunk(w3, wb, g * 128, small[:, O_QKN + 1:O_QKN + 2], cg, qst[i2], B_[f"qst{i2}"],
                                     after=(store if cg is cgs[-1] else None))
                    for tt in range(6):
                        ci = cg_of_tile(p, tt)
                        vb = nb()
                        for k in range(KC):
                            MM(banks[vb][:, 0:256], hT[:, k, tt * 128:(tt + 1) * 128], w3[:, k, 256:512],
                               k == 0, k == KC - 1, [wb, hB[k][ci]], [bankB[vb]])
                        COPY(vst[:, tt, :], banks[vb][:, 0:256], [bankB[vb]], [B_["vst"]], "act" if tt % 2 else "dve")
                        if tt < 4:
                            pipe_step()
                    pipe_flush()
                    S.dma("sp", v0_scr[:, p * 6:(p + 1) * 6, :], vst, "st_v", reads=[B_["vst"]], writes=[scrB["v0"][p]])
                stage(specs, fn)

            for j in range(8):
                conv_stage(j)
            for qt in range(2):
                q_stage(qt)
            kv_stage()

        def attention(p, cgs, layer):
            ws = WS()
            kT = [ws.bf(NTOK) for _ in range(2)]
            Vt = [ws.bf(18 * 128).rearrange("p (a t) -> p a t", a=18) for _ in range(2)]
            krT = ws.bf(NTOK)
            qh = [ws.bf(TP) for _ in range(2)]
            qrh = [ws.bf(TP) for _ in range(2)]
            pT = [ws.bf(512) for _ in range(4)]
            rec = ws.f32(512)
            gux = [ws.f32(514) for _ in range(2)]
            gbx = [ws.f32(512) for _ in range(2)]
            acc = [ws.f32(512) for _ in range(2)]
            B_ = {n: Buf(f"att_{n}") for n in ("kT0", "kT1", "V0", "V1", "krT", "qh0", "qh1", "qrh0", "qrh1",
                                               "pT0", "pT1", "pT2", "pT3", "rec", "gux0", "gux1", "gbx0", "gbx1", "acc0", "acc1")}
            sbanks, obanks, dbanks = [0, 1, 2, 3], [4, 5], [6, 7]
            st_ = {"o": 0, "q": 0, "kv": 0}
            allp = lambda name: scrB[name]
            PD = 2

            def mk_unit(q_ap, qB, qr_ap, qrB, kt_ap, ktB, kr_ap, krB, v_ap, vB, cg, scale, chunk):
                oi = st_["o"]; st_["o"] ^= 1
                return dict(q=q_ap, qB=qB, qr=qr_ap, qrB=qrB, kt=kt_ap, ktB=ktB, kr=kr_ap, krB=krB, v=v_ap, vB=vB,
                            cg=cg, scale=scale, chunk=chunk, nkt=(2 if cg["kind"] == "c" else 18),
                            ob=obanks[oi], db=dbanks[oi])

            def run_units(gen, side=()):
                side = list(side)
                flat = []
                it = iter(gen)

                def ensure(idx):
                    while len(flat) <= idx:
                        try:
                            u = next(it)
                        except StopIteration:
                            return False
                        for kt in range(u["nkt"]):
                            flat.append((u, kt))
                    return True

                def s_mm(idx):
                    u, kt = flat[idx]
                    n, off = u["cg"]["n"], u["cg"]["off"]
                    sbk = sbanks[idx % 4]
                    MM(banks[sbk][:, 0:n], u["kt"][:, kt * 128:(kt + 1) * 128], u["q"][:, off:off + n], True, u["qr"] is None,
                       [u["ktB"], u["qB"]], [bankB[sbk]])
                    if u["qr"] is not None:
                        MM(banks[sbk][:, 0:n], u["kr"][:, kt * 128:(kt + 1) * 128], u["qr"][:, off:off + n], False, True,
                           [u["krB"], u["qrB"]], [bankB[sbk]])

                t = 0
                issued = 0
                while ensure(t):
                    while issued <= t + PD and ensure(issued):
                        s_mm(issued)
                        issued += 1
                    u, kt = flat[t]
                    n, off, ci = u["cg"]["n"], u["cg"]["off"], u["cg"]["ci"]
                    sbk = sbanks[t % 4]
                    pt = pT[t % 4]; ptB_ = B_[f"pT{t % 4}"]
                    ACT(pt[:, 0:n], banks[sbk][:, 0:n], AF.Exp, [bankB[sbk]], [ptB_], scale=u["scale"])
                    last = kt == u["nkt"] - 1
                    ob, db = u["ob"], u["db"]
                    MM(banks[ob][:, 0:n], u["v"][:, kt, :], pt[:, 0:n], kt == 0, last, [u["vB"], ptB_], [bankB[ob]])
                    MM(banks[db][:, 0:n], ones[:], pt[:, 0:n], kt == 0, last, [onesB, ptB_], [bankB[db]])
                    if last:
                        RECIP(rec[:, 0:n], banks[db][:, 0:n], [bankB[db]], [B_["rec"]])
                        TT(hT[:, u["chunk"], off:off + n], banks[ob][:, 0:n], rec[:, 0:n], ALU.mult,
                           [bankB[ob], B_["rec"]], [hB[u["chunk"]][ci]])
                        if side:
                            side.pop(0)()
                    t += 1
                while side:
                    side.pop(0)()

            c0 = cgs[0]["off"]; c1 = cgs[-1]["off"] + cgs[-1]["n"]

            def gqa_units():
                for g in range(2):
                    i = st_["kv"]; st_["kv"] ^= 1
                    S.dma("sp", kT[i], k0_scr[:, g, :], f"ld_kT{i}", reads=allp("k0"), writes=[B_[f"kT{i}"]])
                    S.dma("sp", Vt[i], v0_scr[:, :, g * 128:(g + 1) * 128], f"ld_V{i}", reads=allp("v0"),
                          writes=[B_[f"V{i}"]])
                    for r in range(4):
                        h = g * 4 + r
                        qi = st_["q"]; st_["q"] ^= 1
                        S.dma("sp", qh[qi][:, c0:c1], q0_scr[:, h, p * TP + c0:p * TP + c1], f"ld_q{qi}",
                              reads=[scrB["q0"][p]], writes=[B_[f"qh{qi}"]])
                        for cg in cgs:
                            yield mk_unit(qh[qi], B_[f"qh{qi}"], None, None, kT[i], B_[f"kT{i}"], None, None,
                                          Vt[i], B_[f"V{i}"], cg, 128 ** -0.5, 8 + h)

            def mla_units():
                MEMSET(krT[64:128, :], 0.0, [B_["krT"]])
                for i_ in range(2):
                    MEMSET(qrh[i_][64:128, :], 0.0, [B_[f"qrh{i_}"]])
                S.dma("sp", krT[0:64, :], kr_scr, "ld_kr", reads=allp("kr"), writes=[B_["krT"]])
                for h in range(16):
                    i = st_["kv"]; st_["kv"] ^= 1
                    S.dma("sp", kT[i], kn_scr[:, h, :], f"ld_kT{i}", reads=allp("kn"), writes=[B_[f"kT{i}"]])
                    S.dma("sp", Vt[i], v1_scr[:, :, h * 128:(h + 1) * 128], f"ld_V{i}", reads=allp("v1"),
                          writes=[B_[f"V{i}"]])
                    qi = st_["q"]; st_["q"] ^= 1
                    S.dma("sp", qh[qi][:, c0:c1], qn_scr[:, h, p * TP + c0:p * TP + c1], f"ld_q{qi}",
                          reads=[scrB["qn"][p]], writes=[B_[f"qh{qi}"]])
                    S.dma("sp", qrh[qi][0:64, c0:c1], qr_scr[:, h, p * TP + c0:p * TP + c1], f"ld_qr{qi}",
                          reads=[scrB["qr"][p]], writes=[B_[f"qrh{qi}"]])
                    for cg in cgs:
                        yield mk_unit(qh[qi], B_[f"qh{qi}"], qrh[qi], B_[f"qrh{qi}"], kT[i], B_[f"kT{i}"], krT, B_["krT"],
                                      Vt[i], B_[f"V{i}"], cg, 192 ** -0.5, h)

            def gqa_fn(_):
                run_units(gqa_units(), side=conv_jobs())

            def mla_fn(_):
                run_units(mla_units())

            def conv_jobs():
                jobs = []
                cnt = {"u": 0}
                for j in range(8):
                    for cg in cgs:
                        def job(j=j, cg=cg):
                            n, off, ci, a = cg["n"], cg["off"], cg["ci"], cg["g0"]
                            b = a + n
                            s_lo, s_hi = (0, NCTX) if cg["kind"] == "c" else (NCTX, NTOK)
                            i = cnt["u"] % 2; cnt["u"] += 1
                            lo = max(a - 1, s_lo); hi = min(b + 1, s_hi)
                            if a - 1 < s_lo:
                                MEMSET(gux[i][:, 0:1], 0.0, [B_[f"gux{i}"]])
                            if b + 1 > s_hi:
                                MEMSET(gux[i][:, n + 1:n + 2], 0.0, [B_[f"gux{i}"]])
                            S.dma("sp", gux[i][:, lo - (a - 1):hi - (a - 1)], gu_scr[:, j, lo:hi], f"ld_gux{i}",
                                  reads=allp("gu"), writes=[B_[f"gux{i}"]])
                            S.dma("sp", gbx[i][:, 0:n], gb_scr[:, j, a:b], f"ld_gbx{i}", reads=allp("gb"),
                                  writes=[B_[f"gbx{i}"]])
                            cw = lambda tap: small[:, O_CONVW + tap * 8 + j:O_CONVW + tap * 8 + j + 1]
                            TS(acc[i][:, 0:n], gux[i][:, 1:n + 1], cw(1), ALU.mult, [B_[f"gux{i}"], smallB], [B_[f"acc{i}"]])
                            STT(acc[i][:, 0:n], gux[i][:, 0:n], cw(0), acc[i][:, 0:n], ALU.mult, ALU.add,
                                [B_[f"gux{i}"], smallB, B_[f"acc{i}"]], [B_[f"acc{i}"]])
                            STT(acc[i][:, 0:n], gux[i][:, 2:n + 2], cw(2), acc[i][:, 0:n], ALU.mult, ALU.add,
                                [B_[f"gux{i}"], smallB, B_[f"acc{i}"]], [B_[f"acc{i}"]])
                            TT(hT[:, j, off:off + n], gbx[i][:, 0:n], acc[i][:, 0:n], ALU.mult,
                               [B_[f"gbx{i}"], B_[f"acc{i}"]], [hB[j][ci]])
                        jobs.append(job)
                return jobs

            if layer == 0:
                stage([], gqa_fn)
            else:
                stage([], mla_fn)

        def mla_inproj(p):
            cgs = cginfo(p)
            lat = [cg for cg in cgs if cg["kind"] == "l"]
            ws = WS()
            ckv32 = ws.f32(4 * TP).rearrange("p (a t) -> p a t", a=4)
            kr32 = ws.f32(TP)
            rq = ws.f32(TP); rkv = ws.f32(TP)
            ropeT = ws.f32(2 * TP).rearrange("p (a t) -> p a t", a=2)
            qr32_ = [ws.f32(512) for _ in range(2)]; t1_ = [ws.f32(512) for _ in range(2)]
            t2_ = [ws.f32(512) for _ in range(2)]
            t1, t2 = t1_[0], t2_[0]
            cqg = ws.bf(6 * TP).rearrange("p (a t) -> p a t", a=6)
            ckvn = ws.bf(4 * TP).rearrange("p (a t) -> p a t", a=4)
            sq1_ = [ws.bf(512) for _ in range(2)]; qrb_ = [ws.bf(512) for _ in range(2)]; krb = ws.bf(TP)
            zz = {"sq": 0, "q": 0}
            qnst = [ws.bf(TP) for _ in range(2)]
            qrst = [ws.bf(TP) for _ in range(2)]
            knst = [ws.bf(TP) for _ in range(2)]
            krst = ws.bf(TP)
            vst = [ws.bf(512) for _ in range(2)]
            names = ["kr32", "rq0", "rq1", "rkv0", "rkv1", "rope", "qr320", "qr321", "t1", "t2", "t11", "t21", "sq10", "sq11",
                     "qrb0", "qrb1", "krb", "qnst0", "qnst1",
                     "qrst0", "qrst1", "knst0", "knst1", "krst", "vst0", "vst1"]
            B_ = {n: Buf("mla_" + n) for n in names}
            cqB = [[Buf(f"cqg{c}_{i}") for i in range(2)] for c in range(6)]
            ckv32B = [[Buf(f"ckv32{c}_{i}") for i in range(2)] for c in range(4)]
            ckvnB = [[Buf(f"ckvn{c}_{i}") for i in range(2)] for c in range(4)]
            Wd_, Wuq, Wukv = mla_w_down[0], mla_w_uq[0], mla_w_ukv[0]
            gq = lambda c: small[:, O_QKN + 2 + c:O_QKN + 3 + c]
            gkv = lambda c: small[:, O_QKN + 8 + c:O_QKN + 9 + c]

            def pre(_):
                if "mrope" not in skip:
                    load_rope(p, cgs, ropeM_d, 64, ropeT, B_["rope"], base=64)
            stage([], pre)

            def down_chunk(w3, wb, col0, M, cg):
                n, off, ci = cg["n"], cg["off"], cg["ci"]
                b = nb()
                for k in range(KC):
                    MM(banks[b][0:M, 0:n], w3[:, k, col0:col0 + M], hT[:, k, off:off + n], k == 0, k == KC - 1,
                       [wb, hB[k][ci]], [bankB[b]])
                return b

            def stats_acc(b, cg, first, last_, sbank, fin=None):
                n = cg["n"]
                if "mstats" in skip:
                    return
                z = zz["sq"]; zz["sq"] ^= 1
                sq1 = sq1_[z]; sqB_ = B_[f"sq1{z}"]
                ACT(sq1[:, 0:n], banks[b][:, 0:n], AF.Square, [bankB[b]], [sqB_])
                pipe_step()

                def Bf():
                    MM(banks[sbank][:, 0:n], ones[:], sq1[:, 0:n], first, last_, [onesB, sqB_], [bankB[sbank]], inc=True)
                    if fin is not None:
                        fin()
                pipeB.append(Bf)

            def fin_stats(cg, sbank, dst, dstB, inv_n):
                n, off = cg["n"], cg["off"]
                RSQRT(dst[:, off:off + n], banks[sbank][:, 0:n], [bankB[sbank]], [dstB], inv_n)

            def down_stage(t):
                ncol = 512 if t < 2 else 320
                specs = [lambda wt: [(V3(wt, 16, 512)[:, :, 0:ncol], wsrc(Wd_, t * 512, ncol))]]

                def fn(slots):
                    (wt, wb), = slots
                    w3 = V3(wt, 16, 512)
                    for cg in cgs:
                        n, off, ci = cg["n"], cg["off"], cg["ci"]
                        for cl in range(4 if t < 2 else 3):
                            gc = t * 4 + cl
                            if gc < 6:
                                b = down_chunk(w3, wb, cl * 128, 128, cg)
                                TS(cqg[:, gc, off:off + n], banks[b][:, 0:n], gq(gc), ALU.mult, [bankB[b], smallB],
                                   [cqB[gc][ci]])
                                stats_acc(b, cg, gc == 0, gc == 5, 7 - ci,
                                          fin=((lambda cg=cg, ci=ci: fin_stats(cg, 7 - ci, rq, B_[f"rq{ci}"], 1.0 / 768))
                                               if gc == 5 else None))
                            elif gc < 10:
                                c = gc - 6
                                b = down_chunk(w3, wb, cl * 128, 128, cg)
                                TS(ckv32[:, c, off:off + n], banks[b][:, 0:n], gkv(c), ALU.mult, [bankB[b], smallB],
                                   [ckv32B[c][ci]])
                                def fin_kv(cg=cg, ci=ci, n=n, off=off):
                                    fin_stats(cg, 7 - ci, rkv, B_[f"rkv{ci}"], 1.0 / 512)
                                    for c2 in range(4):
                                        TT(ckvn[:, c2, off:off + n], ckv32[:, c2, off:off + n], rkv[:, off:off + n],
                                           ALU.mult, [ckv32B[c2][ci], B_[f"rkv{ci}"]], [ckvnB[c2][ci]])
                                stats_acc(b, cg, c == 0, c == 3, 7 - ci, fin=(fin_kv if c == 3 else None))
                            else:
                                b = down_chunk(w3, wb, cl * 128 - 64, 128, cg)
                                if cg["kind"] == "c":
                                    COPY(krst[64:128, off:off + n], banks[b][64:128, 0:n], [bankB[b]], [B_["krst"]], "act")
                                else:
                                    COPY(kr32[:, off:off + n], banks[b][:, 0:n], [bankB[b]], [B_["kr32"]], "dve")
                                    COPY(krb[:, off:off + n], kr32[:, off:off + n], [B_["kr32"]], [B_["krb"]], "act")
                                    rope_hi_tail(n, kr32[:, off:off + n], B_["kr32"], krb[:, off:off + n], B_["krb"],
                                                 ropeT[64:128, 0, off:off + n], ropeT[64:128, 1, off:off + n], B_["rope"],
                                                 t1, B_["t1"], t2, B_["t2"], krst[64:128, off:off + n], B_["krst"])
                    if t == 2:
                        pipe_flush()
                        S.dma("sp", kr_scr[:, p * TP:(p + 1) * TP], krst[64:128, :], "st_kr", reads=[B_["krst"]],
                              writes=[scrB["kr"][p]])
                stage(specs, fn)

            def uq_stage(t):
                specs = [lambda wt: [(V3(wt, 6, 384), wsrc(Wuq, t * 384, 384))]]

                def fn(slots):
                    (wt, wb), = slots
                    w3 = V3(wt, 6, 384)
                    for hh in range(2):
                        h = t * 2 + hh
                        i2 = h % 2
                        if not lat:
                            continue
                        for cg in lat:
                            n, off, ci = cg["n"], cg["off"], cg["ci"]
                            b = nb()
                            for k in range(6):
                                MM(banks[b][:, 0:n], w3[:, k, hh * 192:hh * 192 + 128], cqg[:, k, off:off + n],
                                   k == 0, k == 5, [wb, cqB[k][ci]], [bankB[b]])
                            TT(qnst[i2][:, off:off + n], banks[b][:, 0:n], rq[:, off:off + n], ALU.mult,
                               [bankB[b], B_[f"rq{ci}"]], [B_[f"qnst{i2}"]])
                            b = nb()
                            for k in range(6):
                                MM(banks[b][:, 0:n], w3[:, k, hh * 192 + 64:hh * 192 + 192], cqg[:, k, off:off + n],
                                   k == 0, k == 5, [wb, cqB[k][ci]], [bankB[b]])
                            z = zz["q"]; zz["q"] ^= 1
                            qr32, qrb, t1z, t2z = qr32_[z], qrb_[z], t1_[z], t2_[z]
                            qB_, bB_, t1B_, t2B_ = B_[f"qr32{z}"], B_[f"qrb{z}"], B_["t1" if z == 0 else "t11"], B_["t2" if z == 0 else "t21"]
                            TT(qr32[:, 0:n], banks[b][:, 0:n], rq[:, off:off + n], ALU.mult,
                               [bankB[b], B_[f"rq{ci}"]], [qB_])
                            COPY(qrb[:, 0:n], qr32[:, 0:n], [qB_], [bB_], "act")
                            pipe_step()
                            is_last = cg is lat[-1]

                            def Cf(n=n, off=off, qr32=qr32, qrb=qrb, t1z=t1z, t2z=t2z, qB_=qB_, bB_=bB_, t1B_=t1B_, t2B_=t2B_,
                                   i2=i2, h=h, is_last=is_last):
                                rope_hi_tail(n, qr32, qB_, qrb, bB_,
                                             ropeT[64:128, 0, off:off + n], ropeT[64:128, 1, off:off + n], B_["rope"],
                                             t1z, t1B_, t2z, t2B_, qrst[i2][64:128, off:off + n], B_[f"qrst{i2}"])
                                if is_last:
                                    c0 = lat[0]["off"]; c1 = lat[-1]["off"] + lat[-1]["n"]
                                    S.dma("sp", qn_scr[:, h, p * TP + c0:p * TP + c1], qnst[i2][:, c0:c1], f"st_qn{i2}",
                                          reads=[B_[f"qnst{i2}"]], writes=[scrB["qn"][p]])
                                    S.dma("sp", qr_scr[:, h, p * TP + c0:p * TP + c1], qrst[i2][64:128, c0:c1], f"st_qr{i2}",
                                          reads=[B_[f"qrst{i2}"]], writes=[scrB["qr"][p]])
                            pipeB.append(Cf)
                    if t == 7:
                        pipe_flush()
                stage(specs, fn)

            def ukv_stage(t):
                specs = [lambda wt: [(V3(wt, 4, 2048), wsrc(Wukv, t * 2048, 2048))]]

                def fn(slots):
                    (wt, wb), = slots
                    w3 = V3(wt, 4, 2048)
                    w4 = wt[:, 0:8192].rearrange("p (k h c) -> p k h c", k=4, h=8)
                    for hh in range(8):
                        h = t * 8 + hh
                        i2 = h % 2
                        for cg in cgs:
                            n, off, ci = cg["n"], cg["off"], cg["ci"]
                            b = nb()
                            for k in range(4):
                                MM(banks[b][:, 0:n], w3[:, k, hh * 256:hh * 256 + 128], ckvn[:, k, off:off + n],
                                   k == 0, k == 3, [wb, ckvnB[k][ci]], [bankB[b]])
                            COPY(knst[i2][:, off:off + n], banks[b][:, 0:n], [bankB[b]], [B_[f"knst{i2}"]],
                                 "act" if ci == 0 else "dve")
                        S.dma("sp", kn_scr[:, h, p * TP:(p + 1) * TP], knst[i2], f"st_kn{i2}",
                              reads=[B_[f"knst{i2}"]], writes=[scrB["kn"][p]])
                    u = 0
                    for tt in range(6):
                        ci = cg_of_tile(p, tt)
                        for hg in range(2):
                            b = nb()
                            for k in range(4):
                                MM(banks[b][:, 0:512], ckvn[:, k, tt * 128:(tt + 1) * 128], w4[:, k, hg * 4:(hg + 1) * 4, 128:256],
                                   k == 0, k == 3, [wb, ckvnB[k][ci]], [bankB[b]])
                            i2 = u % 2; u += 1
                            COPY(vst[i2][:, 0:512], banks[b][:, 0:512], [bankB[b]], [B_[f"vst{i2}"]], "act" if i2 else "dve")
                            h0 = t * 8 + hg * 4
                            S.dma("sp", v1_scr[:, p * 6 + tt, h0 * 128:(h0 + 4) * 128], vst[i2][:, 0:512], f"st_v1{i2}",
                                  reads=[B_[f"vst{i2}"]], writes=[scrB["v1"][p]])
                stage(specs, fn)

            for t in range(3):
                if "mdown" not in skip:
                    down_stage(t)
                if p == 0:
                    mark(f"A1d{t}")
            for t in range(8):
                uq_stage(t)
            if p == 0:
                mark("A1q")
            for t in range(2):
                ukv_stage(t)
            if p == 0:
                mark("A1k")

        def final_out(p, cgs):
            def fn(_):
                for cg in cgs:
                    n, off, ci = cg["n"], cg["off"], cg["ci"]
                    norm_stats(cg, KC, lambda d: xT[:, d, off:off + n], lambda d: [xB[d][ci]], 1.0 / D)
                    for d in range(KC):
                        STT(xT[:, d, off:off + n], xT[:, d, off:off + n], small[:, O_NORM + 96 + d:O_NORM + 97 + d],
                            rstd[:, off:off + n], ALU.mult, ALU.mult, [xB[d][ci], smallB, rstdB[ci]], [xB[d][ci]])
                    for tl in range(n // 128):
                        c0 = off + tl * 128
                        tok0 = cg["s0"] + tl * 128
                        for half in range(2):
                            sg = half
                            for q in range(2):
                                b = nb()
                                for j in range(4):
                                    d = half * 8 + q * 4 + j
                                    S.op("pe", lambda e, b=b, j=j, d=d, c0=c0: e.transpose(
                                        banks[b][:, j * 128:(j + 1) * 128], xT[:, d, c0:c0 + 128], ident[:]),
                                        reads=[xB[d][ci], identB], writes=[bankB[b]], inc=(j == 3))
                                COPY(stg[sg][:, q * 512:(q + 1) * 512], banks[b][:, 0:512], [bankB[b]], [stgB[sg]],
                                     "act" if q == 0 else "dve")
                            S.dma("sp", out_d[tok0:tok0 + 128, half * 1024:(half + 1) * 1024], stg[sg][:], f"st_stg{sg}",
                                  reads=[stgB[sg]], writes=[outB])
            stage([], fn)

        def mark(name):
            stage([], ("mark", name))

        for p in range(3):
            cgs = cginfo(p)
            load_x_from_input(p)
            if p == 0:
                drain_mods(8, urgent_only=True)
            modnorm(0, 0, cgs)
            ffn(0, 0, cgs, Wf["ffn1_w_gate"], Wf["ffn1_w_up"], Wf["ffn1_w_down"], store_p=p)
            if p == 0:
                drain_mods(100, urgent_only=True)
            modnorm(0, 1, cgs)
            barrier_stage(("act", "dve", "sp"))
            hyb_inproj(p)
            barrier_stage(("act", "dve", "sp"))
        mark("A0")
        for p in range(3):
            cgs = cginfo(p)
            attention(p, cgs, 0)
            barrier_stage(("sp",))
            mark(f"B0p{p}a")
            load_x_from_scr(p, cgs)
            proj_residual(0, cgs, hyb_w_out[0], lambda k, off, n: hT[:, k, off:off + n], lambda k, ci: hB[k][ci], 16)
            mark(f"B0p{p}b")
            modnorm(0, 2, cgs)
            ffn(0, 2, cgs, Wf["ffn2_w_gate"], Wf["ffn2_w_up"], Wf["ffn2_w_down"], store_p=p)
            barrier_stage(("act", "dve", "sp"))
            mark(f"B0p{p}c")
        mark("B0")
        drain_mods(100)
        for p in range(3):
            cgs = cginfo(p)
            load_x_from_scr(p, cgs)
            modnorm(1, 0, cgs)
            ffn(1, 0, cgs, Wf["ffn1_w_gate"], Wf["ffn1_w_up"], Wf["ffn1_w_down"], store_p=p)
            modnorm(1, 1, cgs)
            barrier_stage(("act", "dve", "sp"))
            mla_inproj(p)
            barrier_stage(("act", "dve", "sp"))
        mark("A1")
        for p in range(3):
            cgs = [cg for cg in cginfo(p) if cg["kind"] == "l"]
            attention(p, cgs, 1)
            barrier_stage(("sp",))
            load_x_from_scr(p, cgs)
            proj_residual(1, cgs, mla_w_o[0], lambda k, off, n: hT[:, k, off:off + n], lambda k, ci: hB[k][ci], 16)
            modnorm(1, 2, cgs)
            ffn(1, 2, cgs, Wf["ffn2_w_gate"], Wf["ffn2_w_up"], Wf["ffn2_w_down"], store_p=(p if dbg else None))
            final_out(p, cgs)
            barrier_stage(("act", "dve", "sp"))
        mark("B1")

        if stop_after is not None:
            cut = None
            for i, (specs, fn) in enumerate(stages):
                if isinstance(fn, tuple) and fn[1] == stop_after:
                    cut = i
            stages[:] = stages[:cut]
        stages[:] = [s for s in stages if not isinstance(s[1], tuple)]

        flat, first = [], []
        for specs, fn in stages:
            first.append(len(flat))
            flat.extend(specs)
        ptr, released = 0, 0
        for si, (specs, fn) in enumerate(stages):
            end = first[si] + len(specs)
            while ptr < len(flat) and ptr < released + NSLOT:
                sl = ptr % NSLOT
                for (o, i_) in flat[ptr](wslot[sl]):
                    S.dma("pool", o, i_, f"w{sl}", writes=[wB[sl]])
                ptr += 1
            assert ptr >= end
            fn([(wslot[n % NSLOT], wB[n % NSLOT]) for n in range(first[si], end)])
            released = end

        S.barrier(engs=("sp",))
        S.final_wait("sp", [outB] + [b for row in xscrB for b in row])
        emit_program(nc, S)
    return nc, S


def _rope_tables(n_tok, dim):
    grid_w = 64
    n_rows = n_tok // grid_w
    row = np.repeat(np.arange(n_rows), grid_w).astype(np.float32)
    col = np.tile(np.arange(grid_w), n_rows).astype(np.float32)
    half = dim // 2
    inv = (1.0 / (np.float32(10000.0) ** (np.arange(0, half, 2, dtype=np.float32) / np.float32(half)))).astype(np.float32)
    ang = np.concatenate([row[:, None] * inv, col[:, None] * inv], axis=-1).astype(np.float32)
    cos = np.cos(ang).astype(np.float32); sin = np.sin(ang).astype(np.float32)
    tab = np.stack([np.repeat(cos.T, 2, axis=0), np.repeat(sin.T, 2, axis=0)], axis=1)
    return np.ascontiguousarray(tab.astype(np.float32))


def _rot_lhsT(n):
    m = np.zeros((n, n), np.float32)
    for i in range(n // 2):
        m[2 * i + 1, 2 * i] = -1.0
        m[2 * i, 2 * i + 1] = 1.0
    return m


def _cols(v, nch):
    return np.ascontiguousarray(np.asarray(v, np.float32).reshape(nch, 128).T)


_CACHE = {}


def kernel(**inputs):
    inp = {k: np.asarray(v) for k, v in inputs.items()}
    if "prog" not in _CACHE:
        _CACHE["prog"] = build_program()[0]
    nc = _CACHE["prog"]
    B = inp["x"].shape[0]
    pt = np.zeros((128, 256), np.float32)
    pt[:, 0:128] = _rot_lhsT(128)
    pt[64:128, 192:256] = _rot_lhsT(64)
    shared = {
        "ident": np.eye(128, dtype=np.float32), "pt": pt,
        "ropeA": _rope_tables(NLAT, 128), "ropeM": _rope_tables(NLAT, 64),
        "mod_w": inp["mod_w"],
        "hyb_w_in": inp["hyb_w_in"], "hyb_w_out": inp["hyb_w_out"], "mla_w_down": inp["mla_w_down"],
        "mla_w_uq": inp["mla_w_uq"], "mla_w_ukv": inp["mla_w_ukv"], "mla_w_o": inp["mla_w_o"],
    }
    for n in ("ffn1_w_gate", "ffn1_w_up", "ffn1_w_down", "ffn2_w_gate", "ffn2_w_up", "ffn2_w_down"):
        shared[n] = inp[n]
    in_maps = []
    for b in range(B):
        small = np.zeros((128, NSMALL), np.float32)
        small[:, O_CVEC:O_CVEC + 16] = _cols(inp["c"][b], 16)
        small[:, O_CVEC + 16:O_CVEC + 32] = _cols(inp["c_ctx"], 16)
        for l in range(2):
            small[:, O_MODB + l * 144:O_MODB + (l + 1) * 144] = _cols(inp["mod_b"][l], 144)
            for s, nm in enumerate(("norm_ffn1", "norm_mix", "norm_ffn2")):
                small[:, O_NORM + (l * 3 + s) * 16:O_NORM + (l * 3 + s + 1) * 16] = _cols(inp[nm][l], 16)
        small[:, O_NORM + 96:O_NORM + 112] = _cols(inp["final_norm"], 16)
        for tap in range(3):
            small[:, O_CONVW + tap * 8:O_CONVW + (tap + 1) * 8] = _cols(inp["hyb_conv_w"][0, tap], 8)
        small[:, O_QKN] = inp["hyb_q_norm"][0]
        small[:, O_QKN + 1] = inp["hyb_k_norm"][0]
        small[:, O_QKN + 2:O_QKN + 8] = _cols(inp["mla_q_norm"][0], 6)
        small[:, O_QKN + 8:O_QKN + 12] = _cols(inp["mla_kv_norm"][0], 4)
        m = dict(shared)
        m["xin"] = np.ascontiguousarray(np.concatenate([inp["ctx"][b], inp["x"][b]], axis=0).astype(np.float32))
        m["small"] = small
        in_maps.append(m)
    res = run_bass_kernel_spmd(nc, in_maps, core_ids=list(range(B)))
    return np.stack([np.asarray(r["out"], dtype=np.float32) for r in res.results], axis=0)
```
